# Optimizing a Trainium2 kernel written in Bass

```python
import jax, jax.numpy as jnp
from jax import lax
import numpy as np

D_MODEL = 1024
BATCH = 32
SEQ = 2048
DEPTH = 1
DEC_BATCH = 8
DEC_SEQ = 4096
PAST_LEN = 128

D_MIX = D_MODEL
D_ATT = D_MIX // 2
N_HEADS_ATT = 8
HD_ATT = D_ATT // N_HEADS_ATT
D_MLSTM = D_MIX - D_ATT
N_HEADS_M = 4
HD_M = D_MLSTM // N_HEADS_M
N_GATES = 4 * N_HEADS_M
SPLIT_SIZES = (D_ATT, D_ATT, D_ATT, 2 * D_MLSTM, D_MLSTM, D_MLSTM, N_GATES)
D_IN = 3 * D_ATT + 4 * D_MLSTM + N_GATES
D_FF = 2816
CONV_W = 3
GRID_W = 64
WIN_ROWS = 8
WIN_COLS = 16
Q_BLOCK_COLS = 16
K_BLOCK_COLS = Q_BLOCK_COLS + WIN_COLS
CHUNK = 64
EPS = 1e-6
NEG_INF = -1e30

kernel_name = 'hymba_natten_mlstm_encoder'


def rms_norm(x, g):
    xf = x.astype(jnp.float32)
    y = xf * lax.rsqrt(jnp.mean(xf * xf, axis=-1, keepdims=True) + EPS)
    return (y * g.astype(jnp.float32)).astype(x.dtype)


def dwconv_centered(x, w):
    T = x.shape[1]
    pad = CONV_W // 2
    xp = jnp.pad(x, ((0, 0), (pad, CONV_W - 1 - pad), (0, 0)))
    y = xp[:, 0:T] * w[0]
    for j in range(1, CONV_W):
        y = y + xp[:, j:j + T] * w[j]
    return y


def neighbourhood_attention(q, k, v, rpb):
    B, T, H, d = q.shape
    rows = T // GRID_W
    kr = min(WIN_ROWS, rows)
    qg = q.reshape(B, rows, GRID_W, H, d)
    kg = k.reshape(B, rows, GRID_W, H, d)
    vg = v.reshape(B, rows, GRID_W, H, d)
    r = np.arange(rows)
    row_start = np.clip(r - WIN_ROWS // 2, 0, rows - kr)
    row_idx = row_start[:, None] + np.arange(kr)[None, :]
    k_rows = kg[:, row_idx]
    v_rows = vg[:, row_idx]
    dr = row_idx - r[:, None] + (WIN_ROWS - 1)
    scale = d ** -0.5
    outs = []
    for c0 in range(0, GRID_W, Q_BLOCK_COLS):
        ks = min(max(c0 - WIN_COLS // 2, 0), GRID_W - K_BLOCK_COLS)
        qc = np.arange(c0, c0 + Q_BLOCK_COLS)
        kc = np.arange(ks, ks + K_BLOCK_COLS)
        cstart = np.clip(qc - WIN_COLS // 2, 0, GRID_W - WIN_COLS)
        valid = (kc[None, :] >= cstart[:, None]) & (kc[None, :] < cstart[:, None] + WIN_COLS)
        dc = np.clip(kc[None, :] - qc[:, None] + (WIN_COLS - 1), 0, 2 * WIN_COLS - 2)
        q_blk = qg[:, :, c0:c0 + Q_BLOCK_COLS]
        k_blk = k_rows[:, :, :, ks:ks + K_BLOCK_COLS]
        v_blk = v_rows[:, :, :, ks:ks + K_BLOCK_COLS]
        s = jnp.einsum('brqhd,brkchd->bhrqkc', q_blk, k_blk).astype(jnp.float32) * scale
        bias = rpb[:, dr[:, None, :, None], dc[None, :, None, :]].astype(jnp.float32)
        s = jnp.where(valid[:, None, :], s + bias[None], NEG_INF)
        p = jax.nn.softmax(s.reshape(s.shape[:4] + (kr * K_BLOCK_COLS,)), axis=-1)
        p = p.reshape(s.shape).astype(v.dtype)
        outs.append(jnp.einsum('bhrqkc,brkchd->brqhd', p, v_blk))
    return jnp.concatenate(outs, axis=2).reshape(B, T, H * d)


def mlstm_scan(q, k, v, i_pre, log_f):
    B, T, H, d = q.shape
    nc = T // CHUNK

    def to_chunks(a):
        return a.reshape(B, nc, CHUNK, H, d).transpose(1, 0, 3, 2, 4)

    def to_chunks_g(a):
        return a.reshape(B, nc, CHUNK, H).transpose(1, 0, 3, 2)

    tri = jnp.tril(jnp.ones((CHUNK, CHUNK), dtype=bool))

    def step(carry, inp):
        C, n, m = carry
        qc, kc, vc, ic, lfc = inp
        b = jnp.cumsum(lfc, axis=-1)
        D = jnp.where(tri, b[..., :, None] - b[..., None, :] + ic[..., None, :], NEG_INF)
        inter = b + m[..., None]
        m_t = jnp.maximum(inter, jnp.max(D, axis=-1))
        a = jnp.einsum('bhtd,bhsd->bhts', qc, kc) * jnp.exp(D - m_t[..., None])
        w_inter = jnp.exp(inter - m_t)
        num = w_inter[..., None] * jnp.einsum('bhtd,bhde->bhte', qc, C) + jnp.einsum('bhts,bhse->bhte', a, vc)
        den = w_inter * jnp.einsum('bhtd,bhd->bht', qc, n) + jnp.sum(a, axis=-1)
        h = num / jnp.maximum(jnp.abs(den), jnp.exp(-m_t))[..., None]
        bL = b[..., -1]
        g = bL[..., None] - b + ic
        m_new = jnp.maximum(bL + m, jnp.max(g, axis=-1))
        decay = jnp.exp(bL + m - m_new)
        kw = kc * jnp.exp(g - m_new[..., None])[..., None]
        C_new = decay[..., None, None] * C + jnp.einsum('bhsd,bhse->bhde', kw, vc)
        n_new = decay[..., None] * n + jnp.sum(kw, axis=2)
        return (C_new, n_new, m_new), h

    init = (jnp.zeros((B, H, d, d), jnp.float32),
            jnp.zeros((B, H, d), jnp.float32),
            jnp.full((B, H), NEG_INF, jnp.float32))
    _, h = lax.scan(step, init, (to_chunks(q), to_chunks(k), to_chunks(v), to_chunks_g(i_pre), to_chunks_g(log_f)))
    return h.transpose(1, 0, 3, 2, 4).reshape(B, T, H, d)


def mlstm_mixer(qk_m, v_m, o_m, gates, conv_w, gate_b, norm_g):
    B, T, _ = v_m.shape
    qk = jax.nn.silu(dwconv_centered(qk_m, conv_w)).astype(jnp.float32)
    q = qk[..., :D_MLSTM].reshape(B, T, N_HEADS_M, HD_M)
    k = qk[..., D_MLSTM:].reshape(B, T, N_HEADS_M, HD_M) * (HD_M ** -0.5)
    v = v_m.astype(jnp.float32).reshape(B, T, N_HEADS_M, HD_M)
    g = gates.astype(jnp.float32) + gate_b.astype(jnp.float32)
    i_fw, f_fw, i_bw, f_bw = jnp.split(g, 4, axis=-1)
    h_fw = mlstm_scan(q, k, v, i_fw, jax.nn.log_sigmoid(f_fw))

    def rev(a):
        return jnp.flip(a, axis=1)

    h_bw = rev(mlstm_scan(rev(q), rev(k), rev(v), rev(i_bw), rev(jax.nn.log_sigmoid(f_bw))))
    h = rms_norm(h_fw + h_bw, norm_g.reshape(N_HEADS_M, HD_M))
    return (jax.nn.sigmoid(o_m.astype(jnp.float32)) * h.reshape(B, T, D_MLSTM)).astype(v_m.dtype)


def encoder(x, norm_mix_g, w_in, mlstm_conv_w, gate_b, attn_rpb, attn_norm_g, mlstm_norm_g,
            w_out, norm_ffn_g, w_up, ffn_conv_w, w_down, norm_final_g):
    B, T, _ = x.shape
    cuts = [int(c) for c in np.cumsum(SPLIT_SIZES)[:-1]]
    for l in range(DEPTH):
        h = rms_norm(x, norm_mix_g[l])
        proj = h @ w_in[l]
        q_a, k_a, v_a, qk_m, v_m, o_m, gates = jnp.split(proj, cuts, axis=-1)
        y_att = neighbourhood_attention(q_a.reshape(B, T, N_HEADS_ATT, HD_ATT),
                                        k_a.reshape(B, T, N_HEADS_ATT, HD_ATT),
                                        v_a.reshape(B, T, N_HEADS_ATT, HD_ATT), attn_rpb[l])
        y_att = rms_norm(y_att, attn_norm_g[l])
        y_m = mlstm_mixer(qk_m, v_m, o_m, gates, mlstm_conv_w[l], gate_b[l], mlstm_norm_g[l])
        x = x + jnp.concatenate([y_att, y_m], axis=-1) @ w_out[l]
        h = rms_norm(x, norm_ffn_g[l])
        gate, val = jnp.split(h @ w_up[l], 2, axis=-1)
        x = x + (jax.nn.gelu(dwconv_centered(gate, ffn_conv_w[l])) * val) @ w_down[l]
    return rms_norm(x, norm_final_g)


def setup_inputs(seed: int = 0) -> dict:
    key = jax.random.key(seed)
    ks = jax.random.split(key, 20)
    f32 = jnp.float32

    def nrm(k, shape, s):
        return jax.random.normal(k, shape, f32) * s

    def gain(k, shape):
        return 1.0 + 0.01 * jax.random.normal(k, shape, f32)

    f_lin = jnp.linspace(3.0, 6.0, N_HEADS_M, dtype=f32)
    zeros_h = jnp.zeros((N_HEADS_M,), f32)
    gate_base = jnp.concatenate([zeros_h, f_lin, zeros_h, f_lin])
    return {
        'x_prompt': jax.random.normal(ks[0], (BATCH, SEQ, D_MODEL), f32),
        'x_sample': jax.random.normal(ks[1], (DEC_BATCH, DEC_SEQ, D_MODEL), f32),
        'norm_mix_g': gain(ks[2], (DEPTH, D_MODEL)),
        'w_in': nrm(ks[3], (DEPTH, D_MODEL, D_IN), D_MODEL ** -0.5),
        'mlstm_conv_w': nrm(ks[4], (DEPTH, CONV_W, 2 * D_MLSTM), CONV_W ** -0.5),
        'gate_b': gate_base[None, :] + nrm(ks[5], (DEPTH, N_GATES), 0.1),
        'attn_rpb': nrm(ks[6], (DEPTH, N_HEADS_ATT, 2 * WIN_ROWS - 1, 2 * WIN_COLS - 1), 0.1),
        'attn_norm_g': gain(ks[7], (DEPTH, D_ATT)),
        'mlstm_norm_g': gain(ks[8], (DEPTH, D_MLSTM)),
        'w_out': nrm(ks[9], (DEPTH, D_MIX, D_MODEL), D_MIX ** -0.5),
        'norm_ffn_g': gain(ks[10], (DEPTH, D_MODEL)),
        'w_up': nrm(ks[11], (DEPTH, D_MODEL, 2 * D_FF), D_MODEL ** -0.5),
        'ffn_conv_w': nrm(ks[12], (DEPTH, CONV_W, D_FF), CONV_W ** -0.5),
        'w_down': nrm(ks[13], (DEPTH, D_FF, D_MODEL), D_FF ** -0.5),
        'norm_final_g': gain(ks[14], (D_MODEL,)),
    }


def reference(x_prompt, x_sample, norm_mix_g, w_in, mlstm_conv_w, gate_b, attn_rpb, attn_norm_g,
              mlstm_norm_g, w_out, norm_ffn_g, w_up, ffn_conv_w, w_down, norm_final_g):
    y_prompt = encoder(x_prompt, norm_mix_g, w_in, mlstm_conv_w, gate_b, attn_rpb, attn_norm_g,
                       mlstm_norm_g, w_out, norm_ffn_g, w_up, ffn_conv_w, w_down, norm_final_g)
    y_sample = encoder(x_sample, norm_mix_g, w_in, mlstm_conv_w, gate_b, attn_rpb, attn_norm_g,
                       mlstm_norm_g, w_out, norm_ffn_g, w_up, ffn_conv_w, w_down, norm_final_g)
    return (y_prompt, y_sample)
```

```python
import math
from contextlib import ExitStack

import numpy as np
import ml_dtypes
import concourse.bass as bass
import concourse.mybir as mybir
from concourse.bass_utils import run_bass_kernel_spmd

F32 = mybir.dt.float32
BF16 = mybir.dt.bfloat16
U8 = mybir.dt.uint8
ALU = mybir.AluOpType
AF = mybir.ActivationFunctionType

NCORES = 8
D = 1024
DIN = 3600
DFF = 2816
NFC = DFF // 128
EPS = 1e-6
SEQS_FULL = (2048, 2048, 2048, 2048, 4096)
NRING = 24
ALL_ENG = ("pe", "act", "dve", "pool", "sp")
MASKV = -30000.0


class Sched:
    def __init__(self, nc):
        self.nc = nc
        self.ops = []
        self.last_w = {}
        self.readers = {}
        self.bar = []
        self.dma_since_bar = {e: [] for e in ALL_ENG}
        self.recent = {e: [] for e in ALL_ENG}

    def add(self, eng, fn, reads=(), writes=(), dma=False, marker=False, extra_deps=()):
        idx = len(self.ops)
        deps = {}
        for r in reads:
            i = self.last_w.get(r)
            if i is not None:
                deps[i] = "raw"
        for w in writes:
            i = self.last_w.get(w)
            if i is not None and i not in deps:
                deps[i] = "waw"
            for i in self.readers.get(w, {}).values():
                if isinstance(i, list):
                    for ii in i:
                        deps.setdefault(ii, "war")
                else:
                    deps.setdefault(i, "war")
        for i in extra_deps:
            deps[i] = "raw"
        for i in self.bar:
            deps.setdefault(i, "raw")
        for w in writes:
            self.last_w[w] = idx
            self.readers[w] = {}
        for r in reads:
            d = self.readers.setdefault(r, {})
            if dma:
                d.setdefault("dma_" + eng, []).append(idx)
            else:
                d[eng] = idx
        keep = []
        for i, typ in deps.items():
            o = self.ops[i]
            if o["eng"] == eng and not o["dma"]:
                if dma:
                    keep.append(i)
                elif o["marker"]:
                    continue
                elif eng == "pe":
                    continue
                else:
                    keep.append(i)
            else:
                keep.append(i)
        self.ops.append(dict(eng=eng, fn=fn, deps=sorted(keep), dma=dma, marker=marker))
        if not dma:
            self.recent[eng] = (self.recent[eng] + [idx])[-2:]
        if dma:
            self.dma_since_bar[eng].append(idx)
        return idx

    def barrier(self):
        marks = []
        for e in ALL_ENG:
            ex = list(self.dma_since_bar[e])
            self.dma_since_bar[e] = []
            marks.append(self.add(e, lambda eng: eng.drain(), marker=True, extra_deps=ex))
        self.bar = marks
        self.last_w = {}
        self.readers = {}

    def emit(self):
        nc = self.nc
        ops = self.ops
        needed = set()
        for o in ops:
            for i in o["deps"]:
                needed.add(i)
        by_eng = {}
        for idx, o in enumerate(ops):
            by_eng.setdefault(o["eng"], []).append(idx)
        sig = {}
        n_sig = {e: 0 for e in by_eng}
        n_dma = {e: 0 for e in by_eng}
        for e, lst in by_eng.items():
            for idx in lst:
                o = ops[idx]
                if o["dma"]:
                    k = n_dma[e]
                    n_dma[e] += 1
                    sig[idx] = ("d", e, k % NRING, 16 * (k // NRING + 1))
                elif idx in needed:
                    n_sig[e] += 1
                    sig[idx] = ("c", e, 0, n_sig[e])
        nwaits = {e: 0 for e in by_eng}
        with ExitStack() as st:
            sems = {}
            for e in by_eng:
                if n_sig[e]:
                    sems[("c", e, 0)] = st.enter_context(nc.semaphore(f"c_{e}"))
                for j in range(min(NRING, n_dma[e])):
                    sems[("d", e, j)] = st.enter_context(nc.semaphore(f"d_{e}_{j}"))
            block = st.enter_context(nc.Block())

            def run(e, eng):
                waited = {}

                def wait(key, val):
                    if waited.get(key, 0) < val:
                        eng.wait_ge(sems[key], val)
                        waited[key] = val
                        nwaits[e] += 1

                for idx in by_eng.get(e, []):
                    o = ops[idx]
                    for i in o["deps"]:
                        s = sig[i]
                        wait((s[0], s[1], s[2]), s[3])
                    s = sig.get(idx)
                    if o["dma"] and s[3] > 16:
                        wait((s[0], s[1], s[2]), s[3] - 16)
                    inst = o["fn"](eng)
                    if s is not None:
                        inst.then_inc(sems[(s[0], s[1], s[2])], 16 if o["dma"] else 1)
                if n_dma.get(e, 0):
                    last = {}
                    for idx in by_eng[e]:
                        s = sig.get(idx)
                        if s is not None and s[0] == "d":
                            last[s[2]] = max(last.get(s[2], 0), s[3])
                    for j, v in last.items():
                        wait(("d", e, j), v)

            if "pe" in by_eng:
                block.tensor(lambda eng: run("pe", eng))
            if "act" in by_eng:
                block.scalar(lambda eng: run("act", eng))
            if "dve" in by_eng:
                block.vector(lambda eng: run("dve", eng))
            if "pool" in by_eng:
                block.gpsimd(lambda eng: run("pool", eng))
            if "sp" in by_eng:
                block.sync(lambda eng: run("sp", eng))
        return {e: (len(l), nwaits[e]) for e, l in by_eng.items()}


_DT_BYTES = {F32: 4, BF16: 2, U8: 1}


class Ctx:
    def __init__(self, nc, S, sbt, banks):
        self.nc, self.S, self.sbt, self.banks = nc, S, sbt, banks
        self.off = 0
        self.cap = sbt.shape[1]

    def reset(self):
        self.off = 0

    def sb(self, shape, dt):
        n = 1
        for s in shape[1:]:
            n *= s
        nb = n * _DT_BYTES[dt]
        nb_al = (nb + 63) // 64 * 64
        assert self.off + nb_al <= self.cap, f"SBUF overflow: {self.off + nb_al} > {self.cap}"
        ap = self.sbt[0:shape[0], self.off:self.off + nb].bitcast(dt)
        self.off += nb_al
        if len(shape) == 3:
            ap = ap.rearrange("p (a b) -> p a b", b=shape[2])
        elif len(shape) == 4:
            ap = ap.rearrange("p (a b c) -> p a b c", b=shape[2], c=shape[3])
        return ap

    def ps(self, bank, dt, nbanks=1):
        t = self.banks[bank]
        assert nbanks == 1
        return t[:, :].bitcast(dt)

    def mm(self, out, lhsT, rhs, start, stop, r, w):
        self.S.add("pe", lambda e: e.matmul(out, lhsT=lhsT, rhs=rhs, start=start, stop=stop), reads=r, writes=w)

    def tr(self, out, in_, ident, r, w):
        self.S.add("pe", lambda e: e.transpose(out=out, in_=in_, identity=ident), reads=r, writes=w)

    def act(self, out, in_, func, r, w, scale=1.0, bias=None, accum=None):
        kw = {}
        if bias is not None:
            kw["bias"] = bias
        if accum is not None:
            kw["accum_out"] = accum
        self.S.add("act", lambda e: e.activation(out=out, in_=in_, func=func, scale=scale, **kw), reads=r, writes=w)

    def copy(self, eng, out, in_, r, w):
        if eng == "act":
            self.S.add("act", lambda e: e.activation(out=out, in_=in_, func=AF.Copy), reads=r, writes=w)
        else:
            self.S.add(eng, lambda e: e.tensor_copy(out=out, in_=in_), reads=r, writes=w)

    def tt(self, eng, out, in0, in1, op, r, w):
        self.S.add(eng, lambda e: e.tensor_tensor(out=out, in0=in0, in1=in1, op=op), reads=r, writes=w)

    def ts(self, eng, out, in0, s1, s2, op0, op1, r, w):
        if s2 is None:
            self.S.add(eng, lambda e: e.tensor_scalar(out=out, in0=in0, scalar1=s1, scalar2=None, op0=op0), reads=r, writes=w)
        else:
            self.S.add(eng, lambda e: e.tensor_scalar(out=out, in0=in0, scalar1=s1, scalar2=s2, op0=op0, op1=op1), reads=r, writes=w)

    def stt(self, out, in0, scalar, in1, op0, op1, r, w):
        self.S.add("dve", lambda e: e.scalar_tensor_tensor(out=out, in0=in0, scalar=scalar, in1=in1, op0=op0, op1=op1), reads=r, writes=w)

    def memset(self, eng, ap, val, w):
        self.S.add(eng, lambda e: e.memset(ap, val), writes=w)

    def recip(self, out, in_, r, w):
        self.S.add("dve", lambda e: e.reciprocal(out=out, in_=in_), reads=r, writes=w)

    def dma(self, out, in_, r, w, eng="sp"):
        self.S.add(eng, lambda e: e.dma_start(out=out, in_=in_), reads=r, writes=w, dma=True)

    def rstd(self, ss, ms, rs, nh, n, r_name, tag):
        self.ts("dve", ms, ss, 1.0 / n, EPS, ALU.mult, ALU.add, [r_name], [("ms", tag)])
        self.tt("pool", rs, ms, nh, ALU.pow, [("ms", tag), "nh"], [("rs", tag)])


def bc(ap, shape):
    return ap.to_broadcast(list(shape))


def build_program(seqs=SEQS_FULL, upto="D", debug=False):
    NTOK = sum(seqs)
    nc = bass.Bass("TRN2", target_bir_lowering=False)
    dram = {}

    def din(name, shape, dt=F32):
        dram[name] = nc.dram_tensor(name, list(shape), dt, kind="ExternalInput").ap()
        return dram[name]

    def dscr(name, shape, dt):
        kind = "ExternalOutput" if debug else "Internal"
        dram[name] = nc.dram_tensor(name, list(shape), dt, kind=kind).ap()
        return dram[name]

    x = din("x", [NTOK, D])
    w_in = din("w_in", [D, DIN])
    w_out = din("w_out", [D, D])
    w_up = din("w_up", [D, 2 * DFF])
    w_down = din("w_down", [DFF, D])
    gvec = din("gvec", [4, D])
    mconv = din("mconv", [128, 8, 3])
    fconv = din("fconv", [128, NFC, 3])
    gateb = din("gateb", [1, 16])
    rpbg = din("rpbg", [128, 4, 960])
    c_ident = din("c_ident", [128, 128], BF16)
    c_mask = din("c_mask", [128, 960])
    c_mats = din("c_mats", [4, 128, 128])
    c_tri = din("c_tri", [2, 128, 64], BF16)
    c_cadd = din("c_cadd", [128, 16])
    y = nc.dram_tensor("y", [NTOK, D], F32, kind="ExternalOutput").ap()
    dram["y"] = y

    QBD = dscr("QBD", [4, 128, NTOK * 2], BF16)
    XT = dscr("XT", [12, 128, NTOK], BF16)
    VA = dscr("VA", [NTOK, 520], BF16)
    VM = dscr("VM", [NTOK, 516], BF16)
    OM = dscr("OM", [NTOK, 512], BF16)
    GT = dscr("GT", [NTOK, 16], F32)
    QMT = dscr("QMT", [8, 128, NTOK], BF16)
    KTOK = dscr("KTOK", [NTOK, 512], BF16)
    Y = dscr("Y", [NTOK, D], BF16)
    XMID = dscr("XMID", [NTOK, D], F32)

    S = Sched(nc)
    with ExitStack() as st:
        sbt = st.enter_context(nc.sbuf_tensor("arena", [128, 206 * 1024], U8))
        banks = [st.enter_context(nc.psum_tensor(f"bank{i}", [128, 2048], U8)) for i in range(8)]
        k = Ctx(nc, S, sbt, banks)
        phases = ["A", "B1", "B2a", "B2", "C", "D"]
        fns = {"A": phase_A, "B1": phase_B1, "B2a": phase_B2a, "B2": phase_B2, "C": phase_C, "D": phase_D}
        for ph in phases:
            k.reset()
            fns[ph](k, dram, seqs)
            S.barrier()
            if ph == upto:
                break
        stats = S.emit()
    return nc, stats


def load_bcast_row(k, dst, src_row, w):
    P = dst.shape[0]
    k.dma(dst, src_row.partition_broadcast(P), [], w)


def phase_A(k, dram, seqs):
    NTOK = sum(seqs)
    x, w_in, gvec, ident_d = dram["x"], dram["w_in"], dram["gvec"], dram["c_ident"]
    QBD, XT, VA, VM, OM, GT = (dram[n] for n in ("QBD", "XT", "VA", "VM", "OM", "GT"))
    W = k.sb([128, 8, DIN], BF16)
    gm = k.sb([128, D], F32)
    ident = k.sb([128, 128], BF16)
    nh = k.sb([128, 4], F32)
    xt = [k.sb([128, D], F32) for _ in range(3)]
    sq = k.sb([128, D], BF16)
    ss = [k.sb([128, 1], F32) for _ in range(2)]
    ms = [k.sb([128, 1], F32) for _ in range(2)]
    rs = [k.sb([128, 1], F32) for _ in range(2)]
    xn = [k.sb([128, D], BF16) for _ in range(2)]
    xnT = [k.sb([128, 8, 512], BF16) for _ in range(2)]
    st_va = [k.sb([128, 8, 65], BF16) for _ in range(2)]
    st_vm = [k.sb([128, 4, 129], BF16) for _ in range(2)]
    st_om = [k.sb([128, 512], BF16) for _ in range(2)]
    st_g = [k.sb([128, 16], F32) for _ in range(2)]
    st_qbd = [k.sb([128, 8, 128], BF16) for _ in range(2)]
    st_fm = [k.sb([128, 512], BF16) for _ in range(3)]
    pT = k.ps(0, BF16).rearrange("p (a b) -> p a b", b=128)
    ptm = [k.ps(1 + i, F32) for i in range(3)]
    pfm = [k.ps(4 + i, F32) for i in range(3)]

    k.dma(ident, ident_d, [], ["ident"])
    load_bcast_row(k, gm, gvec[0:1, :], ["gm"])
    k.memset("pool", nh, -0.5, ["nh"])
    for b in range(2):
        k.memset("pool", st_va[b], 1.0, [("st_va", b)])
        k.memset("pool", st_vm[b], 1.0, [("st_vm", b)])
        k.memset("pool", st_qbd[b], 0.0, [("st_qbd", b)])
    blocks = [(1024, 1536), (2560, 3072), (3072, 3584), (3584, 3600), (0, 512), (512, 1024), (1536, 2048), (2048, 2560)]
    wsrc = w_in.rearrange("(kc p) n -> p kc n", p=128)

    def wname(col):
        return ("W", col // 512)

    for (c0, c1) in blocks:
        k.dma(W[:, :, c0:c1], wsrc[:, :, c0:c1], [], [wname(c0)], eng="pool")

    NT = NTOK // 128
    NM = NTOK // 512
    TM = [("va", 1024, 512), ("vm", 2560, 512), ("om", 3072, 512), ("g", 3584, 16)]
    FM = [("qa", i, 128 * i) for i in range(4)] + [("x", i, 512 + 128 * i) for i in range(4)] + \
         [("x", 4 + i, 1536 + 128 * i) for i in range(8)]
    cnt = {"tm": 0, "fm": 0, "qbd": 0, "stfm": 0}

    def L(t):
        k.dma(xt[t % 3], x[t * 128:(t + 1) * 128, :], [], [("xt", t % 3)])

    def N(t):
        b = t % 2
        k.act(sq, xt[t % 3], AF.Square, [("xt", t % 3)], ["sq", ("ss", b)], accum=ss[b])
        k.ts("dve", ms[b], ss[b], 1.0 / D, EPS, ALU.mult, ALU.add, [("ss", b)], [("ms", b)])
        k.tt("pool", rs[b], ms[b], nh[:, 0:1], ALU.pow, [("ms", b), "nh"], [("rs", b)])
        k.stt(xn[b], xt[t % 3], rs[b], gm, ALU.mult, ALU.mult, [("xt", t % 3), ("rs", b), "gm"], [("xn", b)])

    def T(t):
        b = t % 2
        m, j = divmod(t, 4)
        for kc in range(8):
            k.tr(pT[:, kc, :], xn[b][:, kc * 128:(kc + 1) * 128], ident, [("xn", b), "ident"], ["pT"])
        k.copy("act", xnT[m % 2][:, :, j * 128:(j + 1) * 128], pT, ["pT"], [("xnT", m % 2, j)])

    def Mgrp(t, gi):
        b = t % 2
        m, j = divmod(t, 4)
        name, c0, ncol = TM[gi]
        pi = cnt["tm"] % 3
        cnt["tm"] += 1
        ps = ptm[pi]
        for kc in range(8):
            k.mm(ps[:, 0:ncol], xnT[m % 2][:, kc, j * 128:(j + 1) * 128], W[:, kc, c0:c0 + ncol], kc == 0, kc == 7,
                 [("xnT", m % 2, j), wname(c0)], [("ptm", pi)])
        tok = slice(t * 128, (t + 1) * 128)
        if name == "va":
            k.copy("dve", st_va[b][:, :, 0:64], ps.rearrange("p (a b) -> p a b", b=64), [("ptm", pi)], [("st_va", b)])
            k.dma(VA[tok, :], st_va[b].rearrange("p a b -> p (a b)"), [("st_va", b)], [])
        elif name == "vm":
            k.copy("act", st_vm[b][:, :, 0:128], ps.rearrange("p (a b) -> p a b", b=128), [("ptm", pi)], [("st_vm", b)])
            k.dma(VM[tok, :], st_vm[b].rearrange("p a b -> p (a b)"), [("st_vm", b)], [])
        elif name == "om":
            k.copy("dve", st_om[b], ps, [("ptm", pi)], [("st_om", b)])
            k.dma(OM[tok, :], st_om[b], [("st_om", b)], [])
        else:
            k.copy("act", st_g[b], ps[:, 0:16], [("ptm", pi)], [("st_g", b)])
            k.dma(GT[tok, :], st_g[b], [("st_g", b)], [])

    def Fchunk(m, fi):
        kind, idx, col = FM[fi]
        pi = cnt["fm"] % 3
        cnt["fm"] += 1
        ps = pfm[pi]
        for kc in range(8):
            k.mm(ps, W[:, kc, col:col + 128], xnT[m % 2][:, kc, :], kc == 0, kc == 7,
                 [("xnT", m % 2, j) for j in range(4)] + [wname(col)], [("pfm", pi)])
        if kind == "qa":
            qi = cnt["qbd"] % 2
            cnt["qbd"] += 1
            psv = ps.rearrange("p (r c) -> p r c", c=64)
            k.copy("dve", st_qbd[qi][0:64, :, 0:64], psv[0:64], [("pfm", pi)], [("st_qbd", qi)])
            k.copy("act", st_qbd[qi][64:128, :, 64:128], psv[64:128], [("pfm", pi)], [("st_qbd", qi)])
            k.dma(QBD[idx][:, m * 1024:(m + 1) * 1024], st_qbd[qi].rearrange("p a b -> p (a b)"), [("st_qbd", qi)], [])
        else:
            si = cnt["stfm"] % 3
            cnt["stfm"] += 1
            k.copy("act" if (fi % 2) else "dve", st_fm[si], ps, [("pfm", pi)], [("st_fm", si)])
            k.dma(XT[idx][:, m * 512:(m + 1) * 512], st_fm[si], [("st_fm", si)], [])

    L(0)
    if NT > 1:
        L(1)
    N(0)
    T(0)
    for t in range(NT):
        m, j = divmod(t, 4)
        if t + 2 < NT:
            L(t + 2)
        if t + 1 < NT:
            N(t + 1)
        Mgrp(t, 0)
        Mgrp(t, 1)
        if j < 3 and t + 1 < NT:
            T(t + 1)
        Mgrp(t, 2)
        Mgrp(t, 3)
        if j == 3:
            for fi in range(16):
                Fchunk(m, fi)
                if fi == 7 and t + 1 < NT:
                    T(t + 1)


def phase_B1(k, dram, seqs):
    QBD, XT, VA, Y, gvec = dram["QBD"], dram["XT"], dram["VA"], dram["Y"], dram["gvec"]
    TMAX = max(seqs)
    NTMAX = TMAX // 128
    ident = k.sb([128, 128], BF16)
    nh = k.sb([128, 4], F32)
    gc = k.sb([64, 512], F32)
    bext = k.sb([128, 4, 960], BF16)
    kT = k.sb([128, 4, TMAX], BF16)
    vA = k.sb([128, NTMAX, 520], BF16)
    vB = k.sb([128, NTMAX, 520], BF16)
    qb = [k.sb([128, 4, 1024], BF16) for _ in range(2)]
    pexp = [k.sb([128, 512], BF16) for _ in range(2)]
    PT = [k.sb([128, 4, 128], BF16) for _ in range(2)]
    rinv = [k.sb([64, 8], F32) for _ in range(2)]
    ya = [k.sb([64, 512], F32) for _ in range(2)]
    sqj = k.sb([64, 512], BF16)
    ss = [k.sb([64, 1], F32) for _ in range(2)]
    ms = [k.sb([64, 1], F32) for _ in range(2)]
    rs = [k.sb([64, 1], F32) for _ in range(2)]
    yst = [k.sb([64, 512], BF16) for _ in range(2)]
    rp = k.sb([128, 4, 960], F32)
    mk = k.sb([128, 960], F32)
    pS = [k.ps(i, F32) for i in range(2)]
    pPT = [k.ps(2 + i, BF16)[:, 0:512].rearrange("p (a b) -> p a b", b=128) for i in range(2)]
    pY = [[k.ps(4 + 2 * i + b, F32) for b in range(2)] for i in range(2)]

    k.dma(ident, dram["c_ident"], [], ["ident"])
    k.memset("pool", nh, -0.5, ["nh"])
    load_bcast_row(k, gc, gvec[1:2, 0:512], ["gc"])
    k.dma(rp, dram["rpbg"], [], ["rp"])
    k.dma(mk, dram["c_mask"], [], ["mk"])
    for p in range(4):
        k.stt(bext[:, p, :], rp[:, p, :], 8.0, mk, ALU.mult, ALU.add, ["rp", "mk"], ["bext"])

    tok0 = 0
    gcount = 0
    for T in seqs:
        R = T // 64
        NT = T // 128
        for p in range(4):
            k.dma(kT[:, p, 0:T], XT[p][:, tok0:tok0 + T], [], ["kT"])
        k.dma(vA[:, 0:NT, :], VA[tok0:tok0 + T, :].rearrange("(n p) c -> p n c", p=128), [], ["vA"])
        k.dma(vB[:, 0:NT - 1, :], VA[tok0 + 64:tok0 + T - 64, :].rearrange("(n p) c -> p n c", p=128), [], ["vB"])
        units = [(r, p) for r in range(R) for p in range(4)]
        U = len(units)

        def loadq(g):
            gi = (gcount + g) % 2
            c0 = (tok0 // 64 + 8 * g) * 128
            k.dma(qb[gi], QBD[:, :, c0:c0 + 1024].rearrange("c p t -> p c t"), [], [("qb", gi)])

        def Sst(u):
            r, p = units[u]
            g = r // 8
            gi = (gcount + g) % 2
            if r % 8 == 0 and p == 0 and g + 1 < R // 8:
                loadq(g + 1)
            rs_ = min(max(r - 4, 0), R - 8)
            off = r - rs_
            bcol = (7 - off) * 64
            ps = pS[u % 2]
            k.mm(ps, ident, bext[:, p, bcol:bcol + 512], True, False, ["ident", "bext"], [("pS", u % 2)])
            k.mm(ps, qb[gi][:, p, (r % 8) * 128:(r % 8) * 128 + 128], kT[:, p, rs_ * 64:rs_ * 64 + 512], False, True,
                 [("qb", gi), "kT"], [("pS", u % 2)])
            k.act(pexp[u % 2], ps, AF.Exp, [("pS", u % 2)], [("pexp", u % 2)], scale=0.125)

        def Tst(u):
            for kc in range(4):
                k.tr(pPT[u % 2][:, kc, :], pexp[u % 2][:, kc * 128:(kc + 1) * 128], ident, [("pexp", u % 2), "ident"],
                     [("pPT", u % 2)])
            k.copy("dve", PT[u % 2], pPT[u % 2], [("pPT", u % 2)], [("PT", u % 2)])

        def PVst(u):
            r, p = units[u]
            rs_ = min(max(r - 4, 0), R - 8)
            ry = r % 2
            for hl in range(2):
                h = 2 * p + hl
                bank = h // 4
                col = (h % 4) * 65
                for kc in range(4):
                    if rs_ % 2 == 0:
                        vt, vn = vA, "vA"
                        ti = rs_ // 2 + kc
                    else:
                        vt, vn = vB, "vB"
                        ti = (rs_ - 1) // 2 + kc
                    k.mm(pY[ry][bank][0:64, col:col + 65], PT[u % 2][:, kc, hl * 64:(hl + 1) * 64],
                         vt[:, ti, h * 65:(h + 1) * 65], kc == 0, kc == 3, [("PT", u % 2), vn], [("pY", ry, bank)])
            if p == 3:
                pyv = [pY[ry][b_][0:64, 0:260].rearrange("p (h e) -> p h e", e=65) for b_ in range(2)]
                ytok = tok0 + r * 64

                def e1(ry=ry, pyv=pyv):
                    for b_ in range(2):
                        k.recip(rinv[ry][:, 4 * b_:4 * b_ + 4], pyv[b_][:, :, 64], [("pY", ry, b_)], [("rinv", ry)])
                    for b_ in range(2):
                        k.tt("dve", ya[ry][:, 256 * b_:256 * b_ + 256].rearrange("p (h e) -> p h e", e=64),
                             pyv[b_][:, :, 0:64], bc(rinv[ry][:, 4 * b_:4 * b_ + 4].unsqueeze(2), [64, 4, 64]), ALU.mult,
                             [("pY", ry, b_), ("rinv", ry)], [("ya", ry)])

                def e2(ry=ry):
                    k.act(sqj, ya[ry], AF.Square, [("ya", ry)], ["sqj", ("ss", ry)], accum=ss[ry])

                def e3(ry=ry):
                    k.ts("dve", ms[ry], ss[ry], 1.0 / 512, EPS, ALU.mult, ALU.add, [("ss", ry)], [("ms", ry)])
                    k.tt("pool", rs[ry], ms[ry], nh[0:64, 0:1], ALU.pow, [("ms", ry), "nh"], [("rs", ry)])

                def e4(ry=ry, ytok=ytok):
                    k.stt(yst[ry], ya[ry], rs[ry], gc, ALU.mult, ALU.mult, [("ya", ry), ("rs", ry), "gc"], [("yst", ry)])
                    k.dma(Y[ytok:ytok + 64, 0:512], yst[ry], [("yst", ry)], [])

                deferred.append((u + 1, e1))
                deferred.append((u + 2, e2))
                deferred.append((u + 3, e3))
                deferred.append((u + 4, e4))

        loadq(0)
        deferred = []
        for u in range(-2, U + 5):
            if 0 <= u + 2 < U:
                Sst(u + 2)
            if 0 <= u + 1 < U:
                Tst(u + 1)
            if 0 <= u < U:
                PVst(u)
            for (du, fn) in list(deferred):
                if du <= u:
                    fn()
                    deferred.remove((du, fn))
        assert not deferred
        gcount += R // 8
        tok0 += T


def phase_B2a(k, dram, seqs):
    XT, QMT, KTOK = dram["XT"], dram["QMT"], dram["KTOK"]
    ident = k.sb([128, 128], BF16)
    cw = k.sb([128, 8, 3], F32)
    raw = [k.sb([128, 8, 514], BF16) for _ in range(2)]
    tcv = [k.sb([128, 512], F32) for _ in range(2)]
    qk = [k.sb([128, 8, 512], BF16) for _ in range(2)]
    kst = [k.sb([128, 4, 512], BF16) for _ in range(2)]
    pK = [k.ps(i, BF16)[:, 0:512].rearrange("p (a b) -> p a b", b=128) for i in range(2)]
    k.dma(ident, dram["c_ident"], [], ["ident"])
    k.dma(cw, dram["mconv"], [], ["cw"])
    mts = []
    tok0 = 0
    for T in seqs:
        for m in range(T // 512):
            mts.append((tok0 + m * 512, m == 0, m == T // 512 - 1))
        tok0 += T

    def loadraw(mi):
        a, first, last = mts[mi]
        b = mi % 2
        lo = 1 if first else 0
        hi = 513 if last else 514
        if first:
            k.memset("pool", raw[b][:, :, 0:1], 0.0, [("raw", b)])
        if last:
            k.memset("pool", raw[b][:, :, 513:514], 0.0, [("raw", b)])
        k.dma(raw[b][:, :, lo:hi], XT[4:12, :, a - 1 + lo:a - 1 + hi].rearrange("c p t -> p c t"), [], [("raw", b)])

    nt = 0
    loadraw(0)
    for mi in range(len(mts)):
        a, first, last = mts[mi]
        b = mi % 2
        if mi + 1 < len(mts):
            loadraw(mi + 1)
        for c in range(8):
            tb = (mi * 8 + c) % 2
            k.ts("dve", tcv[tb], raw[b][:, c, 1:513], cw[:, c, 1:2], None, ALU.mult, None, [("raw", b), "cw"], [("tcv", tb)])
            k.stt(tcv[tb], raw[b][:, c, 0:512], cw[:, c, 0:1], tcv[tb], ALU.mult, ALU.add, [("raw", b), "cw", ("tcv", tb)], [("tcv", tb)])
            k.stt(tcv[tb], raw[b][:, c, 2:514], cw[:, c, 2:3], tcv[tb], ALU.mult, ALU.add, [("raw", b), "cw", ("tcv", tb)], [("tcv", tb)])
            k.act(qk[b][:, c, :], tcv[tb], AF.Silu, [("tcv", tb)], [("qk", b, c)])
        k.dma(QMT[:, :, a:a + 512].rearrange("c p t -> p c t"), qk[b], [("qk", b, c) for c in range(8)], [])
        for j in range(4):
            pb = nt % 2
            nt += 1
            for h in range(4):
                k.tr(pK[pb][:, h, :], qk[b][:, 4 + h, j * 128:(j + 1) * 128], ident, [("qk", b, 4 + h), "ident"], [("pK", pb)])
            k.copy("act", kst[b][:, j, :], pK[pb].rearrange("p a b -> p (a b)"), [("pK", pb)], [("kst", b)])
        k.dma(KTOK[a:a + 512, :].rearrange("(j p) c -> p j c", p=128), kst[b], [("kst", b)], [])


def phase_B2(k, dram, seqs):
    QMT, KTOK, VM, OM, GT, Y, gvec = (dram[n] for n in ("QMT", "KTOK", "VM", "OM", "GT", "Y", "gvec"))
    TMAX = max(seqs)
    NTm = TMAX // 128
    mats = k.sb([128, 4, 128], F32)
    tri = k.sb([128, 2, 64], BF16)
    gbb = k.sb([128, 16], F32)
    cadd = k.sb([128, 16], F32)
    nh = k.sb([128, 4], F32)
    gcm = k.sb([128, 512], F32)
    G = k.sb([128, NTm, 16], F32)
    Z = k.sb([128, NTm, 16], F32)
    Lf = k.sb([128, NTm * 8], F32)
    REM = k.sb([128, NTm * 8], F32)
    FL = k.sb([128, NTm * 8], F32)
    WVz = k.sb([128, 2, NTm * 8], F32)
    tmpi = k.sb([128, NTm * 8], F32)
    GD = k.sb([128, 2, NTm * 8], F32)
    hbuf = k.sb([128, NTm, 512], F32)
    C = [[k.sb([128, 4, 129], F32) for _ in range(2)] for _ in range(2)]
    cver = [0, 0]
    Cg = [k.sb([128, 4, 129], F32) for _ in range(2)]
    Cs = [[k.sb([128, 4, 129], BF16) for _ in range(2)] for _ in range(2)]
    qkT = [[k.sb([128, 8, 128], BF16) for _ in range(3)] for _ in range(2)]
    ktk = [[k.sb([128, 512], BF16) for _ in range(3)] for _ in range(2)]
    vau = [[k.sb([128, 4, 129], BF16) for _ in range(3)] for _ in range(2)]
    veg = [[k.sb([128, 2, 4, 129], BF16) for _ in range(2)] for _ in range(2)]
    a0T = [k.sb([128, 4, 64], BF16) for _ in range(2)]
    den = [k.sb([128, 4], F32) for _ in range(2)]
    rden = [k.sb([128, 4], F32) for _ in range(2)]
    htmp = [k.sb([128, 512], F32) for _ in range(2)]
    omt = [k.sb([128, 512], BF16) for _ in range(2)]
    sqj = [k.sb([128, 128], BF16) for _ in range(4)]
    ssm = [k.sb([128, 4], F32) for _ in range(2)]
    msm = [k.sb([128, 4], F32) for _ in range(2)]
    rsm = [k.sb([128, 4], F32) for _ in range(2)]
    sg = [k.sb([128, 512], F32) for _ in range(2)]
    t1 = [k.sb([128, 512], F32) for _ in range(2)]
    yst = [k.sb([128, 512], BF16) for _ in range(2)]
    bk = [k.ps(i, F32) for i in range(8)]

    def BK(i):
        return ("bk", i)

    k.dma(mats, dram["c_mats"].rearrange("m p t -> p m t"), [], ["mats"])
    k.dma(tri, dram["c_tri"].rearrange("m p t -> p m t"), [], ["tri"])
    load_bcast_row(k, gbb, dram["gateb"], ["gbb"])
    k.dma(cadd, dram["c_cadd"], [], ["cadd"])
    load_bcast_row(k, gcm, gvec[1:2, 512:1024], ["gcm"])
    k.memset("pool", nh, -0.5, ["nh"])
    k.tt("dve", gbb, gbb, cadd, ALU.add, ["gbb", "cadd"], ["gbb"])

    tok0 = 0
    cc = 0
    lc = [0, 0]
    ecs = {"n": 0}
    for T in seqs:
        NT = T // 128
        N8 = NT * 8
        k.dma(G[:, 0:NT, :], GT[tok0:tok0 + T, :].rearrange("(n p) g -> p n g", p=128), [], ["G"])
        k.tt("dve", Z[:, 0:NT, :], G[:, 0:NT, :], bc(gbb.unsqueeze(1), [128, NT, 16]), ALU.add, ["G", "gbb"], ["Z"])
        Zv = Z[:, 0:NT, :].rearrange("p n (d e) -> p n d e", e=8)

        def v4(t):
            return t[:, 0:N8].rearrange("p (n d e) -> p n d e", d=2, e=4)

        k.act(v4(Lf), Zv[:, :, :, 4:8], AF.Exp, ["Z"], ["Lf"], scale=-1.0)
        k.act(Lf[:, 0:N8], Lf[:, 0:N8], AF.Ln, ["Lf"], ["Lf"], bias=1.0)
        for i in range(4):
            k.mm(bk[i][:, 0:N8], mats[:, i, :], Lf[:, 0:N8], True, True, ["mats", "Lf"], [BK(i)])
        k.copy("act", v4(REM)[:, :, 0, :], v4(bk[0])[:, :, 0, :], [BK(0)], ["REM"])
        k.copy("dve", v4(REM)[:, :, 1, :], v4(bk[1])[:, :, 1, :], [BK(1)], ["REM"])
        k.act(FL[:, 0:N8], REM[:, 0:N8], AF.Exp, ["REM"], ["FL"], scale=-1.0)
        k.tt("dve", v4(tmpi), Zv[:, :, :, 0:4], v4(REM), ALU.subtract, ["Z", "REM"], ["tmpi"])
        k.memset("pool", WVz[:, :, 0:N8], 0.0, ["WV"])
        k.act(WVz[0:64, 0, 0:N8], tmpi[0:64, 0:N8], AF.Exp, ["tmpi"], ["WV"])
        k.act(WVz[64:128, 1, 0:N8], tmpi[64:128, 0:N8], AF.Exp, ["tmpi"], ["WV"])
        k.act(GD[:, 0, 0:N8], bk[2][:, 0:N8], AF.Exp, [BK(2)], ["GD"], scale=-1.0)
        k.act(GD[:, 1, 0:N8], bk[3][:, 0:N8], AF.Exp, [BK(3)], ["GD"], scale=-1.0)
        for d in range(2):
            k.memset("pool", C[d][cver[d]], 0.0, [("C", d, cver[d])])

        tiles = [list(range(NT)), list(range(NT - 1, -1, -1))]
        visit = {}
        for j in range(NT):
            visit[(0, tiles[0][j])] = 2 * j
            visit[(1, tiles[1][j])] = 2 * j + 1

        def load(d, j):
            n = tiles[d][j]
            s = lc[d] % 3
            lc[d] += 1
            t0_ = tok0 + n * 128
            k.dma(qkT[d][s], QMT[:, :, t0_:t0_ + 128].rearrange("c p t -> p c t"), [], [("qkT", d, s)])
            k.dma(ktk[d][s], KTOK[t0_:t0_ + 128, :], [], [("ktk", d, s)])
            k.dma(vau[d][s].rearrange("p a b -> p (a b)"), VM[t0_:t0_ + 128, :], [], [("vau", d, s)])
            return s

        slots = {}
        steps = []
        for j in range(NT):
            for d in range(2):
                steps.append((d, j, tiles[d][j]))

        def sfront(i):
            d, j, n = steps[i]
            rb = d
            s = slots[(d, j)]
            qn = ("qkT", d, s)
            for hf in range(2):
                pr = slice(hf * 64, hf * 64 + 64)
                for h in range(4):
                    k.mm(bk[rb][pr, h * 64:(h + 1) * 64], qkT[d][s][:, 4 + h, pr], qkT[d][s][:, h, pr], True, True, [qn], [BK(rb)])
            k.tt("dve", a0T[rb], bk[rb][:, 0:256].rearrange("p (a b) -> p a b", b=64),
                 bc(tri[:, d, :].unsqueeze(1), [128, 4, 64]), ALU.mult, [BK(rb), "tri"], [("a0T", rb)])

        def half_state(i, hx):
            d, j, n = steps[i]
            rb = d
            vb = (i // 2) % 2
            s = slots[(d, j)]
            gi = n * 8 + d * 4
            qn, kn = ("qkT", d, s), ("ktk", d, s)
            hf = ((0, 1) if d == 0 else (1, 0))[hx]
            pr = slice(hf * 64, hf * 64 + 64)
            v = cver[d]
            Cc, Cn = C[d][v], C[d][1 - v]
            cver[d] = 1 - v
            vg = veg[rb][vb][:, hf]
            vgn = ("veg", rb, vb)
            k.tt("dve", Cg[d], Cc, bc(GD[:, hf, gi:gi + 4].unsqueeze(2), [128, 4, 129]), ALU.mult, [("C", d, v), "GD"], [("Cg", d)])
            k.copy("act", Cs[d][v], Cg[d], [("Cg", d)], [("Cs", d, v)])
            for h in range(4):
                b_, c_ = h // 2, (h % 2) * 129
                k.mm(bk[6 + b_][:, c_:c_ + 129], ktk[d][s][:, h * 128:(h + 1) * 128], vg[:, h, :], True, True,
                     [kn, vgn], [BK(6 + b_)])
            for b_ in range(2):
                k.tt("dve", Cn[:, 2 * b_:2 * b_ + 2, :], Cg[d][:, 2 * b_:2 * b_ + 2, :],
                     bk[6 + b_][:, 0:258].rearrange("p (a b) -> p a b", b=129), ALU.add, [("Cg", d), BK(6 + b_)], [("C", d, 1 - v)])

            def pmm():
                pb_ = 2 + 2 * rb + vb
                for h in range(4):
                    o = bk[pb_][pr, h * 128:(h + 1) * 128]
                    k.mm(o, qkT[d][s][:, h, pr], Cs[d][v][:, h, 0:128], True, False, [qn, ("Cs", d, v)], [BK(pb_)])
                    k.mm(o, a0T[rb][:, h, :], vg[:, h, 0:128], False, True, [("a0T", rb), vgn], [BK(pb_)])
                for h in range(4):
                    c0 = 256 + 8 * vb + 2 * h
                    o = bk[rb][pr, c0:c0 + 2]
                    k.mm(o, qkT[d][s][:, h, pr], Cs[d][v][:, h, 127:129], True, False, [qn, ("Cs", d, v)], [BK(rb)])
                    k.mm(o, a0T[rb][:, h, :], vg[:, h, 127:129], False, True, [("a0T", rb), vgn], [BK(rb)])
            return pmm

        def vegop(i):
            d, j, n = steps[i]
            rb = d
            vb = (i // 2) % 2
            s = slots[(d, j)]
            gi = n * 8 + d * 4
            for hf in range(2):
                k.tt("pool", veg[rb][vb][:, hf], vau[d][s], bc(WVz[:, hf, gi:gi + 4].unsqueeze(2), [128, 4, 129]), ALU.mult,
                     [("vau", d, s), "WV"], [("veg", rb, vb)])

        def back_a(i):
            d, j, n = steps[i]
            rb = d
            vb = (i // 2) % 2
            dv = bk[rb][:, 256 + 8 * vb:256 + 8 * vb + 8].rearrange("p (h t) -> p h t", t=2)
            k.act(den[rb], dv[:, :, 1], AF.Abs, [BK(rb)], [("den", rb)])

        def back_b(i):
            d, j, n = steps[i]
            rb = d
            gi = n * 8 + d * 4
            k.tt("dve", den[rb], den[rb], FL[:, gi:gi + 4], ALU.max, [("den", rb), "FL"], [("den", rb)])
            k.recip(rden[rb], den[rb], [("den", rb)], [("rden", rb)])

        def back_c(i):
            d, j, n = steps[i]
            rb = d
            first = visit[(d, n)] < visit[(1 - d, n)]
            vb = (i // 2) % 2
            pb_ = 2 + 2 * rb + vb
            if True:
                dst = hbuf[:, n, :] if first else htmp[rb]
                wn = [("hbuf", n, h) for h in range(4)] if first else [("htmp", rb, h) for h in range(4)]
                k.tt("dve", dst.rearrange("p (a b) -> p a b", b=128), bk[pb_].rearrange("p (a b) -> p a b", b=128),
                     bc(rden[rb].unsqueeze(2), [128, 4, 128]), ALU.mult, [BK(pb_), ("rden", rb)], wn)
                return
            for h in range(4):
                dst = hbuf[:, n, h * 128:(h + 1) * 128] if first else htmp[rb][:, h * 128:(h + 1) * 128]
                k.act(dst, bk[pb_][:, h * 128:(h + 1) * 128], AF.Copy, [BK(pb_), ("rden", rb)],
                      [("hbuf", n, h) if first else ("htmp", rb, h)], scale=rden[rb][:, h:h + 1])

        def back_d(i):
            d, j, n = steps[i]
            rb = d
            first = visit[(d, n)] < visit[(1 - d, n)]
            if not first:
                k.tt("dve", hbuf[:, n, :], hbuf[:, n, :], htmp[rb], ALU.add,
                     [("hbuf", n, h) for h in range(4)] + [("htmp", rb, h) for h in range(4)], [("hbuf", n, h) for h in range(4)])

        for j in range(min(2, NT)):
            for d in range(2):
                slots[(d, j)] = load(d, j)
        nsteps = len(steps)

        def b2e(n):
            b = ecs["n"] % 2
            ecs["n"] += 1
            t0_ = tok0 + n * 128
            k.dma(omt[b], OM[t0_:t0_ + 128, :], [], [("omt", b)])
            hn = [("hbuf", n, h_) for h_ in range(4)]
            for h in range(4):
                k.act(sqj[h], hbuf[:, n, h * 128:(h + 1) * 128], AF.Square, hn, [("sqj", h), ("ssm", b, h)], accum=ssm[b][:, h:h + 1])
            k.ts("dve", msm[b], ssm[b], 1.0 / 128, EPS, ALU.mult, ALU.add, [("ssm", b, h) for h in range(4)], [("msm", b)])
            k.tt("pool", rsm[b], msm[b], nh, ALU.pow, [("msm", b), "nh"], [("rsm", b)])
            k.act(sg[b], omt[b], AF.Sigmoid, [("omt", b)], [("sg", b)])
            k.tt("dve", t1[b].rearrange("p (a b) -> p a b", b=128), hbuf[:, n, :].rearrange("p (a b) -> p a b", b=128),
                 bc(rsm[b].unsqueeze(2), [128, 4, 128]), ALU.mult, hn + [("rsm", b)], [("t1", b)])
            k.tt("pool", t1[b], t1[b], gcm, ALU.mult, [("t1", b), "gcm"], [("t1", b)])
            k.tt("dve", yst[b], t1[b], sg[b], ALU.mult, [("t1", b), ("sg", b)], [("yst", b)])
            k.dma(Y[t0_:t0_ + 128, 512:1024], yst[b], [("yst", b)], [])

        pending = []
        npairs = nsteps // 2
        vegop(0)
        vegop(1)
        def backs(p):
            i0_, i1_ = 2 * p, 2 * p + 1
            for fn in (back_a, back_b, back_c, back_d):
                fn(i0_)
                fn(i1_)
            for ii in (i0_, i1_):
                d_, j_, n_ = steps[ii]
                if visit[(d_, n_)] > visit[(1 - d_, n_)]:
                    pending.append((p + 2, n_))

        for p in range(npairs + 4):
            if p < npairs:
                i0_, i1_ = 2 * p, 2 * p + 1
                j = steps[i0_][1]
                if j + 2 < NT:
                    for d in range(2):
                        slots[(d, j + 2)] = load(d, j + 2)
                if p + 1 < npairs:
                    vegop(i0_ + 2)
                    vegop(i1_ + 2)
                sfront(i0_)
                sfront(i1_)
                pa = half_state(i0_, 0)
                pb = half_state(i1_, 0)
                if p >= 1:
                    backs(p - 1)
                pa()
                pb()
                pa = half_state(i0_, 1)
                pb = half_state(i1_, 1)
                pa()
                pb()
            elif p == npairs:
                backs(p - 1)
            for (tp, n_) in list(pending):
                if tp <= p:
                    b2e(n_)
                    pending.remove((tp, n_))
        assert not pending
        cc += nsteps

        tok0 += T


def load_ffn_weights(k, dram, Wu, Wd):
    usrc = dram["w_up"].rearrange("(kc p) n -> p kc n", p=128)
    for c0 in range(0, 2 * DFF, 512):
        k.dma(Wu[:, :, c0:c0 + 512], usrc[:, :, c0:c0 + 512], [], [], eng="pool")
    dsrc = dram["w_down"].rearrange("(fc p) n -> p fc n", p=128)
    for half in range(2):
        k.dma(Wd[:, :, half * 512:(half + 1) * 512], dsrc[:, :, half * 512:(half + 1) * 512], [], [], eng="pool")


def phase_C(k, dram, seqs):
    NTOK = sum(seqs)
    x, w_out, Y, XMID = dram["x"], dram["w_out"], dram["Y"], dram["XMID"]
    Wu = k.sb([128, 8, 2 * DFF], BF16)
    Wd = k.sb([128, NFC, D], BF16)
    Wo = k.sb([128, 8, D], BF16)
    ident = k.sb([128, 128], BF16)
    yt = [k.sb([128, D], BF16) for _ in range(2)]
    yT = [k.sb([128, 8, 128], BF16) for _ in range(2)]
    xt = [k.sb([128, D], F32) for _ in range(2)]
    xm = [k.sb([128, D], F32) for _ in range(2)]
    pTs = [k.ps(0, BF16).rearrange("p (a b) -> p a b", b=128), k.ps(5, BF16).rearrange("p (a b) -> p a b", b=128)]
    pO = [[k.ps(1 + 2 * i + h, F32) for h in range(2)] for i in range(2)]
    k.dma(ident, dram["c_ident"], [], ["ident"])
    wsrc = w_out.rearrange("(kc p) n -> p kc n", p=128)
    for h in range(2):
        k.dma(Wo[:, :, h * 512:(h + 1) * 512], wsrc[:, :, h * 512:(h + 1) * 512], [], [("Wo", h)], eng="pool")
    load_ffn_weights(k, dram, Wu, Wd)
    NT = NTOK // 128

    def load(t):
        b = t % 2
        k.dma(yt[b], Y[t * 128:(t + 1) * 128, :], [], [("yt", b)])
        k.dma(xt[b], x[t * 128:(t + 1) * 128, :], [], [("xt", b)])

    load(0)
    for t in range(NT):
        b = t % 2
        if t + 1 < NT:
            load(t + 1)
        pT = pTs[b]
        for kc in range(8):
            k.tr(pT[:, kc, :], yt[b][:, kc * 128:(kc + 1) * 128], ident, [("yt", b), "ident"], [("bk", 5 * b)])
        k.copy("act", yT[b], pT, [("bk", 5 * b)], [("yT", b)])
        for h in range(2):
            bi = 1 + 2 * b + h
            for kc in range(8):
                k.mm(pO[b][h], yT[b][:, kc, :], Wo[:, kc, h * 512:(h + 1) * 512], kc == 0, kc == 7, [("yT", b), ("Wo", h)], [("bk", bi)])
            k.tt("dve", xm[b][:, h * 512:(h + 1) * 512], xt[b][:, h * 512:(h + 1) * 512], pO[b][h], ALU.add,
                 [("xt", b), ("bk", bi)], [("xm", b, h)])
        k.dma(XMID[t * 128:(t + 1) * 128, :], xm[b], [("xm", b, 0), ("xm", b, 1)], [])


def phase_D(k, dram, seqs):
    XMID, gvec, y = dram["XMID"], dram["gvec"], dram["y"]
    Wu = k.sb([128, 8, 2 * DFF], BF16)
    Wd = k.sb([128, NFC, D], BF16)
    ident = k.sb([128, 128], BF16)
    nh = k.sb([128, 4], F32)
    gff = k.sb([128, D], F32)
    gfin = k.sb([128, D], F32)
    fcw = k.sb([128, NFC, 3], F32)
    xt = [k.sb([128, D], F32) for _ in range(3)]
    hst = k.sb([2, 4], F32)
    ss = [k.sb([128, 1], F32) for _ in range(2)]
    ms = [k.sb([128, 1], F32) for _ in range(2)]
    rs = [k.sb([128, 1], F32) for _ in range(2)]
    h2 = [k.sb([128, D], BF16) for _ in range(2)]
    h2T = k.sb([128, 8, 512], BF16)
    h2hTs = [k.sb([128, 8, 2], BF16) for _ in range(2)]
    aT = k.sb([128, NFC, 512], BF16)
    gx = [k.sb([128, 514], F32) for _ in range(2)]
    cv = [k.sb([128, 512], F32) for _ in range(2)]
    ge = [k.sb([128, 512], BF16) for _ in range(2)]
    yfs = [k.sb([128, D], F32) for _ in range(2)]
    hxn = yfs[1].bitcast(BF16)[0:2, 0:D]
    hx = yfs[0][0:2, :]
    junk = gx[0].bitcast(BF16)[:, 0:D]
    HX = [("yf", 0, 0), ("yf", 0, 1)]
    JUNK = ("gx", 0)
    pT = k.ps(0, BF16).rearrange("p (a b) -> p a b", b=128)
    pG = [k.ps(1 + i, F32) for i in range(2)]
    pV = [k.ps(3 + i, F32) for i in range(2)]
    pTh = k.ps(5, BF16)[:, 0:16].rearrange("p (a b) -> p a b", b=2)
    pGh = k.ps(5, F32)[:, 64:66]
    pD = [k.ps(6 + i, F32) for i in range(2)]

    def BK(i):
        return ("bk", i)

    k.dma(ident, dram["c_ident"], [], ["ident"])
    k.memset("pool", nh, -0.5, ["nh"])
    load_bcast_row(k, gff, gvec[2:3, :], ["gff"])
    load_bcast_row(k, gfin, gvec[3:4, :], ["gfin"])
    k.dma(fcw, dram["fconv"], [], ["fcw"])

    mts = []
    tok0 = 0
    for T in seqs:
        for M in range(T // 512):
            mts.append((tok0 + M * 512, M == 0, M == T // 512 - 1))
        tok0 += T
    cnt = {"t": 0, "d": 0, "y": 0}
    uses = [mts[0][0] + j * 128 for j in range(4)]
    for mi in range(len(mts)):
        a0_ = mts[mi][0]
        if mi + 1 < len(mts):
            a1_ = mts[mi + 1][0]
            uses += [a1_, a1_ + 128, a0_, a1_ + 256, a0_ + 128, a1_ + 384, a0_ + 256, a0_ + 384]
        else:
            uses += [a0_ + j * 128 for j in range(4)]
    issued = {"n": 0}

    def xt_issue(upto):
        while issued["n"] <= upto and issued["n"] < len(uses):
            u = issued["n"]
            ta_ = uses[u]
            k.dma(xt[u % 3], XMID[ta_:ta_ + 128, :], [], [("xt", u % 3)])
            issued["n"] += 1

    def next_xt(ta):
        u = cnt["t"]
        cnt["t"] += 1
        assert uses[u] == ta, (u, uses[u], ta)
        xt_issue(u + 2)
        return u % 3

    def prep_sub(mi, j):
        a = mts[mi][0]
        ta = a + j * 128
        xb = next_xt(ta)
        b = 0
        hb = j % 2
        k.act(h2[hb], xt[xb], AF.Square, [("xt", xb)], [("h2", hb), ("ss", b)], accum=ss[b])
        k.ts("dve", ms[b], ss[b], 1.0 / D, EPS, ALU.mult, ALU.add, [("ss", b)], [("ms", b)])
        k.tt("pool", rs[b], ms[b], nh[:, 0:1], ALU.pow, [("ms", b), "nh"], [("rs", b)])
        k.stt(h2[hb], xt[xb], rs[b], gff, ALU.mult, ALU.mult, [("xt", xb), ("rs", b), "gff"], [("h2", hb)])

    def prep_T(mi, j):
        hb = j % 2
        for kc in range(8):
            k.tr(pT[:, kc, :], h2[hb][:, kc * 128:(kc + 1) * 128], ident, [("h2", hb), "ident"], [BK(0)])
        k.copy("act", h2T[:, :, j * 128:(j + 1) * 128], pT, [BK(0)], [("h2T", j)])

    def halo(mi):
        a, first, last = mts[mi]
        HXN = ("yf", 1, 0)
        h2hT = h2hTs[mi % 2]
        k.memset("pool", hx, 0.0, HX)
        if not first:
            k.dma(hx[0:1, :], XMID[a - 1:a, :], [], HX)
        if not last:
            k.dma(hx[1:2, :], XMID[a + 512:a + 513, :], [], HX)
        k.act(hxn, hx, AF.Square, HX, [HXN, "hss"], accum=hst[:, 0:1])
        k.ts("dve", hst[:, 1:2], hst[:, 0:1], 1.0 / D, EPS, ALU.mult, ALU.add, ["hss"], ["hms"])
        k.tt("pool", hst[:, 2:3], hst[:, 1:2], nh[0:2, 0:1], ALU.pow, ["hms", "nh"], ["hrs"])
        k.stt(hxn, hx, hst[:, 2:3], gff[0:2, :], ALU.mult, ALU.mult, HX + ["hrs", "gff"], [HXN])
        for kc in range(8):
            k.tr(pTh[:, kc, :], hxn[0:2, kc * 128:(kc + 1) * 128], ident[0:2, 0:2], [HXN, "ident"], [BK(5)])
        k.copy("act", h2hT, pTh, [BK(5)], [("h2hT", mi % 2)])

    h2Tn = [("h2T", j) for j in range(4)]

    def S1(fc, mi):
        b = fc % 2
        h2hT = h2hTs[mi % 2]
        for kc in range(8):
            k.mm(pG[b], Wu[:, kc, fc * 128:(fc + 1) * 128], h2T[:, kc, :], kc == 0, kc == 7, h2Tn, [BK(1 + b)])
        for kc in range(8):
            k.mm(pGh, Wu[:, kc, fc * 128:(fc + 1) * 128], h2hT[:, kc, :], kc == 0, kc == 7, [("h2hT", mi % 2)], [BK(5)])
        for kc in range(8):
            k.mm(pV[b], Wu[:, kc, DFF + fc * 128:DFF + (fc + 1) * 128], h2T[:, kc, :], kc == 0, kc == 7, h2Tn, [BK(3 + b)])

    def S2(fc):
        b = fc % 2
        k.copy("act", gx[b][:, 1:513], pG[b], [BK(1 + b)], [("gx", b)])
        k.copy("act", gx[b][:, 0:1], pGh[:, 0:1], [BK(5)], [("gx", b)])
        k.copy("act", gx[b][:, 513:514], pGh[:, 1:2], [BK(5)], [("gx", b)])
        k.copy("act", aT[:, fc, :], pV[b], [BK(3 + b)], [("aT", fc)])

    def S3(fc):
        b = fc % 2
        k.ts("dve", cv[b], gx[b][:, 0:512], fcw[:, fc, 0:1], None, ALU.mult, None, [("gx", b), "fcw"], [("cv", b)])
        k.stt(cv[b], gx[b][:, 1:513], fcw[:, fc, 1:2], cv[b], ALU.mult, ALU.add, [("gx", b), "fcw", ("cv", b)], [("cv", b)])
        k.stt(cv[b], gx[b][:, 2:514], fcw[:, fc, 2:3], cv[b], ALU.mult, ALU.add, [("gx", b), "fcw", ("cv", b)], [("cv", b)])

    def S4(fc):
        b = fc % 2
        k.act(ge[b], cv[b], AF.Gelu_apprx_tanh, [("cv", b)], [("ge", b)])

    def S5(fc):
        b = fc % 2
        k.tt("dve", aT[:, fc, :], aT[:, fc, :], ge[b], ALU.mult, [("ge", b), ("aT", fc)], [("aT", fc)])

    aTn = [("aT", fc) for fc in range(NFC)]

    def down_sub(mi, j):
        a = mts[mi][0]
        ta = a + j * 128
        xb = next_xt(ta)
        b = 1
        yi = cnt["y"] % 2
        cnt["y"] += 1
        yf = yfs[yi]
        for half in range(2):
            pi = cnt["d"] % 2
            cnt["d"] += 1
            for fc in range(NFC):
                k.mm(pD[pi], aT[:, fc, j * 128:(j + 1) * 128], Wd[:, fc, half * 512:(half + 1) * 512], fc == 0, fc == NFC - 1,
                     [("aT", fc)], [BK(6 + pi)])
            k.tt("dve", yf[:, half * 512:(half + 1) * 512], xt[xb][:, half * 512:(half + 1) * 512], pD[pi], ALU.add,
                 [("xt", xb), BK(6 + pi)], [("yf", yi, half)])
        yn = [("yf", yi, 0), ("yf", yi, 1)]
        k.act(junk, yf, AF.Square, yn, [JUNK, ("ss", b)], accum=ss[b])
        k.ts("dve", ms[b], ss[b], 1.0 / D, EPS, ALU.mult, ALU.add, [("ss", b)], [("ms", b)])
        k.tt("pool", rs[b], ms[b], nh[:, 0:1], ALU.pow, [("ms", b), "nh"], [("rs", b)])
        k.stt(yf, yf, rs[b], gfin, ALU.mult, ALU.mult, yn + [("rs", b), "gfin"], yn)
        k.dma(y[ta:ta + 128, :], yf, yn, [])

    for j in range(4):
        prep_sub(0, j)
        prep_T(0, j)
    halo(0)
    for mi in range(len(mts)):
        nxt = mi + 1 < len(mts)
        for fc in range(NFC + 1):
            if fc < NFC:
                S1(fc, mi)
                S2(fc)
                S3(fc)
            if fc >= 1:
                S4(fc - 1)
                S5(fc - 1)
            if fc == 3 and nxt:
                halo(mi + 1)
        if nxt:
            prep_sub(mi + 1, 0)
            prep_sub(mi + 1, 1)
            down_sub(mi, 0)
            prep_T(mi + 1, 0)
            prep_sub(mi + 1, 2)
            down_sub(mi, 1)
            prep_T(mi + 1, 1)
            prep_sub(mi + 1, 3)
            down_sub(mi, 2)
            prep_T(mi + 1, 2)
            down_sub(mi, 3)
            prep_T(mi + 1, 3)
        else:
            for j in range(4):
                down_sub(mi, j)


def host_consts():
    c = {}
    c["c_ident"] = np.eye(128, dtype=np.float32).astype(ml_dtypes.bfloat16)
    q = np.arange(64)
    kc = np.arange(64)
    cstart = np.clip(q - 8, 0, 48)
    valid = (kc[None, :] >= cstart[:, None]) & (kc[None, :] < cstart[:, None] + 16)
    m = np.where(valid, 0.0, MASKV).astype(np.float32)
    m = np.tile(m[:, None, :], (1, 15, 1)).reshape(64, 960)
    c["c_mask"] = np.concatenate([m, m], axis=0)
    s = np.arange(128)[:, None]
    t = np.arange(128)[None, :]
    same = (s // 64) == (t // 64)
    MFm = (same & (s > t)).astype(np.float32)
    MBm = (same & (s < t)).astype(np.float32)
    S0 = np.broadcast_to((s < 64), (128, 128)).astype(np.float32)
    S1 = np.broadcast_to((s >= 64), (128, 128)).astype(np.float32)
    c["c_mats"] = np.stack([MFm, MBm, S0, S1]).astype(np.float32)
    sl = (np.arange(128) % 64)[:, None]
    tl = np.arange(64)[None, :]
    c["c_tri"] = np.stack([(sl <= tl), (sl >= tl)]).astype(np.float32).astype(ml_dtypes.bfloat16)
    cadd = np.zeros((128, 16), np.float32)
    cadd[:, 0:4] = math.log(128 ** -0.5)
    cadd[:, 8:12] = math.log(128 ** -0.5)
    c["c_cadd"] = cadd
    return c


def pack_params(norm_mix_g, w_in, mlstm_conv_w, gate_b, attn_rpb, attn_norm_g, mlstm_norm_g, w_out,
                norm_ffn_g, w_up, ffn_conv_w, w_down, norm_final_g):
    f = np.float32
    p = {}
    p["w_in"] = np.ascontiguousarray(np.asarray(w_in, f)[0])
    p["w_out"] = np.ascontiguousarray(np.asarray(w_out, f)[0])
    p["w_up"] = np.ascontiguousarray(np.asarray(w_up, f)[0])
    p["w_down"] = np.ascontiguousarray(np.asarray(w_down, f)[0])
    p["gvec"] = np.stack([np.asarray(norm_mix_g, f)[0],
                          np.concatenate([np.asarray(attn_norm_g, f)[0], np.asarray(mlstm_norm_g, f)[0]]),
                          np.asarray(norm_ffn_g, f)[0], np.asarray(norm_final_g, f)])
    mc = np.asarray(mlstm_conv_w, f)[0]
    p["mconv"] = np.ascontiguousarray(mc.reshape(3, 8, 128).transpose(2, 1, 0))
    fc = np.asarray(ffn_conv_w, f)[0]
    p["fconv"] = np.ascontiguousarray(fc.reshape(3, NFC, 128).transpose(2, 1, 0))
    p["gateb"] = np.asarray(gate_b, f).reshape(1, 16)
    rpb = np.asarray(attn_rpb, f)[0]
    q = np.arange(64)[:, None]
    kc = np.arange(64)[None, :]
    dc = np.clip(kc - q + 15, 0, 30)
    g = rpb[:, :, dc]
    g = g.transpose(0, 2, 1, 3).reshape(4, 2, 64, 960)
    p["rpbg"] = np.ascontiguousarray(g.transpose(1, 2, 0, 3).reshape(128, 4, 960))
    return p


_CACHE = {}


def _get_program(seqs=SEQS_FULL, upto="D", debug=False):
    key = (tuple(seqs), upto, debug)
    if key not in _CACHE:
        _CACHE[key] = build_program(seqs, upto, debug)
    return _CACHE[key]


def kernel(x_prompt, x_sample, norm_mix_g, w_in, mlstm_conv_w, gate_b, attn_rpb, attn_norm_g, mlstm_norm_g,
           w_out, norm_ffn_g, w_up, ffn_conv_w, w_down, norm_final_g):
    xp = np.asarray(x_prompt, np.float32)
    xs = np.asarray(x_sample, np.float32)
    common = pack_params(norm_mix_g, w_in, mlstm_conv_w, gate_b, attn_rpb, attn_norm_g, mlstm_norm_g, w_out,
                         norm_ffn_g, w_up, ffn_conv_w, w_down, norm_final_g)
    common.update(host_consts())
    nc, _ = _get_program()
    in_maps = []
    for c in range(NCORES):
        xc = np.concatenate([xp[4 * c:4 * c + 4].reshape(4 * 2048, D), xs[c]], axis=0)
        d = dict(common)
        d["x"] = np.ascontiguousarray(xc)
        in_maps.append(d)
    res = run_bass_kernel_spmd(nc, in_maps, core_ids=list(range(NCORES)))
    yp = np.empty((32, 2048, D), np.float32)
    ys = np.empty((8, 4096, D), np.float32)
    for c in range(NCORES):
        yc = np.asarray(res.results[c]["y"], np.float32)
        yp[4 * c:4 * c + 4] = yc[:8192].reshape(4, 2048, D)
        ys[c] = yc[8192:]
    return yp, ys
```

```python
import math
from contextlib import ExitStack

import numpy as np
import ml_dtypes
import concourse.bass as bass
import concourse.mybir as mybir
from concourse.bass_utils import run_bass_kernel_spmd

F32 = mybir.dt.float32
BF16 = mybir.dt.bfloat16
U8 = mybir.dt.uint8
ALU = mybir.AluOpType
AF = mybir.ActivationFunctionType

NCORES = 8
D = 1024
DIN = 3600
DFF = 2816
NFC = DFF // 128
EPS = 1e-6
SEQS_FULL = (2048, 2048, 2048, 2048, 4096)
NRING = 24
ALL_ENG = ("pe", "act", "dve", "pool", "sp")
MASKV = -30000.0


class Sched:
    def __init__(self, nc):
        self.nc = nc
        self.ops = []
        self.last_w = {}
        self.readers = {}
        self.bar = []
        self.dma_since_bar = {e: [] for e in ALL_ENG}
        self.recent = {e: [] for e in ALL_ENG}

    def add(self, eng, fn, reads=(), writes=(), dma=False, marker=False, extra_deps=()):
        idx = len(self.ops)
        deps = {}
        for r in reads:
            i = self.last_w.get(r)
            if i is not None:
                deps[i] = "raw"
        for w in writes:
            i = self.last_w.get(w)
            if i is not None and i not in deps:
                deps[i] = "waw"
            for i in self.readers.get(w, {}).values():
                if isinstance(i, list):
                    for ii in i:
                        deps.setdefault(ii, "war")
                else:
                    deps.setdefault(i, "war")
        for i in extra_deps:
            deps[i] = "raw"
        for i in self.bar:
            deps.setdefault(i, "raw")
        for w in writes:
            self.last_w[w] = idx
            self.readers[w] = {}
        for r in reads:
            d = self.readers.setdefault(r, {})
            if dma:
                d.setdefault("dma_" + eng, []).append(idx)
            else:
                d[eng] = idx
        keep = []
        for i, typ in deps.items():
            o = self.ops[i]
            if o["eng"] == eng and not o["dma"]:
                if dma:
                    keep.append(i)
                elif o["marker"]:
                    continue
                elif eng == "pe":
                    continue
                else:
                    keep.append(i)
            else:
                keep.append(i)
        self.ops.append(dict(eng=eng, fn=fn, deps=sorted(keep), dma=dma, marker=marker))
        if not dma:
            self.recent[eng] = (self.recent[eng] + [idx])[-2:]
        if dma:
            self.dma_since_bar[eng].append(idx)
        return idx

    def barrier(self):
        marks = []
        for e in ALL_ENG:
            ex = list(self.dma_since_bar[e])
            self.dma_since_bar[e] = []
            marks.append(self.add(e, lambda eng: eng.drain(), marker=True, extra_deps=ex))
        self.bar = marks
        self.last_w = {}
        self.readers = {}

    def emit(self):
        nc = self.nc
        ops = self.ops
        needed = set()
        for o in ops:
            for i in o["deps"]:
                needed.add(i)
        by_eng = {}
        for idx, o in enumerate(ops):
            by_eng.setdefault(o["eng"], []).append(idx)
        sig = {}
        n_sig = {e: 0 for e in by_eng}
        n_dma = {e: 0 for e in by_eng}
        for e, lst in by_eng.items():
            for idx in lst:
                o = ops[idx]
                if o["dma"]:
                    k = n_dma[e]
                    n_dma[e] += 1
                    sig[idx] = ("d", e, k % NRING, 16 * (k // NRING + 1))
                elif idx in needed:
                    n_sig[e] += 1
                    sig[idx] = ("c", e, 0, n_sig[e])
        nwaits = {e: 0 for e in by_eng}
        with ExitStack() as st:
            sems = {}
            for e in by_eng:
                if n_sig[e]:
                    sems[("c", e, 0)] = st.enter_context(nc.semaphore(f"c_{e}"))
                for j in range(min(NRING, n_dma[e])):
                    sems[("d", e, j)] = st.enter_context(nc.semaphore(f"d_{e}_{j}"))
            block = st.enter_context(nc.Block())

            def run(e, eng):
                waited = {}

                def wait(key, val):
                    if waited.get(key, 0) < val:
                        eng.wait_ge(sems[key], val)
                        waited[key] = val
                        nwaits[e] += 1

                for idx in by_eng.get(e, []):
                    o = ops[idx]
                    for i in o["deps"]:
                        s = sig[i]
                        wait((s[0], s[1], s[2]), s[3])
                    s = sig.get(idx)
                    if o["dma"] and s[3] > 16:
                        wait((s[0], s[1], s[2]), s[3] - 16)
                    inst = o["fn"](eng)
                    if s is not None:
                        inst.then_inc(sems[(s[0], s[1], s[2])], 16 if o["dma"] else 1)
                if n_dma.get(e, 0):
                    last = {}
                    for idx in by_eng[e]:
                        s = sig.get(idx)
                        if s is not None and s[0] == "d":
                            last[s[2]] = max(last.get(s[2], 0), s[3])
                    for j, v in last.items():
                        wait(("d", e, j), v)

            if "pe" in by_eng:
                block.tensor(lambda eng: run("pe", eng))
            if "act" in by_eng:
                block.scalar(lambda eng: run("act", eng))
            if "dve" in by_eng:
                block.vector(lambda eng: run("dve", eng))
            if "pool" in by_eng:
                block.gpsimd(lambda eng: run("pool", eng))
            if "sp" in by_eng:
                block.sync(lambda eng: run("sp", eng))
        return {e: (len(l), nwaits[e]) for e, l in by_eng.items()}


_DT_BYTES = {F32: 4, BF16: 2, U8: 1}


class Ctx:
    def __init__(self, nc, S, sbt, banks):
        self.nc, self.S, self.sbt, self.banks = nc, S, sbt, banks
        self.off = 0
        self.cap = sbt.shape[1]

    def reset(self):
        self.off = 0

    def sb(self, shape, dt):
        n = 1
        for s in shape[1:]:
            n *= s
        nb = n * _DT_BYTES[dt]
        nb_al = (nb + 63) // 64 * 64
        assert self.off + nb_al <= self.cap, f"SBUF overflow: {self.off + nb_al} > {self.cap}"
        ap = self.sbt[0:shape[0], self.off:self.off + nb].bitcast(dt)
        self.off += nb_al
        if len(shape) == 3:
            ap = ap.rearrange("p (a b) -> p a b", b=shape[2])
        elif len(shape) == 4:
            ap = ap.rearrange("p (a b c) -> p a b c", b=shape[2], c=shape[3])
        return ap

    def ps(self, bank, dt, nbanks=1):
        t = self.banks[bank]
        assert nbanks == 1
        return t[:, :].bitcast(dt)

    def mm(self, out, lhsT, rhs, start, stop, r, w):
        self.S.add("pe", lambda e: e.matmul(out, lhsT=lhsT, rhs=rhs, start=start, stop=stop), reads=r, writes=w)

    def tr(self, out, in_, ident, r, w):
        self.S.add("pe", lambda e: e.transpose(out=out, in_=in_, identity=ident), reads=r, writes=w)

    def act(self, out, in_, func, r, w, scale=1.0, bias=None, accum=None):
        kw = {}
        if bias is not None:
            kw["bias"] = bias
        if accum is not None:
            kw["accum_out"] = accum
        self.S.add("act", lambda e: e.activation(out=out, in_=in_, func=func, scale=scale, **kw), reads=r, writes=w)

    def copy(self, eng, out, in_, r, w):
        if eng == "act":
            self.S.add("act", lambda e: e.activation(out=out, in_=in_, func=AF.Copy), reads=r, writes=w)
        else:
            self.S.add(eng, lambda e: e.tensor_copy(out=out, in_=in_), reads=r, writes=w)

    def tt(self, eng, out, in0, in1, op, r, w):
        self.S.add(eng, lambda e: e.tensor_tensor(out=out, in0=in0, in1=in1, op=op), reads=r, writes=w)

    def ts(self, eng, out, in0, s1, s2, op0, op1, r, w):
        if s2 is None:
            self.S.add(eng, lambda e: e.tensor_scalar(out=out, in0=in0, scalar1=s1, scalar2=None, op0=op0), reads=r, writes=w)
        else:
            self.S.add(eng, lambda e: e.tensor_scalar(out=out, in0=in0, scalar1=s1, scalar2=s2, op0=op0, op1=op1), reads=r, writes=w)

    def stt(self, out, in0, scalar, in1, op0, op1, r, w):
        self.S.add("dve", lambda e: e.scalar_tensor_tensor(out=out, in0=in0, scalar=scalar, in1=in1, op0=op0, op1=op1), reads=r, writes=w)

    def memset(self, eng, ap, val, w):
        self.S.add(eng, lambda e: e.memset(ap, val), writes=w)

    def recip(self, out, in_, r, w):
        self.S.add("dve", lambda e: e.reciprocal(out=out, in_=in_), reads=r, writes=w)

    def dma(self, out, in_, r, w, eng="sp"):
        self.S.add(eng, lambda e: e.dma_start(out=out, in_=in_), reads=r, writes=w, dma=True)

    def rstd(self, ss, ms, rs, nh, n, r_name, tag):
        self.ts("dve", ms, ss, 1.0 / n, EPS, ALU.mult, ALU.add, [r_name], [("ms", tag)])
        self.tt("pool", rs, ms, nh, ALU.pow, [("ms", tag), "nh"], [("rs", tag)])


def bc(ap, shape):
    return ap.to_broadcast(list(shape))


def build_program(seqs=SEQS_FULL, upto="D", debug=False):
    NTOK = sum(seqs)
    nc = bass.Bass("TRN2", target_bir_lowering=False)
    dram = {}

    def din(name, shape, dt=F32):
        dram[name] = nc.dram_tensor(name, list(shape), dt, kind="ExternalInput").ap()
        return dram[name]

    def dscr(name, shape, dt):
        kind = "ExternalOutput" if debug else "Internal"
        dram[name] = nc.dram_tensor(name, list(shape), dt, kind=kind).ap()
        return dram[name]

    x = din("x", [NTOK, D])
    w_in = din("w_in", [D, DIN])
    w_out = din("w_out", [D, D])
    w_up = din("w_up", [D, 2 * DFF])
    w_down = din("w_down", [DFF, D])
    gvec = din("gvec", [4, D])
    mconv = din("mconv", [128, 8, 3])
    fconv = din("fconv", [128, NFC, 3])
    gateb = din("gateb", [1, 16])
    rpbg = din("rpbg", [128, 4, 960])
    c_ident = din("c_ident", [128, 128], BF16)
    c_mask = din("c_mask", [128, 960])
    c_mats = din("c_mats", [4, 128, 128])
    c_tri = din("c_tri", [2, 128, 64], BF16)
    c_cadd = din("c_cadd", [128, 16])
    y = nc.dram_tensor("y", [NTOK, D], F32, kind="ExternalOutput").ap()
    dram["y"] = y

    QBD = dscr("QBD", [4, 128, NTOK * 2], BF16)
    XT = dscr("XT", [12, 128, NTOK], BF16)
    VA = dscr("VA", [NTOK, 520], BF16)
    VM = dscr("VM", [NTOK, 516], BF16)
    OM = dscr("OM", [NTOK, 512], BF16)
    GT = dscr("GT", [NTOK, 16], F32)
    QMT = dscr("QMT", [8, 128, NTOK], BF16)
    KTOK = dscr("KTOK", [NTOK, 512], BF16)
    Y = dscr("Y", [NTOK, D], BF16)
    XMID = dscr("XMID", [NTOK, D], F32)

    S = Sched(nc)
    with ExitStack() as st:
        sbt = st.enter_context(nc.sbuf_tensor("arena", [128, 206 * 1024], U8))
        banks = [st.enter_context(nc.psum_tensor(f"bank{i}", [128, 2048], U8)) for i in range(8)]
        k = Ctx(nc, S, sbt, banks)
        phases = ["A", "B1", "B2a", "B2", "C", "D"]
        fns = {"A": phase_A, "B1": phase_B1, "B2a": phase_B2a, "B2": phase_B2, "C": phase_C, "D": phase_D}
        for ph in phases:
            k.reset()
            fns[ph](k, dram, seqs)
            S.barrier()
            if ph == upto:
                break
        stats = S.emit()
    return nc, stats


def load_bcast_row(k, dst, src_row, w):
    P = dst.shape[0]
    k.dma(dst, src_row.partition_broadcast(P), [], w)


def phase_A(k, dram, seqs):
    NTOK = sum(seqs)
    x, w_in, gvec, ident_d = dram["x"], dram["w_in"], dram["gvec"], dram["c_ident"]
    QBD, XT, VA, VM, OM, GT = (dram[n] for n in ("QBD", "XT", "VA", "VM", "OM", "GT"))
    W = k.sb([128, 8, DIN], BF16)
    gm = k.sb([128, D], F32)
    ident = k.sb([128, 128], BF16)
    nh = k.sb([128, 4], F32)
    xt = [k.sb([128, D], F32) for _ in range(3)]
    sq = k.sb([128, D], BF16)
    ss = [k.sb([128, 1], F32) for _ in range(2)]
    ms = [k.sb([128, 1], F32) for _ in range(2)]
    rs = [k.sb([128, 1], F32) for _ in range(2)]
    xn = [k.sb([128, D], BF16) for _ in range(2)]
    xnT = [k.sb([128, 8, 512], BF16) for _ in range(2)]
    st_va = [k.sb([128, 8, 65], BF16) for _ in range(2)]
    st_vm = [k.sb([128, 4, 129], BF16) for _ in range(2)]
    st_om = [k.sb([128, 512], BF16) for _ in range(2)]
    st_g = [k.sb([128, 16], F32) for _ in range(2)]
    st_qbd = [k.sb([128, 8, 128], BF16) for _ in range(2)]
    st_fm = [k.sb([128, 512], BF16) for _ in range(3)]
    pT = k.ps(0, BF16).rearrange("p (a b) -> p a b", b=128)
    ptm = [k.ps(1 + i, F32) for i in range(3)]
    pfm = [k.ps(4 + i, F32) for i in range(3)]

    k.dma(ident, ident_d, [], ["ident"])
    load_bcast_row(k, gm, gvec[0:1, :], ["gm"])
    k.memset("pool", nh, -0.5, ["nh"])
    for b in range(2):
        k.memset("pool", st_va[b], 1.0, [("st_va", b)])
        k.memset("pool", st_vm[b], 1.0, [("st_vm", b)])
        k.memset("pool", st_qbd[b], 0.0, [("st_qbd", b)])
    blocks = [(1024, 1536), (2560, 3072), (3072, 3584), (3584, 3600), (0, 512), (512, 1024), (1536, 2048), (2048, 2560)]
    wsrc = w_in.rearrange("(kc p) n -> p kc n", p=128)

    def wname(col):
        return ("W", col // 512)

    for (c0, c1) in blocks:
        k.dma(W[:, :, c0:c1], wsrc[:, :, c0:c1], [], [wname(c0)], eng="pool")

    NT = NTOK // 128
    NM = NTOK // 512
    TM = [("va", 1024, 512), ("vm", 2560, 512), ("om", 3072, 512), ("g", 3584, 16)]
    FM = [("qa", i, 128 * i) for i in range(4)] + [("x", i, 512 + 128 * i) for i in range(4)] + \
         [("x", 4 + i, 1536 + 128 * i) for i in range(8)]
    cnt = {"tm": 0, "fm": 0, "qbd": 0, "stfm": 0}

    def L(t):
        k.dma(xt[t % 3], x[t * 128:(t + 1) * 128, :], [], [("xt", t % 3)])

    def N(t):
        b = t % 2
        k.act(sq, xt[t % 3], AF.Square, [("xt", t % 3)], ["sq", ("ss", b)], accum=ss[b])
        k.ts("dve", ms[b], ss[b], 1.0 / D, EPS, ALU.mult, ALU.add, [("ss", b)], [("ms", b)])
        k.tt("pool", rs[b], ms[b], nh[:, 0:1], ALU.pow, [("ms", b), "nh"], [("rs", b)])
        k.stt(xn[b], xt[t % 3], rs[b], gm, ALU.mult, ALU.mult, [("xt", t % 3), ("rs", b), "gm"], [("xn", b)])

    def T(t):
        b = t % 2
        m, j = divmod(t, 4)
        for kc in range(8):
            k.tr(pT[:, kc, :], xn[b][:, kc * 128:(kc + 1) * 128], ident, [("xn", b), "ident"], ["pT"])
        k.copy("act", xnT[m % 2][:, :, j * 128:(j + 1) * 128], pT, ["pT"], [("xnT", m % 2, j)])

    def Mgrp(t, gi):
        b = t % 2
        m, j = divmod(t, 4)
        name, c0, ncol = TM[gi]
        pi = cnt["tm"] % 3
        cnt["tm"] += 1
        ps = ptm[pi]
        for kc in range(8):
            k.mm(ps[:, 0:ncol], xnT[m % 2][:, kc, j * 128:(j + 1) * 128], W[:, kc, c0:c0 + ncol], kc == 0, kc == 7,
                 [("xnT", m % 2, j), wname(c0)], [("ptm", pi)])
        tok = slice(t * 128, (t + 1) * 128)
        if name == "va":
            k.copy("dve", st_va[b][:, :, 0:64], ps.rearrange("p (a b) -> p a b", b=64), [("ptm", pi)], [("st_va", b)])
            k.dma(VA[tok, :], st_va[b].rearrange("p a b -> p (a b)"), [("st_va", b)], [])
        elif name == "vm":
            k.copy("act", st_vm[b][:, :, 0:128], ps.rearrange("p (a b) -> p a b", b=128), [("ptm", pi)], [("st_vm", b)])
            k.dma(VM[tok, :], st_vm[b].rearrange("p a b -> p (a b)"), [("st_vm", b)], [])
        elif name == "om":
            k.copy("dve", st_om[b], ps, [("ptm", pi)], [("st_om", b)])
            k.dma(OM[tok, :], st_om[b], [("st_om", b)], [])
        else:
            k.copy("act", st_g[b], ps[:, 0:16], [("ptm", pi)], [("st_g", b)])
            k.dma(GT[tok, :], st_g[b], [("st_g", b)], [])

    def Fchunk(m, fi):
        kind, idx, col = FM[fi]
        pi = cnt["fm"] % 3
        cnt["fm"] += 1
        ps = pfm[pi]
        for kc in range(8):
            k.mm(ps, W[:, kc, col:col + 128], xnT[m % 2][:, kc, :], kc == 0, kc == 7,
                 [("xnT", m % 2, j) for j in range(4)] + [wname(col)], [("pfm", pi)])
        if kind == "qa":
            qi = cnt["qbd"] % 2
            cnt["qbd"] += 1
            psv = ps.rearrange("p (r c) -> p r c", c=64)
            k.copy("dve", st_qbd[qi][0:64, :, 0:64], psv[0:64], [("pfm", pi)], [("st_qbd", qi)])
            k.copy("act", st_qbd[qi][64:128, :, 64:128], psv[64:128], [("pfm", pi)], [("st_qbd", qi)])
            k.dma(QBD[idx][:, m * 1024:(m + 1) * 1024], st_qbd[qi].rearrange("p a b -> p (a b)"), [("st_qbd", qi)], [])
        else:
            si = cnt["stfm"] % 3
            cnt["stfm"] += 1
            k.copy("act" if (fi % 2) else "dve", st_fm[si], ps, [("pfm", pi)], [("st_fm", si)])
            k.dma(XT[idx][:, m * 512:(m + 1) * 512], st_fm[si], [("st_fm", si)], [])

    L(0)
    if NT > 1:
        L(1)
    N(0)
    T(0)
    for t in range(NT):
        m, j = divmod(t, 4)
        if t + 2 < NT:
            L(t + 2)
        if t + 1 < NT:
            N(t + 1)
        Mgrp(t, 0)
        Mgrp(t, 1)
        if j < 3 and t + 1 < NT:
            T(t + 1)
        Mgrp(t, 2)
        Mgrp(t, 3)
        if j == 3:
            for fi in range(16):
                Fchunk(m, fi)
                if fi == 7 and t + 1 < NT:
                    T(t + 1)


def phase_B1(k, dram, seqs):
    QBD, XT, VA, Y, gvec = dram["QBD"], dram["XT"], dram["VA"], dram["Y"], dram["gvec"]
    TMAX = max(seqs)
    NTMAX = TMAX // 128
    ident = k.sb([128, 128], BF16)
    nh = k.sb([128, 4], F32)
    gc = k.sb([64, 512], F32)
    bext = k.sb([128, 4, 960], BF16)
    kT = k.sb([128, 4, TMAX], BF16)
    vA = k.sb([128, NTMAX, 520], BF16)
    vB = k.sb([128, NTMAX, 520], BF16)
    qb = [k.sb([128, 4, 1024], BF16) for _ in range(2)]
    pexp = [k.sb([128, 512], BF16) for _ in range(2)]
    PT = [k.sb([128, 4, 128], BF16) for _ in range(2)]
    rinv = [k.sb([64, 8], F32) for _ in range(2)]
    ya = [k.sb([64, 512], F32) for _ in range(2)]
    sqj = k.sb([64, 512], BF16)
    ss = [k.sb([64, 1], F32) for _ in range(2)]
    ms = [k.sb([64, 1], F32) for _ in range(2)]
    rs = [k.sb([64, 1], F32) for _ in range(2)]
    yst = [k.sb([64, 512], BF16) for _ in range(2)]
    rp = k.sb([128, 4, 960], F32)
    mk = k.sb([128, 960], F32)
    pS = [k.ps(i, F32) for i in range(2)]
    pPT = [k.ps(2 + i, BF16)[:, 0:512].rearrange("p (a b) -> p a b", b=128) for i in range(2)]
    pY = [[k.ps(4 + 2 * i + b, F32) for b in range(2)] for i in range(2)]

    k.dma(ident, dram["c_ident"], [], ["ident"])
    k.memset("pool", nh, -0.5, ["nh"])
    load_bcast_row(k, gc, gvec[1:2, 0:512], ["gc"])
    k.dma(rp, dram["rpbg"], [], ["rp"])
    k.dma(mk, dram["c_mask"], [], ["mk"])
    for p in range(4):
        k.stt(bext[:, p, :], rp[:, p, :], 8.0, mk, ALU.mult, ALU.add, ["rp", "mk"], ["bext"])

    tok0 = 0
    gcount = 0
    for T in seqs:
        R = T // 64
        NT = T // 128
        for p in range(4):
            k.dma(kT[:, p, 0:T], XT[p][:, tok0:tok0 + T], [], ["kT"])
        k.dma(vA[:, 0:NT, :], VA[tok0:tok0 + T, :].rearrange("(n p) c -> p n c", p=128), [], ["vA"])
        k.dma(vB[:, 0:NT - 1, :], VA[tok0 + 64:tok0 + T - 64, :].rearrange("(n p) c -> p n c", p=128), [], ["vB"])
        units = [(r, p) for r in range(R) for p in range(4)]
        U = len(units)

        def loadq(g):
            gi = (gcount + g) % 2
            c0 = (tok0 // 64 + 8 * g) * 128
            k.dma(qb[gi], QBD[:, :, c0:c0 + 1024].rearrange("c p t -> p c t"), [], [("qb", gi)])

        def Sst(u):
            r, p = units[u]
            g = r // 8
            gi = (gcount + g) % 2
            if r % 8 == 0 and p == 0 and g + 1 < R // 8:
                loadq(g + 1)
            rs_ = min(max(r - 4, 0), R - 8)
            off = r - rs_
            bcol = (7 - off) * 64
            ps = pS[u % 2]
            k.mm(ps, ident, bext[:, p, bcol:bcol + 512], True, False, ["ident", "bext"], [("pS", u % 2)])
            k.mm(ps, qb[gi][:, p, (r % 8) * 128:(r % 8) * 128 + 128], kT[:, p, rs_ * 64:rs_ * 64 + 512], False, True,
                 [("qb", gi), "kT"], [("pS", u % 2)])
            k.act(pexp[u % 2], ps, AF.Exp, [("pS", u % 2)], [("pexp", u % 2)], scale=0.125)

        def Tst(u):
            for kc in range(4):
                k.tr(pPT[u % 2][:, kc, :], pexp[u % 2][:, kc * 128:(kc + 1) * 128], ident, [("pexp", u % 2), "ident"],
                     [("pPT", u % 2)])
            k.copy("dve", PT[u % 2], pPT[u % 2], [("pPT", u % 2)], [("PT", u % 2)])

        def PVst(u):
            r, p = units[u]
            rs_ = min(max(r - 4, 0), R - 8)
            ry = r % 2
            for hl in range(2):
                h = 2 * p + hl
                bank = h // 4
                col = (h % 4) * 65
                for kc in range(4):
                    if rs_ % 2 == 0:
                        vt, vn = vA, "vA"
                        ti = rs_ // 2 + kc
                    else:
                        vt, vn = vB, "vB"
                        ti = (rs_ - 1) // 2 + kc
                    k.mm(pY[ry][bank][0:64, col:col + 65], PT[u % 2][:, kc, hl * 64:(hl + 1) * 64],
                         vt[:, ti, h * 65:(h + 1) * 65], kc == 0, kc == 3, [("PT", u % 2), vn], [("pY", ry, bank)])
            if p == 3:
                pyv = [pY[ry][b_][0:64, 0:260].rearrange("p (h e) -> p h e", e=65) for b_ in range(2)]
                ytok = tok0 + r * 64

                def e1(ry=ry, pyv=pyv):
                    for b_ in range(2):
                        k.recip(rinv[ry][:, 4 * b_:4 * b_ + 4], pyv[b_][:, :, 64], [("pY", ry, b_)], [("rinv", ry)])
                    for b_ in range(2):
                        k.tt("dve", ya[ry][:, 256 * b_:256 * b_ + 256].rearrange("p (h e) -> p h e", e=64),
                             pyv[b_][:, :, 0:64], bc(rinv[ry][:, 4 * b_:4 * b_ + 4].unsqueeze(2), [64, 4, 64]), ALU.mult,
                             [("pY", ry, b_), ("rinv", ry)], [("ya", ry)])

                def e2(ry=ry):
                    k.act(sqj, ya[ry], AF.Square, [("ya", ry)], ["sqj", ("ss", ry)], accum=ss[ry])

                def e3(ry=ry):
                    k.ts("dve", ms[ry], ss[ry], 1.0 / 512, EPS, ALU.mult, ALU.add, [("ss", ry)], [("ms", ry)])
                    k.tt("pool", rs[ry], ms[ry], nh[0:64, 0:1], ALU.pow, [("ms", ry), "nh"], [("rs", ry)])

                def e4(ry=ry, ytok=ytok):
                    k.stt(yst[ry], ya[ry], rs[ry], gc, ALU.mult, ALU.mult, [("ya", ry), ("rs", ry), "gc"], [("yst", ry)])
                    k.dma(Y[ytok:ytok + 64, 0:512], yst[ry], [("yst", ry)], [])

                deferred.append((u + 1, e1))
                deferred.append((u + 2, e2))
                deferred.append((u + 3, e3))
                deferred.append((u + 4, e4))

        loadq(0)
        deferred = []
        for u in range(-2, U + 5):
            if 0 <= u + 2 < U:
                Sst(u + 2)
            if 0 <= u + 1 < U:
                Tst(u + 1)
            if 0 <= u < U:
                PVst(u)
            for (du, fn) in list(deferred):
                if du <= u:
                    fn()
                    deferred.remove((du, fn))
        assert not deferred
        gcount += R // 8
        tok0 += T


def phase_B2a(k, dram, seqs):
    XT, QMT, KTOK = dram["XT"], dram["QMT"], dram["KTOK"]
    ident = k.sb([128, 128], BF16)
    cw = k.sb([128, 8, 3], F32)
    raw = [k.sb([128, 8, 514], BF16) for _ in range(2)]
    tcv = [k.sb([128, 512], F32) for _ in range(2)]
    qk = [k.sb([128, 8, 512], BF16) for _ in range(2)]
    kst = [k.sb([128, 4, 512], BF16) for _ in range(2)]
    pK = [k.ps(i, BF16)[:, 0:512].rearrange("p (a b) -> p a b", b=128) for i in range(2)]
    k.dma(ident, dram["c_ident"], [], ["ident"])
    k.dma(cw, dram["mconv"], [], ["cw"])
    mts = []
    tok0 = 0
    for T in seqs:
        for m in range(T // 512):
            mts.append((tok0 + m * 512, m == 0, m == T // 512 - 1))
        tok0 += T

    def loadraw(mi):
        a, first, last = mts[mi]
        b = mi % 2
        lo = 1 if first else 0
        hi = 513 if last else 514
        if first:
            k.memset("pool", raw[b][:, :, 0:1], 0.0, [("raw", b)])
        if last:
            k.memset("pool", raw[b][:, :, 513:514], 0.0, [("raw", b)])
        k.dma(raw[b][:, :, lo:hi], XT[4:12, :, a - 1 + lo:a - 1 + hi].rearrange("c p t -> p c t"), [], [("raw", b)])

    nt = 0
    loadraw(0)
    for mi in range(len(mts)):
        a, first, last = mts[mi]
        b = mi % 2
        if mi + 1 < len(mts):
            loadraw(mi + 1)
        for c in range(8):
            tb = (mi * 8 + c) % 2
            k.ts("dve", tcv[tb], raw[b][:, c, 1:513], cw[:, c, 1:2], None, ALU.mult, None, [("raw", b), "cw"], [("tcv", tb)])
            k.stt(tcv[tb], raw[b][:, c, 0:512], cw[:, c, 0:1], tcv[tb], ALU.mult, ALU.add, [("raw", b), "cw", ("tcv", tb)], [("tcv", tb)])
            k.stt(tcv[tb], raw[b][:, c, 2:514], cw[:, c, 2:3], tcv[tb], ALU.mult, ALU.add, [("raw", b), "cw", ("tcv", tb)], [("tcv", tb)])
            k.act(qk[b][:, c, :], tcv[tb], AF.Silu, [("tcv", tb)], [("qk", b, c)])
        k.dma(QMT[:, :, a:a + 512].rearrange("c p t -> p c t"), qk[b], [("qk", b, c) for c in range(8)], [])
        for j in range(4):
            pb = nt % 2
            nt += 1
            for h in range(4):
                k.tr(pK[pb][:, h, :], qk[b][:, 4 + h, j * 128:(j + 1) * 128], ident, [("qk", b, 4 + h), "ident"], [("pK", pb)])
            k.copy("act", kst[b][:, j, :], pK[pb].rearrange("p a b -> p (a b)"), [("pK", pb)], [("kst", b)])
        k.dma(KTOK[a:a + 512, :].rearrange("(j p) c -> p j c", p=128), kst[b], [("kst", b)], [])


def phase_B2(k, dram, seqs):
    QMT, KTOK, VM, OM, GT, Y, gvec = (dram[n] for n in ("QMT", "KTOK", "VM", "OM", "GT", "Y", "gvec"))
    TMAX = max(seqs)
    NTm = TMAX // 128
    mats = k.sb([128, 4, 128], F32)
    tri = k.sb([128, 2, 64], BF16)
    gbb = k.sb([128, 16], F32)
    cadd = k.sb([128, 16], F32)
    nh = k.sb([128, 4], F32)
    gcm = k.sb([128, 512], F32)
    G = k.sb([128, NTm, 16], F32)
    Z = k.sb([128, NTm, 16], F32)
    Lf = k.sb([128, NTm * 8], F32)
    REM = k.sb([128, NTm * 8], F32)
    FL = k.sb([128, NTm * 8], F32)
    WVz = k.sb([128, 2, NTm * 8], F32)
    tmpi = k.sb([128, NTm * 8], F32)
    GD = k.sb([128, 2, NTm * 8], F32)
    hbuf = k.sb([128, NTm, 512], F32)
    C = [[k.sb([128, 4, 129], F32) for _ in range(2)] for _ in range(2)]
    cver = [0, 0]
    Cg = [k.sb([128, 4, 129], F32) for _ in range(2)]
    Cs = [[k.sb([128, 4, 129], BF16) for _ in range(2)] for _ in range(2)]
    qkT = [[k.sb([128, 8, 128], BF16) for _ in range(3)] for _ in range(2)]
    ktk = [[k.sb([128, 512], BF16) for _ in range(3)] for _ in range(2)]
    vau = [[k.sb([128, 4, 129], BF16) for _ in range(3)] for _ in range(2)]
    veg = [[k.sb([128, 2, 4, 129], BF16) for _ in range(2)] for _ in range(2)]
    a0T = [k.sb([128, 4, 64], BF16) for _ in range(2)]
    den = [k.sb([128, 4], F32) for _ in range(2)]
    rden = [k.sb([128, 4], F32) for _ in range(2)]
    htmp = [k.sb([128, 512], F32) for _ in range(2)]
    omt = [k.sb([128, 512], BF16) for _ in range(2)]
    sqj = [k.sb([128, 128], BF16) for _ in range(4)]
    ssm = [k.sb([128, 4], F32) for _ in range(2)]
    msm = [k.sb([128, 4], F32) for _ in range(2)]
    rsm = [k.sb([128, 4], F32) for _ in range(2)]
    sg = [k.sb([128, 512], F32) for _ in range(2)]
    t1 = [k.sb([128, 512], F32) for _ in range(2)]
    yst = [k.sb([128, 512], BF16) for _ in range(2)]
    bk = [k.ps(i, F32) for i in range(8)]

    def BK(i):
        return ("bk", i)

    k.dma(mats, dram["c_mats"].rearrange("m p t -> p m t"), [], ["mats"])
    k.dma(tri, dram["c_tri"].rearrange("m p t -> p m t"), [], ["tri"])
    load_bcast_row(k, gbb, dram["gateb"], ["gbb"])
    k.dma(cadd, dram["c_cadd"], [], ["cadd"])
    load_bcast_row(k, gcm, gvec[1:2, 512:1024], ["gcm"])
    k.memset("pool", nh, -0.5, ["nh"])
    k.tt("dve", gbb, gbb, cadd, ALU.add, ["gbb", "cadd"], ["gbb"])

    tok0 = 0
    cc = 0
    lc = [0, 0]
    ecs = {"n": 0}
    for T in seqs:
        NT = T // 128
        N8 = NT * 8
        k.dma(G[:, 0:NT, :], GT[tok0:tok0 + T, :].rearrange("(n p) g -> p n g", p=128), [], ["G"])
        k.tt("dve", Z[:, 0:NT, :], G[:, 0:NT, :], bc(gbb.unsqueeze(1), [128, NT, 16]), ALU.add, ["G", "gbb"], ["Z"])
        Zv = Z[:, 0:NT, :].rearrange("p n (d e) -> p n d e", e=8)

        def v4(t):
            return t[:, 0:N8].rearrange("p (n d e) -> p n d e", d=2, e=4)

        k.act(v4(Lf), Zv[:, :, :, 4:8], AF.Exp, ["Z"], ["Lf"], scale=-1.0)
        k.act(Lf[:, 0:N8], Lf[:, 0:N8], AF.Ln, ["Lf"], ["Lf"], bias=1.0)
        for i in range(4):
            k.mm(bk[i][:, 0:N8], mats[:, i, :], Lf[:, 0:N8], True, True, ["mats", "Lf"], [BK(i)])
        k.copy("act", v4(REM)[:, :, 0, :], v4(bk[0])[:, :, 0, :], [BK(0)], ["REM"])
        k.copy("dve", v4(REM)[:, :, 1, :], v4(bk[1])[:, :, 1, :], [BK(1)], ["REM"])
        k.act(FL[:, 0:N8], REM[:, 0:N8], AF.Exp, ["REM"], ["FL"], scale=-1.0)
        k.tt("dve", v4(tmpi), Zv[:, :, :, 0:4], v4(REM), ALU.subtract, ["Z", "REM"], ["tmpi"])
        k.memset("pool", WVz[:, :, 0:N8], 0.0, ["WV"])
        k.act(WVz[0:64, 0, 0:N8], tmpi[0:64, 0:N8], AF.Exp, ["tmpi"], ["WV"])
        k.act(WVz[64:128, 1, 0:N8], tmpi[64:128, 0:N8], AF.Exp, ["tmpi"], ["WV"])
        k.act(GD[:, 0, 0:N8], bk[2][:, 0:N8], AF.Exp, [BK(2)], ["GD"], scale=-1.0)
        k.act(GD[:, 1, 0:N8], bk[3][:, 0:N8], AF.Exp, [BK(3)], ["GD"], scale=-1.0)
        for d in range(2):
            k.memset("pool", C[d][cver[d]], 0.0, [("C", d, cver[d])])

        tiles = [list(range(NT)), list(range(NT - 1, -1, -1))]
        visit = {}
        for j in range(NT):
            visit[(0, tiles[0][j])] = 2 * j
            visit[(1, tiles[1][j])] = 2 * j + 1

        def load(d, j):
            n = tiles[d][j]
            s = lc[d] % 3
            lc[d] += 1
            t0_ = tok0 + n * 128
            k.dma(qkT[d][s], QMT[:, :, t0_:t0_ + 128].rearrange("c p t -> p c t"), [], [("qkT", d, s)])
            k.dma(ktk[d][s], KTOK[t0_:t0_ + 128, :], [], [("ktk", d, s)])
            k.dma(vau[d][s].rearrange("p a b -> p (a b)"), VM[t0_:t0_ + 128, :], [], [("vau", d, s)])
            return s

        slots = {}
        steps = []
        for j in range(NT):
            for d in range(2):
                steps.append((d, j, tiles[d][j]))

        def sfront(i):
            d, j, n = steps[i]
            rb = d
            s = slots[(d, j)]
            qn = ("qkT", d, s)
            for hf in range(2):
                pr = slice(hf * 64, hf * 64 + 64)
                for h in range(4):
                    k.mm(bk[rb][pr, h * 64:(h + 1) * 64], qkT[d][s][:, 4 + h, pr], qkT[d][s][:, h, pr], True, True, [qn], [BK(rb)])
            k.tt("dve", a0T[rb], bk[rb][:, 0:256].rearrange("p (a b) -> p a b", b=64),
                 bc(tri[:, d, :].unsqueeze(1), [128, 4, 64]), ALU.mult, [BK(rb), "tri"], [("a0T", rb)])

        def half_state(i, hx):
            d, j, n = steps[i]
            rb = d
            vb = (i // 2) % 2
            s = slots[(d, j)]
            gi = n * 8 + d * 4
            qn, kn = ("qkT", d, s), ("ktk", d, s)
            hf = ((0, 1) if d == 0 else (1, 0))[hx]
            pr = slice(hf * 64, hf * 64 + 64)
            v = cver[d]
            Cc, Cn = C[d][v], C[d][1 - v]
            cver[d] = 1 - v
            vg = veg[rb][vb][:, hf]
            vgn = ("veg", rb, vb)
            k.tt("dve", Cg[d], Cc, bc(GD[:, hf, gi:gi + 4].unsqueeze(2), [128, 4, 129]), ALU.mult, [("C", d, v), "GD"], [("Cg", d)])
            k.copy("act", Cs[d][v], Cg[d], [("Cg", d)], [("Cs", d, v)])
            ub = 6 + d
            for h in range(4):
                k.mm(bk[ub][:, h * 128:(h + 1) * 128], ktk[d][s][:, h * 128:(h + 1) * 128], vg[:, h, 0:128], True, True,
                     [kn, vgn], [BK(ub)])
            for h in range(4):
                c0 = 272 + 2 * h
                k.mm(bk[rb][:, c0:c0 + 2], ktk[d][s][:, h * 128:(h + 1) * 128], vg[:, h, 127:129], True, True,
                     [kn, vgn], [BK(rb)])
            k.tt("dve", Cn[:, :, 0:128], Cg[d][:, :, 0:128], bk[ub].rearrange("p (a b) -> p a b", b=128), ALU.add,
                 [("Cg", d), BK(ub)], [("C", d, 1 - v)])
            nv = bk[rb][:, 272:280].rearrange("p (h t) -> p h t", t=2)
            k.tt("dve", Cn[:, :, 128], Cg[d][:, :, 128], nv[:, :, 1], ALU.add, [("Cg", d), BK(rb)], [("C", d, 1 - v)])

            def pmm():
                pb_ = 2 + 2 * rb + vb
                for h in range(4):
                    o = bk[pb_][pr, h * 128:(h + 1) * 128]
                    k.mm(o, qkT[d][s][:, h, pr], Cs[d][v][:, h, 0:128], True, False, [qn, ("Cs", d, v)], [BK(pb_)])
                    k.mm(o, a0T[rb][:, h, :], vg[:, h, 0:128], False, True, [("a0T", rb), vgn], [BK(pb_)])
                for h in range(4):
                    c0 = 256 + 8 * vb + 2 * h
                    o = bk[rb][pr, c0:c0 + 2]
                    k.mm(o, qkT[d][s][:, h, pr], Cs[d][v][:, h, 127:129], True, False, [qn, ("Cs", d, v)], [BK(rb)])
                    k.mm(o, a0T[rb][:, h, :], vg[:, h, 127:129], False, True, [("a0T", rb), vgn], [BK(rb)])
            return pmm

        def vegop(i):
            d, j, n = steps[i]
            rb = d
            vb = (i // 2) % 2
            s = slots[(d, j)]
            gi = n * 8 + d * 4
            for hf in range(2):
                k.tt("pool", veg[rb][vb][:, hf], vau[d][s], bc(WVz[:, hf, gi:gi + 4].unsqueeze(2), [128, 4, 129]), ALU.mult,
                     [("vau", d, s), "WV"], [("veg", rb, vb)])

        def back_a(i):
            d, j, n = steps[i]
            rb = d
            vb = (i // 2) % 2
            dv = bk[rb][:, 256 + 8 * vb:256 + 8 * vb + 8].rearrange("p (h t) -> p h t", t=2)
            k.act(den[rb], dv[:, :, 1], AF.Abs, [BK(rb)], [("den", rb)])

        def back_b(i):
            d, j, n = steps[i]
            rb = d
            gi = n * 8 + d * 4
            k.tt("dve", den[rb], den[rb], FL[:, gi:gi + 4], ALU.max, [("den", rb), "FL"], [("den", rb)])
            k.recip(rden[rb], den[rb], [("den", rb)], [("rden", rb)])

        def back_c(i):
            d, j, n = steps[i]
            rb = d
            first = visit[(d, n)] < visit[(1 - d, n)]
            vb = (i // 2) % 2
            pb_ = 2 + 2 * rb + vb
            if d == 1:
                dst = hbuf[:, n, :] if first else htmp[rb]
                wn = [("hbuf", n, h) for h in range(4)] if first else [("htmp", rb, h) for h in range(4)]
                k.tt("dve", dst.rearrange("p (a b) -> p a b", b=128), bk[pb_].rearrange("p (a b) -> p a b", b=128),
                     bc(rden[rb].unsqueeze(2), [128, 4, 128]), ALU.mult, [BK(pb_), ("rden", rb)], wn)
                return
            for h in range(4):
                dst = hbuf[:, n, h * 128:(h + 1) * 128] if first else htmp[rb][:, h * 128:(h + 1) * 128]
                k.act(dst, bk[pb_][:, h * 128:(h + 1) * 128], AF.Copy, [BK(pb_), ("rden", rb)],
                      [("hbuf", n, h) if first else ("htmp", rb, h)], scale=rden[rb][:, h:h + 1])

        def back_d(i):
            d, j, n = steps[i]
            rb = d
            first = visit[(d, n)] < visit[(1 - d, n)]
            if not first:
                k.tt("dve", hbuf[:, n, :], hbuf[:, n, :], htmp[rb], ALU.add,
                     [("hbuf", n, h) for h in range(4)] + [("htmp", rb, h) for h in range(4)], [("hbuf", n, h) for h in range(4)])

        for j in range(min(2, NT)):
            for d in range(2):
                slots[(d, j)] = load(d, j)
        nsteps = len(steps)

        def b2e(n):
            b = ecs["n"] % 2
            ecs["n"] += 1
            t0_ = tok0 + n * 128
            k.dma(omt[b], OM[t0_:t0_ + 128, :], [], [("omt", b)])
            hn = [("hbuf", n, h_) for h_ in range(4)]
            for h in range(4):
                k.act(sqj[h], hbuf[:, n, h * 128:(h + 1) * 128], AF.Square, hn, [("sqj", h), ("ssm", b, h)], accum=ssm[b][:, h:h + 1])
            k.ts("dve", msm[b], ssm[b], 1.0 / 128, EPS, ALU.mult, ALU.add, [("ssm", b, h) for h in range(4)], [("msm", b)])
            k.tt("pool", rsm[b], msm[b], nh, ALU.pow, [("msm", b), "nh"], [("rsm", b)])
            k.act(sg[b], omt[b], AF.Sigmoid, [("omt", b)], [("sg", b)])
            k.tt("dve", t1[b].rearrange("p (a b) -> p a b", b=128), hbuf[:, n, :].rearrange("p (a b) -> p a b", b=128),
                 bc(rsm[b].unsqueeze(2), [128, 4, 128]), ALU.mult, hn + [("rsm", b)], [("t1", b)])
            k.tt("pool", t1[b], t1[b], gcm, ALU.mult, [("t1", b), "gcm"], [("t1", b)])
            k.tt("dve", yst[b], t1[b], sg[b], ALU.mult, [("t1", b), ("sg", b)], [("yst", b)])
            k.dma(Y[t0_:t0_ + 128, 512:1024], yst[b], [("yst", b)], [])

        pending = []
        npairs = nsteps // 2
        vegop(0)
        vegop(1)
        def backs(p):
            i0_, i1_ = 2 * p, 2 * p + 1
            for fn in (back_a, back_b, back_c, back_d):
                fn(i0_)
                fn(i1_)
            for ii in (i0_, i1_):
                d_, j_, n_ = steps[ii]
                if visit[(d_, n_)] > visit[(1 - d_, n_)]:
                    pending.append((p + 2, n_))

        for p in range(npairs + 4):
            if p < npairs:
                i0_, i1_ = 2 * p, 2 * p + 1
                j = steps[i0_][1]
                if j + 2 < NT:
                    for d in range(2):
                        slots[(d, j + 2)] = load(d, j + 2)
                if p + 1 < npairs:
                    vegop(i0_ + 2)
                    vegop(i1_ + 2)
                sfront(i0_)
                sfront(i1_)
                pa = half_state(i0_, 0)
                pb = half_state(i1_, 0)
                if p >= 1:
                    backs(p - 1)
                pa()
                pb()
                pa = half_state(i0_, 1)
                pb = half_state(i1_, 1)
                pa()
                pb()
            elif p == npairs:
                backs(p - 1)
            for (tp, n_) in list(pending):
                if tp <= p:
                    b2e(n_)
                    pending.remove((tp, n_))
        assert not pending
        cc += nsteps

        tok0 += T


def load_ffn_weights(k, dram, Wu, Wd):
    usrc = dram["w_up"].rearrange("(kc p) n -> p kc n", p=128)
    for c0 in range(0, 2 * DFF, 512):
        k.dma(Wu[:, :, c0:c0 + 512], usrc[:, :, c0:c0 + 512], [], [], eng="pool")
    dsrc = dram["w_down"].rearrange("(fc p) n -> p fc n", p=128)
    for half in range(2):
        k.dma(Wd[:, :, half * 512:(half + 1) * 512], dsrc[:, :, half * 512:(half + 1) * 512], [], [], eng="pool")


def phase_C(k, dram, seqs):
    NTOK = sum(seqs)
    x, w_out, Y, XMID = dram["x"], dram["w_out"], dram["Y"], dram["XMID"]
    Wu = k.sb([128, 8, 2 * DFF], BF16)
    Wd = k.sb([128, NFC, D], BF16)
    Wo = k.sb([128, 8, D], BF16)
    ident = k.sb([128, 128], BF16)
    yt = [k.sb([128, D], BF16) for _ in range(2)]
    yT = [k.sb([128, 8, 128], BF16) for _ in range(2)]
    xt = [k.sb([128, D], F32) for _ in range(2)]
    xm = [k.sb([128, D], F32) for _ in range(2)]
    pTs = [k.ps(0, BF16).rearrange("p (a b) -> p a b", b=128), k.ps(5, BF16).rearrange("p (a b) -> p a b", b=128)]
    pO = [[k.ps(1 + 2 * i + h, F32) for h in range(2)] for i in range(2)]
    k.dma(ident, dram["c_ident"], [], ["ident"])
    wsrc = w_out.rearrange("(kc p) n -> p kc n", p=128)
    for h in range(2):
        k.dma(Wo[:, :, h * 512:(h + 1) * 512], wsrc[:, :, h * 512:(h + 1) * 512], [], [("Wo", h)], eng="pool")
    load_ffn_weights(k, dram, Wu, Wd)
    NT = NTOK // 128

    def load(t):
        b = t % 2
        k.dma(yt[b], Y[t * 128:(t + 1) * 128, :], [], [("yt", b)])
        k.dma(xt[b], x[t * 128:(t + 1) * 128, :], [], [("xt", b)])

    load(0)
    for t in range(NT):
        b = t % 2
        if t + 1 < NT:
            load(t + 1)
        pT = pTs[b]
        for kc in range(8):
            k.tr(pT[:, kc, :], yt[b][:, kc * 128:(kc + 1) * 128], ident, [("yt", b), "ident"], [("bk", 5 * b)])
        k.copy("act", yT[b], pT, [("bk", 5 * b)], [("yT", b)])
        for h in range(2):
            bi = 1 + 2 * b + h
            for kc in range(8):
                k.mm(pO[b][h], yT[b][:, kc, :], Wo[:, kc, h * 512:(h + 1) * 512], kc == 0, kc == 7, [("yT", b), ("Wo", h)], [("bk", bi)])
            k.tt("dve", xm[b][:, h * 512:(h + 1) * 512], xt[b][:, h * 512:(h + 1) * 512], pO[b][h], ALU.add,
                 [("xt", b), ("bk", bi)], [("xm", b, h)])
        k.dma(XMID[t * 128:(t + 1) * 128, :], xm[b], [("xm", b, 0), ("xm", b, 1)], [])


def phase_D(k, dram, seqs):
    XMID, gvec, y = dram["XMID"], dram["gvec"], dram["y"]
    Wu = k.sb([128, 8, 2 * DFF], BF16)
    Wd = k.sb([128, NFC, D], BF16)
    ident = k.sb([128, 128], BF16)
    nh = k.sb([128, 4], F32)
    gff = k.sb([128, D], F32)
    gfin = k.sb([128, D], F32)
    fcw = k.sb([128, NFC, 3], F32)
    xt = [k.sb([128, D], F32) for _ in range(3)]
    hst = k.sb([2, 4], F32)
    ss = [k.sb([128, 1], F32) for _ in range(2)]
    ms = [k.sb([128, 1], F32) for _ in range(2)]
    rs = [k.sb([128, 1], F32) for _ in range(2)]
    h2 = [k.sb([128, D], BF16) for _ in range(2)]
    h2T = k.sb([128, 8, 512], BF16)
    h2hTs = [k.sb([128, 8, 2], BF16) for _ in range(2)]
    aT = k.sb([128, NFC, 512], BF16)
    gx = [k.sb([128, 514], F32) for _ in range(2)]
    cv = [k.sb([128, 512], F32) for _ in range(2)]
    ge = [k.sb([128, 512], BF16) for _ in range(2)]
    yfs = [k.sb([128, D], F32) for _ in range(2)]
    hxn = yfs[1].bitcast(BF16)[0:2, 0:D]
    hx = yfs[0][0:2, :]
    junk = gx[0].bitcast(BF16)[:, 0:D]
    HX = [("yf", 0, 0), ("yf", 0, 1)]
    JUNK = ("gx", 0)
    pT = k.ps(0, BF16).rearrange("p (a b) -> p a b", b=128)
    pG = [k.ps(1 + i, F32) for i in range(2)]
    pV = [k.ps(3 + i, F32) for i in range(2)]
    pTh = k.ps(5, BF16)[:, 0:16].rearrange("p (a b) -> p a b", b=2)
    pGh = k.ps(5, F32)[:, 64:66]
    pD = [k.ps(6 + i, F32) for i in range(2)]

    def BK(i):
        return ("bk", i)

    k.dma(ident, dram["c_ident"], [], ["ident"])
    k.memset("pool", nh, -0.5, ["nh"])
    load_bcast_row(k, gff, gvec[2:3, :], ["gff"])
    load_bcast_row(k, gfin, gvec[3:4, :], ["gfin"])
    k.dma(fcw, dram["fconv"], [], ["fcw"])

    mts = []
    tok0 = 0
    for T in seqs:
        for M in range(T // 512):
            mts.append((tok0 + M * 512, M == 0, M == T // 512 - 1))
        tok0 += T
    cnt = {"t": 0, "d": 0, "y": 0}
    uses = [mts[0][0] + j * 128 for j in range(4)]
    for mi in range(len(mts)):
        a0_ = mts[mi][0]
        if mi + 1 < len(mts):
            a1_ = mts[mi + 1][0]
            uses += [a1_, a1_ + 128, a0_, a1_ + 256, a0_ + 128, a1_ + 384, a0_ + 256, a0_ + 384]
        else:
            uses += [a0_ + j * 128 for j in range(4)]
    issued = {"n": 0}

    def xt_issue(upto):
        while issued["n"] <= upto and issued["n"] < len(uses):
            u = issued["n"]
            ta_ = uses[u]
            k.dma(xt[u % 3], XMID[ta_:ta_ + 128, :], [], [("xt", u % 3)])
            issued["n"] += 1

    def next_xt(ta):
        u = cnt["t"]
        cnt["t"] += 1
        assert uses[u] == ta, (u, uses[u], ta)
        xt_issue(u + 2)
        return u % 3

    def prep_sub(mi, j):
        a = mts[mi][0]
        ta = a + j * 128
        xb = next_xt(ta)
        b = 0
        hb = j % 2
        k.act(h2[hb], xt[xb], AF.Square, [("xt", xb)], [("h2", hb), ("ss", b)], accum=ss[b])
        k.ts("dve", ms[b], ss[b], 1.0 / D, EPS, ALU.mult, ALU.add, [("ss", b)], [("ms", b)])
        k.tt("pool", rs[b], ms[b], nh[:, 0:1], ALU.pow, [("ms", b), "nh"], [("rs", b)])
        k.stt(h2[hb], xt[xb], rs[b], gff, ALU.mult, ALU.mult, [("xt", xb), ("rs", b), "gff"], [("h2", hb)])

    def prep_T(mi, j):
        hb = j % 2
        for kc in range(8):
            k.tr(pT[:, kc, :], h2[hb][:, kc * 128:(kc + 1) * 128], ident, [("h2", hb), "ident"], [BK(0)])
        k.copy("act", h2T[:, :, j * 128:(j + 1) * 128], pT, [BK(0)], [("h2T", j)])

    def halo(mi):
        a, first, last = mts[mi]
        HXN = ("yf", 1, 0)
        h2hT = h2hTs[mi % 2]
        k.memset("pool", hx, 0.0, HX)
        if not first:
            k.dma(hx[0:1, :], XMID[a - 1:a, :], [], HX)
        if not last:
            k.dma(hx[1:2, :], XMID[a + 512:a + 513, :], [], HX)
        k.act(hxn, hx, AF.Square, HX, [HXN, "hss"], accum=hst[:, 0:1])
        k.ts("dve", hst[:, 1:2], hst[:, 0:1], 1.0 / D, EPS, ALU.mult, ALU.add, ["hss"], ["hms"])
        k.tt("pool", hst[:, 2:3], hst[:, 1:2], nh[0:2, 0:1], ALU.pow, ["hms", "nh"], ["hrs"])
        k.stt(hxn, hx, hst[:, 2:3], gff[0:2, :], ALU.mult, ALU.mult, HX + ["hrs", "gff"], [HXN])
        for kc in range(8):
            k.tr(pTh[:, kc, :], hxn[0:2, kc * 128:(kc + 1) * 128], ident[0:2, 0:2], [HXN, "ident"], [BK(5)])
        k.copy("act", h2hT, pTh, [BK(5)], [("h2hT", mi % 2)])

    h2Tn = [("h2T", j) for j in range(4)]

    def S1(fc, mi):
        b = fc % 2
        h2hT = h2hTs[mi % 2]
        for kc in range(8):
            k.mm(pG[b], Wu[:, kc, fc * 128:(fc + 1) * 128], h2T[:, kc, :], kc == 0, kc == 7, h2Tn, [BK(1 + b)])
        for kc in range(8):
            k.mm(pGh, Wu[:, kc, fc * 128:(fc + 1) * 128], h2hT[:, kc, :], kc == 0, kc == 7, [("h2hT", mi % 2)], [BK(5)])
        for kc in range(8):
            k.mm(pV[b], Wu[:, kc, DFF + fc * 128:DFF + (fc + 1) * 128], h2T[:, kc, :], kc == 0, kc == 7, h2Tn, [BK(3 + b)])

    def S2(fc):
        b = fc % 2
        k.copy("act", gx[b][:, 1:513], pG[b], [BK(1 + b)], [("gx", b)])
        k.copy("act", gx[b][:, 0:1], pGh[:, 0:1], [BK(5)], [("gx", b)])
        k.copy("act", gx[b][:, 513:514], pGh[:, 1:2], [BK(5)], [("gx", b)])
        k.copy("act", aT[:, fc, :], pV[b], [BK(3 + b)], [("aT", fc)])

    def S3(fc):
        b = fc % 2
        k.ts("dve", cv[b], gx[b][:, 0:512], fcw[:, fc, 0:1], None, ALU.mult, None, [("gx", b), "fcw"], [("cv", b)])
        k.stt(cv[b], gx[b][:, 1:513], fcw[:, fc, 1:2], cv[b], ALU.mult, ALU.add, [("gx", b), "fcw", ("cv", b)], [("cv", b)])
        k.stt(cv[b], gx[b][:, 2:514], fcw[:, fc, 2:3], cv[b], ALU.mult, ALU.add, [("gx", b), "fcw", ("cv", b)], [("cv", b)])

    def S4(fc):
        b = fc % 2
        k.act(ge[b], cv[b], AF.Gelu_apprx_tanh, [("cv", b)], [("ge", b)])

    def S5(fc):
        b = fc % 2
        k.tt("dve", aT[:, fc, :], aT[:, fc, :], ge[b], ALU.mult, [("ge", b), ("aT", fc)], [("aT", fc)])

    aTn = [("aT", fc) for fc in range(NFC)]

    def down_sub(mi, j):
        a = mts[mi][0]
        ta = a + j * 128
        xb = next_xt(ta)
        b = 1
        yi = cnt["y"] % 2
        cnt["y"] += 1
        yf = yfs[yi]
        for half in range(2):
            pi = cnt["d"] % 2
            cnt["d"] += 1
            for fc in range(NFC):
                k.mm(pD[pi], aT[:, fc, j * 128:(j + 1) * 128], Wd[:, fc, half * 512:(half + 1) * 512], fc == 0, fc == NFC - 1,
                     [("aT", fc)], [BK(6 + pi)])
            k.tt("dve", yf[:, half * 512:(half + 1) * 512], xt[xb][:, half * 512:(half + 1) * 512], pD[pi], ALU.add,
                 [("xt", xb), BK(6 + pi)], [("yf", yi, half)])
        yn = [("yf", yi, 0), ("yf", yi, 1)]
        k.act(junk, yf, AF.Square, yn, [JUNK, ("ss", b)], accum=ss[b])
        k.ts("dve", ms[b], ss[b], 1.0 / D, EPS, ALU.mult, ALU.add, [("ss", b)], [("ms", b)])
        k.tt("pool", rs[b], ms[b], nh[:, 0:1], ALU.pow, [("ms", b), "nh"], [("rs", b)])
        k.stt(yf, yf, rs[b], gfin, ALU.mult, ALU.mult, yn + [("rs", b), "gfin"], yn)
        k.dma(y[ta:ta + 128, :], yf, yn, [])

    for j in range(4):
        prep_sub(0, j)
        prep_T(0, j)
    halo(0)
    for mi in range(len(mts)):
        nxt = mi + 1 < len(mts)
        for fc in range(NFC + 1):
            if fc < NFC:
                S1(fc, mi)
                S2(fc)
                S3(fc)
            if fc >= 1:
                S4(fc - 1)
                S5(fc - 1)
            if fc == 3 and nxt:
                halo(mi + 1)
        if nxt:
            prep_sub(mi + 1, 0)
            prep_sub(mi + 1, 1)
            down_sub(mi, 0)
            prep_T(mi + 1, 0)
            prep_sub(mi + 1, 2)
            down_sub(mi, 1)
            prep_T(mi + 1, 1)
            prep_sub(mi + 1, 3)
            down_sub(mi, 2)
            prep_T(mi + 1, 2)
            down_sub(mi, 3)
            prep_T(mi + 1, 3)
        else:
            for j in range(4):
                down_sub(mi, j)


def host_consts():
    c = {}
    c["c_ident"] = np.eye(128, dtype=np.float32).astype(ml_dtypes.bfloat16)
    q = np.arange(64)
    kc = np.arange(64)
    cstart = np.clip(q - 8, 0, 48)
    valid = (kc[None, :] >= cstart[:, None]) & (kc[None, :] < cstart[:, None] + 16)
    m = np.where(valid, 0.0, MASKV).astype(np.float32)
    m = np.tile(m[:, None, :], (1, 15, 1)).reshape(64, 960)
    c["c_mask"] = np.concatenate([m, m], axis=0)
    s = np.arange(128)[:, None]
    t = np.arange(128)[None, :]
    same = (s // 64) == (t // 64)
    MFm = (same & (s > t)).astype(np.float32)
    MBm = (same & (s < t)).astype(np.float32)
    S0 = np.broadcast_to((s < 64), (128, 128)).astype(np.float32)
    S1 = np.broadcast_to((s >= 64), (128, 128)).astype(np.float32)
    c["c_mats"] = np.stack([MFm, MBm, S0, S1]).astype(np.float32)
    sl = (np.arange(128) % 64)[:, None]
    tl = np.arange(64)[None, :]
    c["c_tri"] = np.stack([(sl <= tl), (sl >= tl)]).astype(np.float32).astype(ml_dtypes.bfloat16)
    cadd = np.zeros((128, 16), np.float32)
    cadd[:, 0:4] = math.log(128 ** -0.5)
    cadd[:, 8:12] = math.log(128 ** -0.5)
    c["c_cadd"] = cadd
    return c


def pack_params(norm_mix_g, w_in, mlstm_conv_w, gate_b, attn_rpb, attn_norm_g, mlstm_norm_g, w_out,
                norm_ffn_g, w_up, ffn_conv_w, w_down, norm_final_g):
    f = np.float32
    p = {}
    p["w_in"] = np.ascontiguousarray(np.asarray(w_in, f)[0])
    p["w_out"] = np.ascontiguousarray(np.asarray(w_out, f)[0])
    p["w_up"] = np.ascontiguousarray(np.asarray(w_up, f)[0])
    p["w_down"] = np.ascontiguousarray(np.asarray(w_down, f)[0])
    p["gvec"] = np.stack([np.asarray(norm_mix_g, f)[0],
                          np.concatenate([np.asarray(attn_norm_g, f)[0], np.asarray(mlstm_norm_g, f)[0]]),
                          np.asarray(norm_ffn_g, f)[0], np.asarray(norm_final_g, f)])
    mc = np.asarray(mlstm_conv_w, f)[0]
    p["mconv"] = np.ascontiguousarray(mc.reshape(3, 8, 128).transpose(2, 1, 0))
    fc = np.asarray(ffn_conv_w, f)[0]
    p["fconv"] = np.ascontiguousarray(fc.reshape(3, NFC, 128).transpose(2, 1, 0))
    p["gateb"] = np.asarray(gate_b, f).reshape(1, 16)
    rpb = np.asarray(attn_rpb, f)[0]
    q = np.arange(64)[:, None]
    kc = np.arange(64)[None, :]
    dc = np.clip(kc - q + 15, 0, 30)
    g = rpb[:, :, dc]
    g = g.transpose(0, 2, 1, 3).reshape(4, 2, 64, 960)
    p["rpbg"] = np.ascontiguousarray(g.transpose(1, 2, 0, 3).reshape(128, 4, 960))
    return p


_CACHE = {}


def _get_program(seqs=SEQS_FULL, upto="D", debug=False):
    key = (tuple(seqs), upto, debug)
    if key not in _CACHE:
        _CACHE[key] = build_program(seqs, upto, debug)
    return _CACHE[key]


def kernel(x_prompt, x_sample, norm_mix_g, w_in, mlstm_conv_w, gate_b, attn_rpb, attn_norm_g, mlstm_norm_g,
           w_out, norm_ffn_g, w_up, ffn_conv_w, w_down, norm_final_g):
    xp = np.asarray(x_prompt, np.float32)
    xs = np.asarray(x_sample, np.float32)
    common = pack_params(norm_mix_g, w_in, mlstm_conv_w, gate_b, attn_rpb, attn_norm_g, mlstm_norm_g, w_out,
                         norm_ffn_g, w_up, ffn_conv_w, w_down, norm_final_g)
    common.update(host_consts())
    nc, _ = _get_program()
    in_maps = []
    for c in range(NCORES):
        xc = np.concatenate([xp[4 * c:4 * c + 4].reshape(4 * 2048, D), xs[c]], axis=0)
        d = dict(common)
        d["x"] = np.ascontiguousarray(xc)
        in_maps.append(d)
    res = run_bass_kernel_spmd(nc, in_maps, core_ids=list(range(NCORES)))
    yp = np.empty((32, 2048, D), np.float32)
    ys = np.empty((8, 4096, D), np.float32)
    for c in range(NCORES):
        yc = np.asarray(res.results[c]["y"], np.float32)
        yp[4 * c:4 * c + 4] = yc[:8192].reshape(4, 2048, D)
        ys[c] = yc[8192:]
    return yp, ys
```

```python
import math
from contextlib import ExitStack

import numpy as np
import ml_dtypes
import concourse.bass as bass
import concourse.mybir as mybir
from concourse.bass_utils import run_bass_kernel_spmd

F32 = mybir.dt.float32
BF16 = mybir.dt.bfloat16
U8 = mybir.dt.uint8
ALU = mybir.AluOpType
AF = mybir.ActivationFunctionType

NCORES = 8
D = 1024
DIN = 3600
DFF = 2816
NFC = DFF // 128
EPS = 1e-6
SEQS_FULL = (2048, 2048, 2048, 2048, 4096)
NRING = 24
ALL_ENG = ("pe", "act", "dve", "pool", "sp")
MASKV = -30000.0


class Sched:
    def __init__(self, nc):
        self.nc = nc
        self.ops = []
        self.last_w = {}
        self.readers = {}
        self.bar = []
        self.dma_since_bar = {e: [] for e in ALL_ENG}
        self.recent = {e: [] for e in ALL_ENG}

    def add(self, eng, fn, reads=(), writes=(), dma=False, marker=False, extra_deps=()):
        idx = len(self.ops)
        deps = {}
        for r in reads:
            i = self.last_w.get(r)
            if i is not None:
                deps[i] = "raw"
        for w in writes:
            i = self.last_w.get(w)
            if i is not None and i not in deps:
                deps[i] = "waw"
            for i in self.readers.get(w, {}).values():
                if isinstance(i, list):
                    for ii in i:
                        deps.setdefault(ii, "war")
                else:
                    deps.setdefault(i, "war")
        for i in extra_deps:
            deps[i] = "raw"
        for i in self.bar:
            deps.setdefault(i, "raw")
        for w in writes:
            self.last_w[w] = idx
            self.readers[w] = {}
        for r in reads:
            d = self.readers.setdefault(r, {})
            if dma:
                d.setdefault("dma_" + eng, []).append(idx)
            else:
                d[eng] = idx
        keep = []
        for i, typ in deps.items():
            o = self.ops[i]
            if o["eng"] == eng and not o["dma"]:
                if dma:
                    keep.append(i)
                elif o["marker"]:
                    continue
                elif eng == "pe":
                    continue
                else:
                    keep.append(i)
            else:
                keep.append(i)
        self.ops.append(dict(eng=eng, fn=fn, deps=sorted(keep), dma=dma, marker=marker))
        if not dma:
            self.recent[eng] = (self.recent[eng] + [idx])[-2:]
        if dma:
            self.dma_since_bar[eng].append(idx)
        return idx

    def barrier(self):
        marks = []
        for e in ALL_ENG:
            ex = list(self.dma_since_bar[e])
            self.dma_since_bar[e] = []
            marks.append(self.add(e, lambda eng: eng.drain(), marker=True, extra_deps=ex))
        self.bar = marks
        self.last_w = {}
        self.readers = {}

    def emit(self):
        nc = self.nc
        ops = self.ops
        needed = set()
        for o in ops:
            for i in o["deps"]:
                needed.add(i)
        by_eng = {}
        for idx, o in enumerate(ops):
            by_eng.setdefault(o["eng"], []).append(idx)
        sig = {}
        n_sig = {e: 0 for e in by_eng}
        n_dma = {e: 0 for e in by_eng}
        for e, lst in by_eng.items():
            for idx in lst:
                o = ops[idx]
                if o["dma"]:
                    k = n_dma[e]
                    n_dma[e] += 1
                    sig[idx] = ("d", e, k % NRING, 16 * (k // NRING + 1))
                elif idx in needed:
                    n_sig[e] += 1
                    sig[idx] = ("c", e, 0, n_sig[e])
        nwaits = {e: 0 for e in by_eng}
        with ExitStack() as st:
            sems = {}
            for e in by_eng:
                if n_sig[e]:
                    sems[("c", e, 0)] = st.enter_context(nc.semaphore(f"c_{e}"))
                for j in range(min(NRING, n_dma[e])):
                    sems[("d", e, j)] = st.enter_context(nc.semaphore(f"d_{e}_{j}"))
            block = st.enter_context(nc.Block())

            def run(e, eng):
                waited = {}

                def wait(key, val):
                    if waited.get(key, 0) < val:
                        eng.wait_ge(sems[key], val)
                        waited[key] = val
                        nwaits[e] += 1

                for idx in by_eng.get(e, []):
                    o = ops[idx]
                    for i in o["deps"]:
                        s = sig[i]
                        wait((s[0], s[1], s[2]), s[3])
                    s = sig.get(idx)
                    if o["dma"] and s[3] > 16:
                        wait((s[0], s[1], s[2]), s[3] - 16)
                    inst = o["fn"](eng)
                    if s is not None:
                        inst.then_inc(sems[(s[0], s[1], s[2])], 16 if o["dma"] else 1)
                if n_dma.get(e, 0):
                    last = {}
                    for idx in by_eng[e]:
                        s = sig.get(idx)
                        if s is not None and s[0] == "d":
                            last[s[2]] = max(last.get(s[2], 0), s[3])
                    for j, v in last.items():
                        wait(("d", e, j), v)

            if "pe" in by_eng:
                block.tensor(lambda eng: run("pe", eng))
            if "act" in by_eng:
                block.scalar(lambda eng: run("act", eng))
            if "dve" in by_eng:
                block.vector(lambda eng: run("dve", eng))
            if "pool" in by_eng:
                block.gpsimd(lambda eng: run("pool", eng))
            if "sp" in by_eng:
                block.sync(lambda eng: run("sp", eng))
        return {e: (len(l), nwaits[e]) for e, l in by_eng.items()}


_DT_BYTES = {F32: 4, BF16: 2, U8: 1}


class Ctx:
    def __init__(self, nc, S, sbt, banks):
        self.nc, self.S, self.sbt, self.banks = nc, S, sbt, banks
        self.off = 0
        self.cap = sbt.shape[1]

    def reset(self):
        self.off = 0

    def sb(self, shape, dt):
        n = 1
        for s in shape[1:]:
            n *= s
        nb = n * _DT_BYTES[dt]
        nb_al = (nb + 63) // 64 * 64
        assert self.off + nb_al <= self.cap, f"SBUF overflow: {self.off + nb_al} > {self.cap}"
        ap = self.sbt[0:shape[0], self.off:self.off + nb].bitcast(dt)
        self.off += nb_al
        if len(shape) == 3:
            ap = ap.rearrange("p (a b) -> p a b", b=shape[2])
        elif len(shape) == 4:
            ap = ap.rearrange("p (a b c) -> p a b c", b=shape[2], c=shape[3])
        return ap

    def ps(self, bank, dt, nbanks=1):
        t = self.banks[bank]
        assert nbanks == 1
        return t[:, :].bitcast(dt)

    def mm(self, out, lhsT, rhs, start, stop, r, w):
        self.S.add("pe", lambda e: e.matmul(out, lhsT=lhsT, rhs=rhs, start=start, stop=stop), reads=r, writes=w)

    def tr(self, out, in_, ident, r, w):
        self.S.add("pe", lambda e: e.transpose(out=out, in_=in_, identity=ident), reads=r, writes=w)

    def act(self, out, in_, func, r, w, scale=1.0, bias=None, accum=None):
        kw = {}
        if bias is not None:
            kw["bias"] = bias
        if accum is not None:
            kw["accum_out"] = accum
        self.S.add("act", lambda e: e.activation(out=out, in_=in_, func=func, scale=scale, **kw), reads=r, writes=w)

    def copy(self, eng, out, in_, r, w):
        if eng == "act":
            self.S.add("act", lambda e: e.activation(out=out, in_=in_, func=AF.Copy), reads=r, writes=w)
        else:
            self.S.add(eng, lambda e: e.tensor_copy(out=out, in_=in_), reads=r, writes=w)

    def tt(self, eng, out, in0, in1, op, r, w):
        self.S.add(eng, lambda e: e.tensor_tensor(out=out, in0=in0, in1=in1, op=op), reads=r, writes=w)

    def ts(self, eng, out, in0, s1, s2, op0, op1, r, w):
        if s2 is None:
            self.S.add(eng, lambda e: e.tensor_scalar(out=out, in0=in0, scalar1=s1, scalar2=None, op0=op0), reads=r, writes=w)
        else:
            self.S.add(eng, lambda e: e.tensor_scalar(out=out, in0=in0, scalar1=s1, scalar2=s2, op0=op0, op1=op1), reads=r, writes=w)

    def stt(self, out, in0, scalar, in1, op0, op1, r, w):
        self.S.add("dve", lambda e: e.scalar_tensor_tensor(out=out, in0=in0, scalar=scalar, in1=in1, op0=op0, op1=op1), reads=r, writes=w)

    def memset(self, eng, ap, val, w):
        self.S.add(eng, lambda e: e.memset(ap, val), writes=w)

    def recip(self, out, in_, r, w):
        self.S.add("dve", lambda e: e.reciprocal(out=out, in_=in_), reads=r, writes=w)

    def dma(self, out, in_, r, w, eng="sp"):
        self.S.add(eng, lambda e: e.dma_start(out=out, in_=in_), reads=r, writes=w, dma=True)

    def rstd(self, ss, ms, rs, nh, n, r_name, tag):
        self.ts("dve", ms, ss, 1.0 / n, EPS, ALU.mult, ALU.add, [r_name], [("ms", tag)])
        self.tt("pool", rs, ms, nh, ALU.pow, [("ms", tag), "nh"], [("rs", tag)])


def bc(ap, shape):
    return ap.to_broadcast(list(shape))


def build_program(seqs=SEQS_FULL, upto="D", debug=False):
    NTOK = sum(seqs)
    nc = bass.Bass("TRN2", target_bir_lowering=False)
    dram = {}

    def din(name, shape, dt=F32):
        dram[name] = nc.dram_tensor(name, list(shape), dt, kind="ExternalInput").ap()
        return dram[name]

    def dscr(name, shape, dt):
        kind = "ExternalOutput" if debug else "Internal"
        dram[name] = nc.dram_tensor(name, list(shape), dt, kind=kind).ap()
        return dram[name]

    x = din("x", [NTOK, D])
    w_in = din("w_in", [D, DIN])
    w_out = din("w_out", [D, D])
    w_up = din("w_up", [D, 2 * DFF])
    w_down = din("w_down", [DFF, D])
    gvec = din("gvec", [4, D])
    mconv = din("mconv", [128, 8, 3])
    fconv = din("fconv", [128, NFC, 3])
    gateb = din("gateb", [1, 16])
    rpbg = din("rpbg", [128, 4, 960])
    c_ident = din("c_ident", [128, 128], BF16)
    c_mask = din("c_mask", [128, 960])
    c_mats = din("c_mats", [4, 128, 128])
    c_tri = din("c_tri", [2, 128, 64], BF16)
    c_cadd = din("c_cadd", [128, 16])
    y = nc.dram_tensor("y", [NTOK, D], F32, kind="ExternalOutput").ap()
    dram["y"] = y

    QBD = dscr("QBD", [4, 128, NTOK * 2], BF16)
    XT = dscr("XT", [12, 128, NTOK], BF16)
    VA = dscr("VA", [NTOK, 520], BF16)
    VM = dscr("VM", [NTOK, 516], BF16)
    OM = dscr("OM", [NTOK, 512], BF16)
    GT = dscr("GT", [NTOK, 16], F32)
    QMT = dscr("QMT", [8, 128, NTOK], BF16)
    KTOK = dscr("KTOK", [NTOK, 512], BF16)
    Y = dscr("Y", [NTOK, D], BF16)
    XMID = dscr("XMID", [NTOK, D], F32)

    S = Sched(nc)
    with ExitStack() as st:
        sbt = st.enter_context(nc.sbuf_tensor("arena", [128, 206 * 1024], U8))
        banks = [st.enter_context(nc.psum_tensor(f"bank{i}", [128, 2048], U8)) for i in range(8)]
        k = Ctx(nc, S, sbt, banks)
        phases = ["A", "B1", "B2a", "B2", "C", "D"]
        fns = {"A": phase_A, "B1": phase_B1, "B2a": phase_B2a, "B2": phase_B2, "C": phase_C, "D": phase_D}
        for ph in phases:
            k.reset()
            fns[ph](k, dram, seqs)
            S.barrier()
            if ph == upto:
                break
        stats = S.emit()
    return nc, stats


def load_bcast_row(k, dst, src_row, w):
    P = dst.shape[0]
    k.dma(dst, src_row.partition_broadcast(P), [], w)


def phase_A(k, dram, seqs):
    NTOK = sum(seqs)
    x, w_in, gvec, ident_d = dram["x"], dram["w_in"], dram["gvec"], dram["c_ident"]
    QBD, XT, VA, VM, OM, GT = (dram[n] for n in ("QBD", "XT", "VA", "VM", "OM", "GT"))
    W = k.sb([128, 8, DIN], BF16)
    gm = k.sb([128, D], F32)
    ident = k.sb([128, 128], BF16)
    nh = k.sb([128, 4], F32)
    xt = [k.sb([128, D], F32) for _ in range(3)]
    sq = k.sb([128, D], BF16)
    ss = [k.sb([128, 1], F32) for _ in range(2)]
    ms = [k.sb([128, 1], F32) for _ in range(2)]
    rs = [k.sb([128, 1], F32) for _ in range(2)]
    xn = [k.sb([128, D], BF16) for _ in range(2)]
    xnT = [k.sb([128, 8, 512], BF16) for _ in range(2)]
    st_va = [k.sb([128, 8, 65], BF16) for _ in range(2)]
    st_vm = [k.sb([128, 4, 129], BF16) for _ in range(2)]
    st_om = [k.sb([128, 512], BF16) for _ in range(2)]
    st_g = [k.sb([128, 16], F32) for _ in range(2)]
    st_qbd = [k.sb([128, 8, 128], BF16) for _ in range(2)]
    st_fm = [k.sb([128, 512], BF16) for _ in range(3)]
    pT = k.ps(0, BF16).rearrange("p (a b) -> p a b", b=128)
    ptm = [k.ps(1 + i, F32) for i in range(3)]
    pfm = [k.ps(4 + i, F32) for i in range(3)]

    k.dma(ident, ident_d, [], ["ident"])
    load_bcast_row(k, gm, gvec[0:1, :], ["gm"])
    k.memset("pool", nh, -0.5, ["nh"])
    for b in range(2):
        k.memset("pool", st_va[b], 1.0, [("st_va", b)])
        k.memset("pool", st_vm[b], 1.0, [("st_vm", b)])
        k.memset("pool", st_qbd[b], 0.0, [("st_qbd", b)])
    blocks = [(1024, 1536), (2560, 3072), (3072, 3584), (3584, 3600), (0, 512), (512, 1024), (1536, 2048), (2048, 2560)]
    wsrc = w_in.rearrange("(kc p) n -> p kc n", p=128)

    def wname(col):
        return ("W", col // 512)

    for (c0, c1) in blocks:
        k.dma(W[:, :, c0:c1], wsrc[:, :, c0:c1], [], [wname(c0)], eng="pool")

    NT = NTOK // 128
    NM = NTOK // 512
    TM = [("va", 1024, 512), ("vm", 2560, 512), ("om", 3072, 512), ("g", 3584, 16)]
    FM = [("qa", i, 128 * i) for i in range(4)] + [("x", i, 512 + 128 * i) for i in range(4)] + \
         [("x", 4 + i, 1536 + 128 * i) for i in range(8)]
    cnt = {"tm": 0, "fm": 0, "qbd": 0, "stfm": 0}

    def L(t):
        k.dma(xt[t % 3], x[t * 128:(t + 1) * 128, :], [], [("xt", t % 3)])

    def N(t):
        b = t % 2
        k.act(sq, xt[t % 3], AF.Square, [("xt", t % 3)], ["sq", ("ss", b)], accum=ss[b])
        k.ts("dve", ms[b], ss[b], 1.0 / D, EPS, ALU.mult, ALU.add, [("ss", b)], [("ms", b)])
        k.tt("pool", rs[b], ms[b], nh[:, 0:1], ALU.pow, [("ms", b), "nh"], [("rs", b)])
        k.stt(xn[b], xt[t % 3], rs[b], gm, ALU.mult, ALU.mult, [("xt", t % 3), ("rs", b), "gm"], [("xn", b)])

    def T(t):
        b = t % 2
        m, j = divmod(t, 4)
        for kc in range(8):
            k.tr(pT[:, kc, :], xn[b][:, kc * 128:(kc + 1) * 128], ident, [("xn", b), "ident"], ["pT"])
        k.copy("act", xnT[m % 2][:, :, j * 128:(j + 1) * 128], pT, ["pT"], [("xnT", m % 2, j)])

    def Mgrp(t, gi):
        b = t % 2
        m, j = divmod(t, 4)
        name, c0, ncol = TM[gi]
        pi = cnt["tm"] % 3
        cnt["tm"] += 1
        ps = ptm[pi]
        for kc in range(8):
            k.mm(ps[:, 0:ncol], xnT[m % 2][:, kc, j * 128:(j + 1) * 128], W[:, kc, c0:c0 + ncol], kc == 0, kc == 7,
                 [("xnT", m % 2, j), wname(c0)], [("ptm", pi)])
        tok = slice(t * 128, (t + 1) * 128)
        if name == "va":
            k.copy("dve", st_va[b][:, :, 0:64], ps.rearrange("p (a b) -> p a b", b=64), [("ptm", pi)], [("st_va", b)])
            k.dma(VA[tok, :], st_va[b].rearrange("p a b -> p (a b)"), [("st_va", b)], [])
        elif name == "vm":
            k.copy("act", st_vm[b][:, :, 0:128], ps.rearrange("p (a b) -> p a b", b=128), [("ptm", pi)], [("st_vm", b)])
            k.dma(VM[tok, :], st_vm[b].rearrange("p a b -> p (a b)"), [("st_vm", b)], [])
        elif name == "om":
            k.copy("dve", st_om[b], ps, [("ptm", pi)], [("st_om", b)])
            k.dma(OM[tok, :], st_om[b], [("st_om", b)], [])
        else:
            k.copy("act", st_g[b], ps[:, 0:16], [("ptm", pi)], [("st_g", b)])
            k.dma(GT[tok, :], st_g[b], [("st_g", b)], [])

    def Fchunk(m, fi):
        kind, idx, col = FM[fi]
        pi = cnt["fm"] % 3
        cnt["fm"] += 1
        ps = pfm[pi]
        for kc in range(8):
            k.mm(ps, W[:, kc, col:col + 128], xnT[m % 2][:, kc, :], kc == 0, kc == 7,
                 [("xnT", m % 2, j) for j in range(4)] + [wname(col)], [("pfm", pi)])
        if kind == "qa":
            qi = cnt["qbd"] % 2
            cnt["qbd"] += 1
            psv = ps.rearrange("p (r c) -> p r c", c=64)
            k.copy("dve", st_qbd[qi][0:64, :, 0:64], psv[0:64], [("pfm", pi)], [("st_qbd", qi)])
            k.copy("act", st_qbd[qi][64:128, :, 64:128], psv[64:128], [("pfm", pi)], [("st_qbd", qi)])
            k.dma(QBD[idx][:, m * 1024:(m + 1) * 1024], st_qbd[qi].rearrange("p a b -> p (a b)"), [("st_qbd", qi)], [])
        else:
            si = cnt["stfm"] % 3
            cnt["stfm"] += 1
            k.copy("act" if (fi % 2) else "dve", st_fm[si], ps, [("pfm", pi)], [("st_fm", si)])
            k.dma(XT[idx][:, m * 512:(m + 1) * 512], st_fm[si], [("st_fm", si)], [])

    L(0)
    if NT > 1:
        L(1)
    N(0)
    T(0)
    for t in range(NT):
        m, j = divmod(t, 4)
        if t + 2 < NT:
            L(t + 2)
        if t + 1 < NT:
            N(t + 1)
        Mgrp(t, 0)
        Mgrp(t, 1)
        if j < 3 and t + 1 < NT:
            T(t + 1)
        Mgrp(t, 2)
        Mgrp(t, 3)
        if j == 3:
            for fi in range(16):
                Fchunk(m, fi)
                if fi == 7 and t + 1 < NT:
                    T(t + 1)


def phase_B1(k, dram, seqs):
    QBD, XT, VA, Y, gvec = dram["QBD"], dram["XT"], dram["VA"], dram["Y"], dram["gvec"]
    TMAX = max(seqs)
    NTMAX = TMAX // 128
    ident = k.sb([128, 128], BF16)
    nh = k.sb([128, 4], F32)
    gc = k.sb([64, 512], F32)
    bext = k.sb([128, 4, 960], BF16)
    kT = k.sb([128, 4, TMAX], BF16)
    vA = k.sb([128, NTMAX, 520], BF16)
    vB = k.sb([128, NTMAX, 520], BF16)
    qb = [k.sb([128, 4, 1024], BF16) for _ in range(2)]
    pexp = [k.sb([128, 512], BF16) for _ in range(2)]
    PT = [k.sb([128, 4, 128], BF16) for _ in range(2)]
    rinv = [k.sb([64, 8], F32) for _ in range(2)]
    ya = [k.sb([64, 512], F32) for _ in range(2)]
    sqj = k.sb([64, 512], BF16)
    ss = [k.sb([64, 1], F32) for _ in range(2)]
    ms = [k.sb([64, 1], F32) for _ in range(2)]
    rs = [k.sb([64, 1], F32) for _ in range(2)]
    yst = [k.sb([64, 512], BF16) for _ in range(2)]
    rp = k.sb([128, 4, 960], F32)
    mk = k.sb([128, 960], F32)
    pS = [k.ps(i, F32) for i in range(2)]
    pPT = [k.ps(2 + i, BF16)[:, 0:512].rearrange("p (a b) -> p a b", b=128) for i in range(2)]
    pY = [[k.ps(4 + 2 * i + b, F32) for b in range(2)] for i in range(2)]

    k.dma(ident, dram["c_ident"], [], ["ident"])
    k.memset("pool", nh, -0.5, ["nh"])
    load_bcast_row(k, gc, gvec[1:2, 0:512], ["gc"])
    k.dma(rp, dram["rpbg"], [], ["rp"])
    k.dma(mk, dram["c_mask"], [], ["mk"])
    for p in range(4):
        k.stt(bext[:, p, :], rp[:, p, :], 8.0, mk, ALU.mult, ALU.add, ["rp", "mk"], ["bext"])

    seq_tok0 = []
    _t = 0
    for Tq in seqs:
        seq_tok0.append(_t)
        _t += Tq
    issued = set()
    seq_index = {"i": 0}

    def load_piece(si, pc):
        Tq = seqs[si]
        t0p = seq_tok0[si]
        ntq = Tq // 128
        issued.add((si, pc))
        k.dma(kT[:, :, pc * 512:(pc + 1) * 512], XT[0:4, :, t0p + pc * 512:t0p + (pc + 1) * 512].rearrange("c p t -> p c t"),
              [], [("kT", pc)])
        k.dma(vA[:, 4 * pc:4 * pc + 4, :], VA[t0p + pc * 512:t0p + (pc + 1) * 512, :].rearrange("(n p) c -> p n c", p=128),
              [], [("vA", pc)])
        nb = min(4, ntq - 1 - 4 * pc)
        if nb > 0:
            b0 = t0p + 64 + pc * 512
            k.dma(vB[:, 4 * pc:4 * pc + nb, :], VA[b0:b0 + nb * 128, :].rearrange("(n p) c -> p n c", p=128), [], [("vB", pc)])

    tok0 = 0
    gcount = 0
    for T in seqs:
        R = T // 64
        NT = T // 128
        si = seq_index["i"]
        seq_index["i"] += 1
        for pc in range(T // 512):
            if (si, pc) not in issued:
                load_piece(si, pc)
        units = [(r, p) for r in range(R) for p in range(4)]
        U = len(units)

        def loadq(g):
            gi = (gcount + g) % 2
            c0 = (tok0 // 64 + 8 * g) * 128
            k.dma(qb[gi], QBD[:, :, c0:c0 + 1024].rearrange("c p t -> p c t"), [], [("qb", gi)])

        def Sst(u):
            r, p = units[u]
            g = r // 8
            gi = (gcount + g) % 2
            if r % 8 == 0 and p == 0 and g + 1 < R // 8:
                loadq(g + 1)
            rs_ = min(max(r - 4, 0), R - 8)
            off = r - rs_
            bcol = (7 - off) * 64
            ps = pS[u % 2]
            k.mm(ps, ident, bext[:, p, bcol:bcol + 512], True, False, ["ident", "bext"], [("pS", u % 2)])
            k.mm(ps, qb[gi][:, p, (r % 8) * 128:(r % 8) * 128 + 128], kT[:, p, rs_ * 64:rs_ * 64 + 512], False, True,
                 [("qb", gi)] + [("kT", pc_) for pc_ in range(rs_ * 64 // 512, (rs_ * 64 + 511) // 512 + 1)], [("pS", u % 2)])
            if p == 0 and r >= 14 and (r - 14) % 8 == 0 and si + 1 < len(seqs):
                pcn = (r - 14) // 8
                if pcn < seqs[si + 1] // 512 and (si + 1, pcn) not in issued:
                    load_piece(si + 1, pcn)
            k.act(pexp[u % 2], ps, AF.Exp, [("pS", u % 2)], [("pexp", u % 2)], scale=0.125)

        def Tst(u):
            for kc in range(4):
                k.tr(pPT[u % 2][:, kc, :], pexp[u % 2][:, kc * 128:(kc + 1) * 128], ident, [("pexp", u % 2), "ident"],
                     [("pPT", u % 2)])
            k.copy("dve", PT[u % 2], pPT[u % 2], [("pPT", u % 2)], [("PT", u % 2)])

        def PVst(u):
            r, p = units[u]
            rs_ = min(max(r - 4, 0), R - 8)
            ry = r % 2
            for hl in range(2):
                h = 2 * p + hl
                bank = h // 4
                col = (h % 4) * 65
                for kc in range(4):
                    if rs_ % 2 == 0:
                        ti = rs_ // 2 + kc
                        vt, vn = vA, ("vA", ti // 4)
                    else:
                        ti = (rs_ - 1) // 2 + kc
                        vt, vn = vB, ("vB", ti // 4)
                    k.mm(pY[ry][bank][0:64, col:col + 65], PT[u % 2][:, kc, hl * 64:(hl + 1) * 64],
                         vt[:, ti, h * 65:(h + 1) * 65], kc == 0, kc == 3, [("PT", u % 2), vn], [("pY", ry, bank)])
            if p == 3:
                pyv = [pY[ry][b_][0:64, 0:260].rearrange("p (h e) -> p h e", e=65) for b_ in range(2)]
                ytok = tok0 + r * 64

                def e1(ry=ry, pyv=pyv):
                    for b_ in range(2):
                        k.recip(rinv[ry][:, 4 * b_:4 * b_ + 4], pyv[b_][:, :, 64], [("pY", ry, b_)], [("rinv", ry)])
                    for b_ in range(2):
                        k.tt("dve", ya[ry][:, 256 * b_:256 * b_ + 256].rearrange("p (h e) -> p h e", e=64),
                             pyv[b_][:, :, 0:64], bc(rinv[ry][:, 4 * b_:4 * b_ + 4].unsqueeze(2), [64, 4, 64]), ALU.mult,
                             [("pY", ry, b_), ("rinv", ry)], [("ya", ry)])

                def e2(ry=ry):
                    k.act(sqj, ya[ry], AF.Square, [("ya", ry)], ["sqj", ("ss", ry)], accum=ss[ry])

                def e3(ry=ry):
                    k.ts("dve", ms[ry], ss[ry], 1.0 / 512, EPS, ALU.mult, ALU.add, [("ss", ry)], [("ms", ry)])
                    k.tt("pool", rs[ry], ms[ry], nh[0:64, 0:1], ALU.pow, [("ms", ry), "nh"], [("rs", ry)])

                def e4(ry=ry, ytok=ytok):
                    k.stt(yst[ry], ya[ry], rs[ry], gc, ALU.mult, ALU.mult, [("ya", ry), ("rs", ry), "gc"], [("yst", ry)])
                    k.dma(Y[ytok:ytok + 64, 0:512], yst[ry], [("yst", ry)], [])

                deferred.append((u + 1, e1))
                deferred.append((u + 2, e2))
                deferred.append((u + 3, e3))
                deferred.append((u + 4, e4))

        loadq(0)
        deferred = []
        for u in range(-2, U + 5):
            if 0 <= u + 2 < U:
                Sst(u + 2)
            if 0 <= u + 1 < U:
                Tst(u + 1)
            if 0 <= u < U:
                PVst(u)
            for (du, fn) in list(deferred):
                if du <= u:
                    fn()
                    deferred.remove((du, fn))
        assert not deferred
        gcount += R // 8
        tok0 += T


def phase_B2a(k, dram, seqs):
    XT, QMT, KTOK = dram["XT"], dram["QMT"], dram["KTOK"]
    ident = k.sb([128, 128], BF16)
    cw = k.sb([128, 8, 3], F32)
    raw = [k.sb([128, 8, 514], BF16) for _ in range(2)]
    tcv = [k.sb([128, 512], F32) for _ in range(2)]
    qk = [k.sb([128, 8, 512], BF16) for _ in range(2)]
    kst = [k.sb([128, 4, 512], BF16) for _ in range(2)]
    pK = [k.ps(i, BF16)[:, 0:512].rearrange("p (a b) -> p a b", b=128) for i in range(2)]
    k.dma(ident, dram["c_ident"], [], ["ident"])
    k.dma(cw, dram["mconv"], [], ["cw"])
    mts = []
    tok0 = 0
    for T in seqs:
        for m in range(T // 512):
            mts.append((tok0 + m * 512, m == 0, m == T // 512 - 1))
        tok0 += T

    def loadraw(mi):
        a, first, last = mts[mi]
        b = mi % 2
        lo = 1 if first else 0
        hi = 513 if last else 514
        if first:
            k.memset("pool", raw[b][:, :, 0:1], 0.0, [("raw", b)])
        if last:
            k.memset("pool", raw[b][:, :, 513:514], 0.0, [("raw", b)])
        k.dma(raw[b][:, :, lo:hi], XT[4:12, :, a - 1 + lo:a - 1 + hi].rearrange("c p t -> p c t"), [], [("raw", b)])

    nt = 0
    loadraw(0)
    for mi in range(len(mts)):
        a, first, last = mts[mi]
        b = mi % 2
        if mi + 1 < len(mts):
            loadraw(mi + 1)
        for c in range(8):
            tb = (mi * 8 + c) % 2
            k.ts("dve", tcv[tb], raw[b][:, c, 1:513], cw[:, c, 1:2], None, ALU.mult, None, [("raw", b), "cw"], [("tcv", tb)])
            k.stt(tcv[tb], raw[b][:, c, 0:512], cw[:, c, 0:1], tcv[tb], ALU.mult, ALU.add, [("raw", b), "cw", ("tcv", tb)], [("tcv", tb)])
            k.stt(tcv[tb], raw[b][:, c, 2:514], cw[:, c, 2:3], tcv[tb], ALU.mult, ALU.add, [("raw", b), "cw", ("tcv", tb)], [("tcv", tb)])
            k.act(qk[b][:, c, :], tcv[tb], AF.Silu, [("tcv", tb)], [("qk", b, c)])
        k.dma(QMT[:, :, a:a + 512].rearrange("c p t -> p c t"), qk[b], [("qk", b, c) for c in range(8)], [])
        for j in range(4):
            pb = nt % 2
            nt += 1
            for h in range(4):
                k.tr(pK[pb][:, h, :], qk[b][:, 4 + h, j * 128:(j + 1) * 128], ident, [("qk", b, 4 + h), "ident"], [("pK", pb)])
            k.copy("act", kst[b][:, j, :], pK[pb].rearrange("p a b -> p (a b)"), [("pK", pb)], [("kst", b)])
        k.dma(KTOK[a:a + 512, :].rearrange("(j p) c -> p j c", p=128), kst[b], [("kst", b)], [])


def phase_B2(k, dram, seqs):
    QMT, KTOK, VM, OM, GT, Y, gvec = (dram[n] for n in ("QMT", "KTOK", "VM", "OM", "GT", "Y", "gvec"))
    TMAX = max(seqs)
    NTm = TMAX // 128
    mats = k.sb([128, 4, 128], F32)
    tri = k.sb([128, 2, 64], BF16)
    gbb = k.sb([128, 16], F32)
    cadd = k.sb([128, 16], F32)
    nh = k.sb([128, 4], F32)
    gcm = k.sb([128, 512], F32)
    G = k.sb([128, NTm, 16], F32)
    Z = k.sb([128, NTm, 16], F32)
    Lf = k.sb([128, NTm * 8], F32)
    REM = k.sb([128, NTm * 8], F32)
    FL = k.sb([128, NTm * 8], F32)
    WVz = k.sb([128, 2, NTm * 8], F32)
    tmpi = k.sb([128, NTm * 8], F32)
    GD = k.sb([128, 2, NTm * 8], F32)
    hbuf = k.sb([128, NTm, 512], F32)
    C = [[k.sb([128, 4, 129], F32) for _ in range(2)] for _ in range(2)]
    cver = [0, 0]
    Cg = [k.sb([128, 4, 129], F32) for _ in range(2)]
    Cs = [[k.sb([128, 4, 129], BF16) for _ in range(2)] for _ in range(2)]
    qkT = [[k.sb([128, 8, 128], BF16) for _ in range(3)] for _ in range(2)]
    ktk = [[k.sb([128, 512], BF16) for _ in range(3)] for _ in range(2)]
    vau = [[k.sb([128, 4, 129], BF16) for _ in range(3)] for _ in range(2)]
    veg = [[k.sb([128, 2, 4, 129], BF16) for _ in range(2)] for _ in range(2)]
    a0T = [k.sb([128, 4, 64], BF16) for _ in range(2)]
    den = [k.sb([128, 4], F32) for _ in range(2)]
    rden = [k.sb([128, 4], F32) for _ in range(2)]
    htmp = [k.sb([128, 512], F32) for _ in range(2)]
    omt = [k.sb([128, 512], BF16) for _ in range(2)]
    sqj = [k.sb([128, 128], BF16) for _ in range(4)]
    ssm = [k.sb([128, 4], F32) for _ in range(2)]
    msm = [k.sb([128, 4], F32) for _ in range(2)]
    rsm = [k.sb([128, 4], F32) for _ in range(2)]
    sg = [k.sb([128, 512], F32) for _ in range(2)]
    t1 = [k.sb([128, 512], F32) for _ in range(2)]
    yst = [k.sb([128, 512], BF16) for _ in range(2)]
    bk = [k.ps(i, F32) for i in range(8)]

    def BK(i):
        return ("bk", i)

    k.dma(mats, dram["c_mats"].rearrange("m p t -> p m t"), [], ["mats"])
    k.dma(tri, dram["c_tri"].rearrange("m p t -> p m t"), [], ["tri"])
    load_bcast_row(k, gbb, dram["gateb"], ["gbb"])
    k.dma(cadd, dram["c_cadd"], [], ["cadd"])
    load_bcast_row(k, gcm, gvec[1:2, 512:1024], ["gcm"])
    k.memset("pool", nh, -0.5, ["nh"])
    k.tt("dve", gbb, gbb, cadd, ALU.add, ["gbb", "cadd"], ["gbb"])

    tok0 = 0
    cc = 0
    lc = [0, 0]
    ecs = {"n": 0}
    for T in seqs:
        NT = T // 128
        N8 = NT * 8
        k.dma(G[:, 0:NT, :], GT[tok0:tok0 + T, :].rearrange("(n p) g -> p n g", p=128), [], ["G"])
        k.tt("dve", Z[:, 0:NT, :], G[:, 0:NT, :], bc(gbb.unsqueeze(1), [128, NT, 16]), ALU.add, ["G", "gbb"], ["Z"])
        Zv = Z[:, 0:NT, :].rearrange("p n (d e) -> p n d e", e=8)

        def v4(t):
            return t[:, 0:N8].rearrange("p (n d e) -> p n d e", d=2, e=4)

        k.act(v4(Lf), Zv[:, :, :, 4:8], AF.Exp, ["Z"], ["Lf"], scale=-1.0)
        k.act(Lf[:, 0:N8], Lf[:, 0:N8], AF.Ln, ["Lf"], ["Lf"], bias=1.0)
        for i in range(4):
            k.mm(bk[i][:, 0:N8], mats[:, i, :], Lf[:, 0:N8], True, True, ["mats", "Lf"], [BK(i)])
        k.copy("act", v4(REM)[:, :, 0, :], v4(bk[0])[:, :, 0, :], [BK(0)], ["REM"])
        k.copy("dve", v4(REM)[:, :, 1, :], v4(bk[1])[:, :, 1, :], [BK(1)], ["REM"])
        k.act(FL[:, 0:N8], REM[:, 0:N8], AF.Exp, ["REM"], ["FL"], scale=-1.0)
        k.tt("dve", v4(tmpi), Zv[:, :, :, 0:4], v4(REM), ALU.subtract, ["Z", "REM"], ["tmpi"])
        k.memset("pool", WVz[:, :, 0:N8], 0.0, ["WV"])
        k.act(WVz[0:64, 0, 0:N8], tmpi[0:64, 0:N8], AF.Exp, ["tmpi"], ["WV"])
        k.act(WVz[64:128, 1, 0:N8], tmpi[64:128, 0:N8], AF.Exp, ["tmpi"], ["WV"])
        k.act(GD[:, 0, 0:N8], bk[2][:, 0:N8], AF.Exp, [BK(2)], ["GD"], scale=-1.0)
        k.act(GD[:, 1, 0:N8], bk[3][:, 0:N8], AF.Exp, [BK(3)], ["GD"], scale=-1.0)
        for d in range(2):
            k.memset("pool", C[d][cver[d]], 0.0, [("C", d, cver[d])])

        tiles = [list(range(NT)), list(range(NT - 1, -1, -1))]
        visit = {}
        for j in range(NT):
            visit[(0, tiles[0][j])] = 2 * j
            visit[(1, tiles[1][j])] = 2 * j + 1

        def load(d, j):
            n = tiles[d][j]
            s = lc[d] % 3
            lc[d] += 1
            t0_ = tok0 + n * 128
            k.dma(qkT[d][s], QMT[:, :, t0_:t0_ + 128].rearrange("c p t -> p c t"), [], [("qkT", d, s)])
            k.dma(ktk[d][s], KTOK[t0_:t0_ + 128, :], [], [("ktk", d, s)])
            k.dma(vau[d][s].rearrange("p a b -> p (a b)"), VM[t0_:t0_ + 128, :], [], [("vau", d, s)])
            return s

        slots = {}
        steps = []
        for j in range(NT):
            for d in range(2):
                steps.append((d, j, tiles[d][j]))

        def sfront(i):
            d, j, n = steps[i]
            rb = d
            s = slots[(d, j)]
            qn = ("qkT", d, s)
            for hf in range(2):
                pr = slice(hf * 64, hf * 64 + 64)
                for h in range(4):
                    k.mm(bk[rb][pr, h * 64:(h + 1) * 64], qkT[d][s][:, 4 + h, pr], qkT[d][s][:, h, pr], True, True, [qn], [BK(rb)])
            k.tt("dve", a0T[rb], bk[rb][:, 0:256].rearrange("p (a b) -> p a b", b=64),
                 bc(tri[:, d, :].unsqueeze(1), [128, 4, 64]), ALU.mult, [BK(rb), "tri"], [("a0T", rb)])

        def half_state(i, hx):
            d, j, n = steps[i]
            rb = d
            vb = (i // 2) % 2
            s = slots[(d, j)]
            gi = n * 8 + d * 4
            qn, kn = ("qkT", d, s), ("ktk", d, s)
            hf = ((0, 1) if d == 0 else (1, 0))[hx]
            pr = slice(hf * 64, hf * 64 + 64)
            v = cver[d]
            Cc, Cn = C[d][v], C[d][1 - v]
            cver[d] = 1 - v
            vg = veg[rb][vb][:, hf]
            vgn = ("veg", rb, vb)
            k.tt("dve", Cg[d], Cc, bc(GD[:, hf, gi:gi + 4].unsqueeze(2), [128, 4, 129]), ALU.mult, [("C", d, v), "GD"], [("Cg", d)])
            k.copy("act", Cs[d][v], Cg[d], [("Cg", d)], [("Cs", d, v)])
            ub = 6 + d
            for h in range(4):
                k.mm(bk[ub][:, h * 128:(h + 1) * 128], ktk[d][s][:, h * 128:(h + 1) * 128], vg[:, h, 0:128], True, True,
                     [kn, vgn], [BK(ub)])
            for h in range(4):
                c0 = 272 + 2 * h
                k.mm(bk[rb][:, c0:c0 + 2], ktk[d][s][:, h * 128:(h + 1) * 128], vg[:, h, 127:129], True, True,
                     [kn, vgn], [BK(rb)])
            k.tt("dve", Cn[:, :, 0:128], Cg[d][:, :, 0:128], bk[ub].rearrange("p (a b) -> p a b", b=128), ALU.add,
                 [("Cg", d), BK(ub)], [("C", d, 1 - v)])
            nv = bk[rb][:, 272:280].rearrange("p (h t) -> p h t", t=2)
            k.tt("dve", Cn[:, :, 128], Cg[d][:, :, 128], nv[:, :, 1], ALU.add, [("Cg", d), BK(rb)], [("C", d, 1 - v)])

            def pmm():
                pb_ = 2 + 2 * rb + vb
                for h in range(4):
                    o = bk[pb_][pr, h * 128:(h + 1) * 128]
                    k.mm(o, qkT[d][s][:, h, pr], Cs[d][v][:, h, 0:128], True, False, [qn, ("Cs", d, v)], [BK(pb_)])
                    k.mm(o, a0T[rb][:, h, :], vg[:, h, 0:128], False, True, [("a0T", rb), vgn], [BK(pb_)])
                for h in range(4):
                    c0 = 256 + 8 * vb + 2 * h
                    o = bk[rb][pr, c0:c0 + 2]
                    k.mm(o, qkT[d][s][:, h, pr], Cs[d][v][:, h, 127:129], True, False, [qn, ("Cs", d, v)], [BK(rb)])
                    k.mm(o, a0T[rb][:, h, :], vg[:, h, 127:129], False, True, [("a0T", rb), vgn], [BK(rb)])
            return pmm

        def vegop(i):
            d, j, n = steps[i]
            rb = d
            vb = (i // 2) % 2
            s = slots[(d, j)]
            gi = n * 8 + d * 4
            for hf in range(2):
                k.tt("pool", veg[rb][vb][:, hf], vau[d][s], bc(WVz[:, hf, gi:gi + 4].unsqueeze(2), [128, 4, 129]), ALU.mult,
                     [("vau", d, s), "WV"], [("veg", rb, vb)])

        def back_a(i):
            d, j, n = steps[i]
            rb = d
            vb = (i // 2) % 2
            dv = bk[rb][:, 256 + 8 * vb:256 + 8 * vb + 8].rearrange("p (h t) -> p h t", t=2)
            k.act(den[rb], dv[:, :, 1], AF.Abs, [BK(rb)], [("den", rb)])

        def back_b(i):
            d, j, n = steps[i]
            rb = d
            gi = n * 8 + d * 4
            k.tt("dve", den[rb], den[rb], FL[:, gi:gi + 4], ALU.max, [("den", rb), "FL"], [("den", rb)])
            k.recip(rden[rb], den[rb], [("den", rb)], [("rden", rb)])

        def back_c(i):
            d, j, n = steps[i]
            rb = d
            first = visit[(d, n)] < visit[(1 - d, n)]
            vb = (i // 2) % 2
            pb_ = 2 + 2 * rb + vb
            if d == 1:
                dst = hbuf[:, n, :] if first else htmp[rb]
                wn = [("hbuf", n, h) for h in range(4)] if first else [("htmp", rb, h) for h in range(4)]
                k.tt("dve", dst.rearrange("p (a b) -> p a b", b=128), bk[pb_].rearrange("p (a b) -> p a b", b=128),
                     bc(rden[rb].unsqueeze(2), [128, 4, 128]), ALU.mult, [BK(pb_), ("rden", rb)], wn)
                return
            for h in range(4):
                dst = hbuf[:, n, h * 128:(h + 1) * 128] if first else htmp[rb][:, h * 128:(h + 1) * 128]
                k.act(dst, bk[pb_][:, h * 128:(h + 1) * 128], AF.Copy, [BK(pb_), ("rden", rb)],
                      [("hbuf", n, h) if first else ("htmp", rb, h)], scale=rden[rb][:, h:h + 1])

        def back_d(i):
            d, j, n = steps[i]
            rb = d
            first = visit[(d, n)] < visit[(1 - d, n)]
            if not first:
                k.tt("dve", hbuf[:, n, :], hbuf[:, n, :], htmp[rb], ALU.add,
                     [("hbuf", n, h) for h in range(4)] + [("htmp", rb, h) for h in range(4)], [("hbuf", n, h) for h in range(4)])

        for j in range(min(2, NT)):
            for d in range(2):
                slots[(d, j)] = load(d, j)
        nsteps = len(steps)

        def b2e(n):
            b = ecs["n"] % 2
            ecs["n"] += 1
            t0_ = tok0 + n * 128
            k.dma(omt[b], OM[t0_:t0_ + 128, :], [], [("omt", b)])
            hn = [("hbuf", n, h_) for h_ in range(4)]
            for h in range(4):
                k.act(sqj[h], hbuf[:, n, h * 128:(h + 1) * 128], AF.Square, hn, [("sqj", h), ("ssm", b, h)], accum=ssm[b][:, h:h + 1])
            k.ts("dve", msm[b], ssm[b], 1.0 / 128, EPS, ALU.mult, ALU.add, [("ssm", b, h) for h in range(4)], [("msm", b)])
            k.tt("pool", rsm[b], msm[b], nh, ALU.pow, [("msm", b), "nh"], [("rsm", b)])
            k.act(sg[b], omt[b], AF.Sigmoid, [("omt", b)], [("sg", b)])
            k.tt("dve", t1[b].rearrange("p (a b) -> p a b", b=128), hbuf[:, n, :].rearrange("p (a b) -> p a b", b=128),
                 bc(rsm[b].unsqueeze(2), [128, 4, 128]), ALU.mult, hn + [("rsm", b)], [("t1", b)])
            k.tt("pool", t1[b], t1[b], gcm, ALU.mult, [("t1", b), "gcm"], [("t1", b)])
            k.tt("dve", yst[b], t1[b], sg[b], ALU.mult, [("t1", b), ("sg", b)], [("yst", b)])
            k.dma(Y[t0_:t0_ + 128, 512:1024], yst[b], [("yst", b)], [])

        pending = []
        npairs = nsteps // 2
        vegop(0)
        vegop(1)
        def backs(p):
            i0_, i1_ = 2 * p, 2 * p + 1
            for fn in (back_a, back_b, back_c, back_d):
                fn(i0_)
                fn(i1_)
            for ii in (i0_, i1_):
                d_, j_, n_ = steps[ii]
                if visit[(d_, n_)] > visit[(1 - d_, n_)]:
                    pending.append((p + 2, n_))

        for p in range(npairs + 4):
            if p < npairs:
                i0_, i1_ = 2 * p, 2 * p + 1
                j = steps[i0_][1]
                if j + 2 < NT:
                    for d in range(2):
                        slots[(d, j + 2)] = load(d, j + 2)
                if p + 1 < npairs:
                    vegop(i0_ + 2)
                    vegop(i1_ + 2)
                sfront(i0_)
                sfront(i1_)
                pa = half_state(i0_, 0)
                pb = half_state(i1_, 0)
                if p >= 1:
                    backs(p - 1)
                pa()
                pb()
                pa = half_state(i0_, 1)
                pb = half_state(i1_, 1)
                pa()
                pb()
            elif p == npairs:
                backs(p - 1)
            for (tp, n_) in list(pending):
                if tp <= p:
                    b2e(n_)
                    pending.remove((tp, n_))
        assert not pending
        cc += nsteps

        tok0 += T


def load_ffn_weights(k, dram, Wu, Wd):
    usrc = dram["w_up"].rearrange("(kc p) n -> p kc n", p=128)
    for c0 in range(0, 2 * DFF, 512):
        k.dma(Wu[:, :, c0:c0 + 512], usrc[:, :, c0:c0 + 512], [], [], eng="pool")
    dsrc = dram["w_down"].rearrange("(fc p) n -> p fc n", p=128)
    for half in range(2):
        k.dma(Wd[:, :, half * 512:(half + 1) * 512], dsrc[:, :, half * 512:(half + 1) * 512], [], [], eng="pool")


def phase_C(k, dram, seqs):
    NTOK = sum(seqs)
    x, w_out, Y, XMID = dram["x"], dram["w_out"], dram["Y"], dram["XMID"]
    Wu = k.sb([128, 8, 2 * DFF], BF16)
    Wd = k.sb([128, NFC, D], BF16)
    Wo = k.sb([128, 8, D], BF16)
    ident = k.sb([128, 128], BF16)
    yt = [k.sb([128, D], BF16) for _ in range(2)]
    yT = [k.sb([128, 8, 128], BF16) for _ in range(2)]
    xt = [k.sb([128, D], F32) for _ in range(2)]
    xm = [k.sb([128, D], F32) for _ in range(2)]
    pTs = [k.ps(0, BF16).rearrange("p (a b) -> p a b", b=128), k.ps(5, BF16).rearrange("p (a b) -> p a b", b=128)]
    pO = [[k.ps(1 + 2 * i + h, F32) for h in range(2)] for i in range(2)]
    k.dma(ident, dram["c_ident"], [], ["ident"])
    wsrc = w_out.rearrange("(kc p) n -> p kc n", p=128)
    for h in range(2):
        k.dma(Wo[:, :, h * 512:(h + 1) * 512], wsrc[:, :, h * 512:(h + 1) * 512], [], [("Wo", h)], eng="pool")
    load_ffn_weights(k, dram, Wu, Wd)
    NT = NTOK // 128

    def load(t):
        b = t % 2
        k.dma(yt[b], Y[t * 128:(t + 1) * 128, :], [], [("yt", b)])
        k.dma(xt[b], x[t * 128:(t + 1) * 128, :], [], [("xt", b)])

    load(0)
    for t in range(NT):
        b = t % 2
        if t + 1 < NT:
            load(t + 1)
        pT = pTs[b]
        for kc in range(8):
            k.tr(pT[:, kc, :], yt[b][:, kc * 128:(kc + 1) * 128], ident, [("yt", b), "ident"], [("bk", 5 * b)])
        k.copy("act", yT[b], pT, [("bk", 5 * b)], [("yT", b)])
        for h in range(2):
            bi = 1 + 2 * b + h
            for kc in range(8):
                k.mm(pO[b][h], yT[b][:, kc, :], Wo[:, kc, h * 512:(h + 1) * 512], kc == 0, kc == 7, [("yT", b), ("Wo", h)], [("bk", bi)])
            k.tt("dve", xm[b][:, h * 512:(h + 1) * 512], xt[b][:, h * 512:(h + 1) * 512], pO[b][h], ALU.add,
                 [("xt", b), ("bk", bi)], [("xm", b, h)])
        k.dma(XMID[t * 128:(t + 1) * 128, :], xm[b], [("xm", b, 0), ("xm", b, 1)], [])


def phase_D(k, dram, seqs):
    XMID, gvec, y = dram["XMID"], dram["gvec"], dram["y"]
    Wu = k.sb([128, 8, 2 * DFF], BF16)
    Wd = k.sb([128, NFC, D], BF16)
    ident = k.sb([128, 128], BF16)
    nh = k.sb([128, 4], F32)
    gff = k.sb([128, D], F32)
    gfin = k.sb([128, D], F32)
    fcw = k.sb([128, NFC, 3], F32)
    xt = [k.sb([128, D], F32) for _ in range(3)]
    hst = k.sb([2, 4], F32)
    ss = [k.sb([128, 1], F32) for _ in range(2)]
    ms = [k.sb([128, 1], F32) for _ in range(2)]
    rs = [k.sb([128, 1], F32) for _ in range(2)]
    h2 = [k.sb([128, D], BF16) for _ in range(2)]
    h2T = k.sb([128, 8, 512], BF16)
    h2hTs = [k.sb([128, 8, 2], BF16) for _ in range(2)]
    aT = k.sb([128, NFC, 512], BF16)
    gx = [k.sb([128, 514], F32) for _ in range(2)]
    cv = [k.sb([128, 512], F32) for _ in range(2)]
    ge = [k.sb([128, 512], BF16) for _ in range(2)]
    yfs = [k.sb([128, D], F32) for _ in range(2)]
    hxn = yfs[1].bitcast(BF16)[0:2, 0:D]
    hx = yfs[0][0:2, :]
    junk = gx[0].bitcast(BF16)[:, 0:D]
    HX = [("yf", 0, 0), ("yf", 0, 1)]
    JUNK = ("gx", 0)
    pT = k.ps(0, BF16).rearrange("p (a b) -> p a b", b=128)
    pG = [k.ps(1 + i, F32) for i in range(2)]
    pV = [k.ps(3 + i, F32) for i in range(2)]
    pTh = k.ps(5, BF16)[:, 0:16].rearrange("p (a b) -> p a b", b=2)
    pGh = k.ps(5, F32)[:, 64:66]
    pD = [k.ps(6 + i, F32) for i in range(2)]

    def BK(i):
        return ("bk", i)

    k.dma(ident, dram["c_ident"], [], ["ident"])
    k.memset("pool", nh, -0.5, ["nh"])
    load_bcast_row(k, gff, gvec[2:3, :], ["gff"])
    load_bcast_row(k, gfin, gvec[3:4, :], ["gfin"])
    k.dma(fcw, dram["fconv"], [], ["fcw"])

    mts = []
    tok0 = 0
    for T in seqs:
        for M in range(T // 512):
            mts.append((tok0 + M * 512, M == 0, M == T // 512 - 1))
        tok0 += T
    cnt = {"t": 0, "d": 0, "y": 0}
    uses = [mts[0][0] + j * 128 for j in range(4)]
    for mi in range(len(mts)):
        a0_ = mts[mi][0]
        if mi + 1 < len(mts):
            a1_ = mts[mi + 1][0]
            uses += [a1_, a1_ + 128, a0_, a1_ + 256, a0_ + 128, a1_ + 384, a0_ + 256, a0_ + 384]
        else:
            uses += [a0_ + j * 128 for j in range(4)]
    issued = {"n": 0}

    def xt_issue(upto):
        while issued["n"] <= upto and issued["n"] < len(uses):
            u = issued["n"]
            ta_ = uses[u]
            k.dma(xt[u % 3], XMID[ta_:ta_ + 128, :], [], [("xt", u % 3)])
            issued["n"] += 1

    def next_xt(ta):
        u = cnt["t"]
        cnt["t"] += 1
        assert uses[u] == ta, (u, uses[u], ta)
        xt_issue(u + 2)
        return u % 3

    def prep_sub(mi, j):
        a = mts[mi][0]
        ta = a + j * 128
        xb = next_xt(ta)
        b = 0
        hb = j % 2
        k.act(h2[hb], xt[xb], AF.Square, [("xt", xb)], [("h2", hb), ("ss", b)], accum=ss[b])
        k.ts("dve", ms[b], ss[b], 1.0 / D, EPS, ALU.mult, ALU.add, [("ss", b)], [("ms", b)])
        k.tt("pool", rs[b], ms[b], nh[:, 0:1], ALU.pow, [("ms", b), "nh"], [("rs", b)])
        k.stt(h2[hb], xt[xb], rs[b], gff, ALU.mult, ALU.mult, [("xt", xb), ("rs", b), "gff"], [("h2", hb)])

    def prep_T(mi, j):
        hb = j % 2
        for kc in range(8):
            k.tr(pT[:, kc, :], h2[hb][:, kc * 128:(kc + 1) * 128], ident, [("h2", hb), "ident"], [BK(0)])
        k.copy("act", h2T[:, :, j * 128:(j + 1) * 128], pT, [BK(0)], [("h2T", j)])

    def halo(mi):
        a, first, last = mts[mi]
        HXN = ("yf", 1, 0)
        h2hT = h2hTs[mi % 2]
        k.memset("pool", hx, 0.0, HX)
        if not first:
            k.dma(hx[0:1, :], XMID[a - 1:a, :], [], HX)
        if not last:
            k.dma(hx[1:2, :], XMID[a + 512:a + 513, :], [], HX)
        k.act(hxn, hx, AF.Square, HX, [HXN, "hss"], accum=hst[:, 0:1])
        k.ts("dve", hst[:, 1:2], hst[:, 0:1], 1.0 / D, EPS, ALU.mult, ALU.add, ["hss"], ["hms"])
        k.tt("pool", hst[:, 2:3], hst[:, 1:2], nh[0:2, 0:1], ALU.pow, ["hms", "nh"], ["hrs"])
        k.stt(hxn, hx, hst[:, 2:3], gff[0:2, :], ALU.mult, ALU.mult, HX + ["hrs", "gff"], [HXN])
        for kc in range(8):
            k.tr(pTh[:, kc, :], hxn[0:2, kc * 128:(kc + 1) * 128], ident[0:2, 0:2], [HXN, "ident"], [BK(5)])
        k.copy("act", h2hT, pTh, [BK(5)], [("h2hT", mi % 2)])

    h2Tn = [("h2T", j) for j in range(4)]

    def S1(fc, mi):
        b = fc % 2
        h2hT = h2hTs[mi % 2]
        for kc in range(8):
            k.mm(pG[b], Wu[:, kc, fc * 128:(fc + 1) * 128], h2T[:, kc, :], kc == 0, kc == 7, h2Tn, [BK(1 + b)])
        for kc in range(8):
            k.mm(pGh, Wu[:, kc, fc * 128:(fc + 1) * 128], h2hT[:, kc, :], kc == 0, kc == 7, [("h2hT", mi % 2)], [BK(5)])
        for kc in range(8):
            k.mm(pV[b], Wu[:, kc, DFF + fc * 128:DFF + (fc + 1) * 128], h2T[:, kc, :], kc == 0, kc == 7, h2Tn, [BK(3 + b)])

    def S2(fc):
        b = fc % 2
        k.copy("act", gx[b][:, 1:513], pG[b], [BK(1 + b)], [("gx", b)])
        k.copy("act", gx[b][:, 0:1], pGh[:, 0:1], [BK(5)], [("gx", b)])
        k.copy("act", gx[b][:, 513:514], pGh[:, 1:2], [BK(5)], [("gx", b)])
        k.copy("act", aT[:, fc, :], pV[b], [BK(3 + b)], [("aT", fc)])

    def S3(fc):
        b = fc % 2
        k.ts("dve", cv[b], gx[b][:, 0:512], fcw[:, fc, 0:1], None, ALU.mult, None, [("gx", b), "fcw"], [("cv", b)])
        k.stt(cv[b], gx[b][:, 1:513], fcw[:, fc, 1:2], cv[b], ALU.mult, ALU.add, [("gx", b), "fcw", ("cv", b)], [("cv", b)])
        k.stt(cv[b], gx[b][:, 2:514], fcw[:, fc, 2:3], cv[b], ALU.mult, ALU.add, [("gx", b), "fcw", ("cv", b)], [("cv", b)])

    def S4(fc):
        b = fc % 2
        k.act(ge[b], cv[b], AF.Gelu_apprx_tanh, [("cv", b)], [("ge", b)])

    def S5(fc):
        b = fc % 2
        k.tt("dve", aT[:, fc, :], aT[:, fc, :], ge[b], ALU.mult, [("ge", b), ("aT", fc)], [("aT", fc)])

    aTn = [("aT", fc) for fc in range(NFC)]

    def down_sub(mi, j):
        a = mts[mi][0]
        ta = a + j * 128
        xb = next_xt(ta)
        b = 1
        yi = cnt["y"] % 2
        cnt["y"] += 1
        yf = yfs[yi]
        for half in range(2):
            pi = cnt["d"] % 2
            cnt["d"] += 1
            for fc in range(NFC):
                k.mm(pD[pi], aT[:, fc, j * 128:(j + 1) * 128], Wd[:, fc, half * 512:(half + 1) * 512], fc == 0, fc == NFC - 1,
                     [("aT", fc)], [BK(6 + pi)])
            k.tt("dve", yf[:, half * 512:(half + 1) * 512], xt[xb][:, half * 512:(half + 1) * 512], pD[pi], ALU.add,
                 [("xt", xb), BK(6 + pi)], [("yf", yi, half)])
        yn = [("yf", yi, 0), ("yf", yi, 1)]
        k.act(junk, yf, AF.Square, yn, [JUNK, ("ss", b)], accum=ss[b])
        k.ts("dve", ms[b], ss[b], 1.0 / D, EPS, ALU.mult, ALU.add, [("ss", b)], [("ms", b)])
        k.tt("pool", rs[b], ms[b], nh[:, 0:1], ALU.pow, [("ms", b), "nh"], [("rs", b)])
        k.stt(yf, yf, rs[b], gfin, ALU.mult, ALU.mult, yn + [("rs", b), "gfin"], yn)
        k.dma(y[ta:ta + 128, :], yf, yn, [])

    for j in range(4):
        prep_sub(0, j)
        prep_T(0, j)
    halo(0)
    for mi in range(len(mts)):
        nxt = mi + 1 < len(mts)
        for fc in range(NFC + 1):
            if fc < NFC:
                S1(fc, mi)
                S2(fc)
                S3(fc)
            if fc >= 1:
                S4(fc - 1)
                S5(fc - 1)
            if fc == 3 and nxt:
                halo(mi + 1)
        if nxt:
            prep_sub(mi + 1, 0)
            prep_sub(mi + 1, 1)
            down_sub(mi, 0)
            prep_T(mi + 1, 0)
            prep_sub(mi + 1, 2)
            down_sub(mi, 1)
            prep_T(mi + 1, 1)
            prep_sub(mi + 1, 3)
            down_sub(mi, 2)
            prep_T(mi + 1, 2)
            down_sub(mi, 3)
            prep_T(mi + 1, 3)
        else:
            for j in range(4):
                down_sub(mi, j)


def host_consts():
    c = {}
    c["c_ident"] = np.eye(128, dtype=np.float32).astype(ml_dtypes.bfloat16)
    q = np.arange(64)
    kc = np.arange(64)
    cstart = np.clip(q - 8, 0, 48)
    valid = (kc[None, :] >= cstart[:, None]) & (kc[None, :] < cstart[:, None] + 16)
    m = np.where(valid, 0.0, MASKV).astype(np.float32)
    m = np.tile(m[:, None, :], (1, 15, 1)).reshape(64, 960)
    c["c_mask"] = np.concatenate([m, m], axis=0)
    s = np.arange(128)[:, None]
    t = np.arange(128)[None, :]
    same = (s // 64) == (t // 64)
    MFm = (same & (s > t)).astype(np.float32)
    MBm = (same & (s < t)).astype(np.float32)
    S0 = np.broadcast_to((s < 64), (128, 128)).astype(np.float32)
    S1 = np.broadcast_to((s >= 64), (128, 128)).astype(np.float32)
    c["c_mats"] = np.stack([MFm, MBm, S0, S1]).astype(np.float32)
    sl = (np.arange(128) % 64)[:, None]
    tl = np.arange(64)[None, :]
    c["c_tri"] = np.stack([(sl <= tl), (sl >= tl)]).astype(np.float32).astype(ml_dtypes.bfloat16)
    cadd = np.zeros((128, 16), np.float32)
    cadd[:, 0:4] = math.log(128 ** -0.5)
    cadd[:, 8:12] = math.log(128 ** -0.5)
    c["c_cadd"] = cadd
    return c


def pack_params(norm_mix_g, w_in, mlstm_conv_w, gate_b, attn_rpb, attn_norm_g, mlstm_norm_g, w_out,
                norm_ffn_g, w_up, ffn_conv_w, w_down, norm_final_g):
    f = np.float32
    p = {}
    p["w_in"] = np.ascontiguousarray(np.asarray(w_in, f)[0])
    p["w_out"] = np.ascontiguousarray(np.asarray(w_out, f)[0])
    p["w_up"] = np.ascontiguousarray(np.asarray(w_up, f)[0])
    p["w_down"] = np.ascontiguousarray(np.asarray(w_down, f)[0])
    p["gvec"] = np.stack([np.asarray(norm_mix_g, f)[0],
                          np.concatenate([np.asarray(attn_norm_g, f)[0], np.asarray(mlstm_norm_g, f)[0]]),
                          np.asarray(norm_ffn_g, f)[0], np.asarray(norm_final_g, f)])
    mc = np.asarray(mlstm_conv_w, f)[0]
    p["mconv"] = np.ascontiguousarray(mc.reshape(3, 8, 128).transpose(2, 1, 0))
    fc = np.asarray(ffn_conv_w, f)[0]
    p["fconv"] = np.ascontiguousarray(fc.reshape(3, NFC, 128).transpose(2, 1, 0))
    p["gateb"] = np.asarray(gate_b, f).reshape(1, 16)
    rpb = np.asarray(attn_rpb, f)[0]
    q = np.arange(64)[:, None]
    kc = np.arange(64)[None, :]
    dc = np.clip(kc - q + 15, 0, 30)
    g = rpb[:, :, dc]
    g = g.transpose(0, 2, 1, 3).reshape(4, 2, 64, 960)
    p["rpbg"] = np.ascontiguousarray(g.transpose(1, 2, 0, 3).reshape(128, 4, 960))
    return p


_CACHE = {}


def _get_program(seqs=SEQS_FULL, upto="D", debug=False):
    key = (tuple(seqs), upto, debug)
    if key not in _CACHE:
        _CACHE[key] = build_program(seqs, upto, debug)
    return _CACHE[key]


def kernel(x_prompt, x_sample, norm_mix_g, w_in, mlstm_conv_w, gate_b, attn_rpb, attn_norm_g, mlstm_norm_g,
           w_out, norm_ffn_g, w_up, ffn_conv_w, w_down, norm_final_g):
    xp = np.asarray(x_prompt, np.float32)
    xs = np.asarray(x_sample, np.float32)
    common = pack_params(norm_mix_g, w_in, mlstm_conv_w, gate_b, attn_rpb, attn_norm_g, mlstm_norm_g, w_out,
                         norm_ffn_g, w_up, ffn_conv_w, w_down, norm_final_g)
    common.update(host_consts())
    nc, _ = _get_program()
    in_maps = []
    for c in range(NCORES):
        xc = np.concatenate([xp[4 * c:4 * c + 4].reshape(4 * 2048, D), xs[c]], axis=0)
        d = dict(common)
        d["x"] = np.ascontiguousarray(xc)
        in_maps.append(d)
    res = run_bass_kernel_spmd(nc, in_maps, core_ids=list(range(NCORES)))
    yp = np.empty((32, 2048, D), np.float32)
    ys = np.empty((8, 4096, D), np.float32)
    for c in range(NCORES):
        yc = np.asarray(res.results[c]["y"], np.float32)
        yp[4 * c:4 * c + 4] = yc[:8192].reshape(4, 2048, D)
        ys[c] = yc[8192:]
    return yp, ys
```
